# Optimizing a Trainium2 kernel written in Bass

```python
import jax, jax.numpy as jnp
from jax import lax
import numpy as np

D_MODEL = 2048
BATCH = 2
SEQ = 8192
DEPTH = 4

GRID_W = 64
CTX_LEN = 256
HEAD_DIM = 128
ATTN_SCALE = HEAD_DIM ** -0.5
ROPE_THETA = 10000.0
NEG_INF = -1e30

A_Q_HEADS = 8
A_KV_HEADS = 2
A_GROUP = A_Q_HEADS // A_KV_HEADS
A_WINDOW = 128
A_BLOCK = 128
B_HEADS = 8
B_WIN_ROWS = 8
B_WIN_COLS = 16
B_QCOLS = 16
B_KCOLS = 32
C_Q_HEADS = 16
C_KV_HEADS = 4
C_GROUP = C_Q_HEADS // C_KV_HEADS
C_BLOCK = 128

MIX_WIDTH = (A_Q_HEADS + B_HEADS) * HEAD_DIM
AB_SIZES = (A_Q_HEADS * HEAD_DIM, A_KV_HEADS * HEAD_DIM, A_KV_HEADS * HEAD_DIM,
            B_HEADS * HEAD_DIM, B_HEADS * HEAD_DIM, B_HEADS * HEAD_DIM)
AB_IN = sum(AB_SIZES)
C_SIZES = (C_Q_HEADS * HEAD_DIM, C_KV_HEADS * HEAD_DIM, C_KV_HEADS * HEAD_DIM)
C_IN = sum(C_SIZES)

PEER_HEADS = 8
PEER_N_KEYS = 128
PEER_N_EXPERTS = PEER_N_KEYS ** 2
PEER_KEY_DIM = 128
PEER_TOPK = 16
PEER_CHUNK = 128

N_MOD = 6
DEEPNORM_ALPHA = (2 * DEPTH) ** 0.25
DEEPNORM_BETA = (8 * DEPTH) ** -0.25
N_EVEN = (DEPTH + 1) // 2
N_ODD = DEPTH // 2

kernel_name = 'hybrid_dit_window_natten_qknorm_peer'


def split_points(sizes):
    return [int(s) for s in np.cumsum(sizes)[:-1]]


def heads(t, n):
    return t.reshape(t.shape[:2] + (n, HEAD_DIM))


def layer_norm(x, g, b, eps=1e-5):
    xf = x.astype(jnp.float32)
    mu = jnp.mean(xf, -1, keepdims=True)
    var = jnp.mean(jnp.square(xf - mu), -1, keepdims=True)
    return ((xf - mu) * lax.rsqrt(var + eps)).astype(x.dtype) * g + b


def rms_norm(x, g, eps=1e-6):
    xf = x.astype(jnp.float32)
    return (xf * lax.rsqrt(jnp.mean(jnp.square(xf), -1, keepdims=True) + eps)).astype(x.dtype) * g


def axial_rope(n_tokens):
    t = jnp.arange(n_tokens, dtype=jnp.int32)
    row = (t // GRID_W).astype(jnp.float32)
    col = (t % GRID_W).astype(jnp.float32)
    n_freq = HEAD_DIM // 4
    inv_freq = ROPE_THETA ** (-jnp.arange(n_freq, dtype=jnp.float32) / n_freq)
    ang_r = row[:, None] * inv_freq[None, :]
    ang_c = col[:, None] * inv_freq[None, :]
    ang = jnp.concatenate([ang_r, ang_r, ang_c, ang_c], axis=-1)[:, None, :]
    return jnp.cos(ang), jnp.sin(ang)


def apply_rope(x, cos, sin):
    x1, x2, x3, x4 = jnp.split(x, 4, axis=-1)
    rot = jnp.concatenate([-x2, x1, -x4, x3], axis=-1)
    return (x.astype(jnp.float32) * cos + rot.astype(jnp.float32) * sin).astype(x.dtype)


def context_attention(q, k, v, sink=None):
    B, C, hkv, g = q.shape[:4]
    s = jnp.einsum('bqhgd,bkhd->bhgqk', q, k).astype(jnp.float32) * ATTN_SCALE
    if sink is not None:
        s_sink = jnp.broadcast_to(sink.astype(jnp.float32).reshape(1, hkv, g, 1, 1), s.shape[:-1] + (1,))
        s = jnp.concatenate([s, s_sink], axis=-1)
    p = jax.nn.softmax(s, axis=-1)
    if sink is not None:
        p = p[..., :-1]
    out = jnp.einsum('bhgqk,bkhd->bqhgd', p.astype(v.dtype), v)
    return out.reshape(B, C, hkv * g * HEAD_DIM)


def window_attention(q, k, v, k_ctx, v_ctx, sink):
    B, L = q.shape[:2]
    nb = L // A_BLOCK
    qb = q.reshape(B, nb, A_BLOCK, A_KV_HEADS, A_GROUP, HEAD_DIM)

    def band(t):
        tp = jnp.pad(t, ((0, 0), (A_BLOCK, A_BLOCK), (0, 0), (0, 0)))
        tp = tp.reshape(B, nb + 2, A_BLOCK, A_KV_HEADS, HEAD_DIM)
        return jnp.concatenate([tp[:, :-2], tp[:, 1:-1], tp[:, 2:]], axis=2)

    kb, vb = band(k), band(v)
    s_loc = jnp.einsum('bnqhgd,bnkhd->bnhgqk', qb, kb).astype(jnp.float32) * ATTN_SCALE
    s_ctx = jnp.einsum('bnqhgd,bchd->bnhgqc', qb, k_ctx).astype(jnp.float32) * ATTN_SCALE
    qi = np.arange(A_BLOCK)[:, None]
    kk = np.arange(3 * A_BLOCK)[None, :]
    in_window = np.abs(kk - A_BLOCK - qi) <= A_WINDOW
    kpos = (np.arange(nb)[:, None] - 1) * A_BLOCK + np.arange(3 * A_BLOCK)[None, :]
    in_range = (kpos >= 0) & (kpos < L)
    mask = in_window[None] & in_range[:, None, :]
    s_loc = jnp.where(mask[None, :, None, None], s_loc, NEG_INF)
    s_sink = jnp.broadcast_to(sink.astype(jnp.float32).reshape(1, 1, A_KV_HEADS, A_GROUP, 1, 1),
                              s_loc.shape[:-1] + (1,))
    p = jax.nn.softmax(jnp.concatenate([s_loc, s_ctx, s_sink], axis=-1), axis=-1).astype(v.dtype)
    n_loc = 3 * A_BLOCK
    n_ctx = k_ctx.shape[1]
    out = (jnp.einsum('bnhgqk,bnkhd->bnqhgd', p[..., :n_loc], vb)
           + jnp.einsum('bnhgqc,bchd->bnqhgd', p[..., n_loc:n_loc + n_ctx], v_ctx))
    return out.reshape(B, L, A_Q_HEADS * HEAD_DIM)


def neighborhood_attention(q, k, v, k_ctx, v_ctx, rpb):
    B, L = q.shape[:2]
    rows = L // GRID_W
    kr = min(B_WIN_ROWS, rows)
    n_cb = GRID_W // B_QCOLS
    qcol = np.arange(GRID_W).reshape(n_cb, B_QCOLS)
    kstart = np.clip(np.arange(n_cb) * B_QCOLS - B_WIN_COLS // 2, 0, GRID_W - B_KCOLS)
    kcol = kstart[:, None] + np.arange(B_KCOLS)[None, :]
    wstart = np.clip(qcol - B_WIN_COLS // 2, 0, GRID_W - B_WIN_COLS)
    col_ok = (kcol[:, None, :] >= wstart[:, :, None]) & (kcol[:, None, :] < wstart[:, :, None] + B_WIN_COLS)
    dcol = np.clip(kcol[:, None, :] - qcol[:, :, None] + B_WIN_COLS - 1, 0, 2 * B_WIN_COLS - 2)
    bias_c = jnp.transpose(rpb[:, :, dcol], (0, 2, 3, 1, 4)).astype(jnp.float32)
    bias_c = jnp.where(col_ok[None, :, :, None, :], bias_c, NEG_INF)
    rstart = np.clip(np.arange(rows) - kr // 2, 0, rows - kr)
    drow = rstart[:, None] + np.arange(kr)[None, :] - np.arange(rows)[:, None] + B_WIN_ROWS - 1
    k_g = k.reshape(B, rows, GRID_W, B_HEADS, HEAD_DIM)
    v_g = v.reshape(B, rows, GRID_W, B_HEADS, HEAD_DIM)
    q_rows = jnp.moveaxis(q.reshape(B, rows, n_cb, B_QCOLS, B_HEADS, HEAD_DIM), 1, 0)
    n_loc = kr * B_KCOLS

    def row_block(args):
        q_r, r0, dr = args
        k_r = lax.dynamic_slice_in_dim(k_g, r0, kr, axis=1)[:, :, kcol]
        v_r = lax.dynamic_slice_in_dim(v_g, r0, kr, axis=1)[:, :, kcol]
        s_loc = jnp.einsum('bnqhd,brnkhd->bhnqrk', q_r, k_r).astype(jnp.float32) * ATTN_SCALE
        s_loc = s_loc + bias_c[:, :, :, dr][None]
        s_loc = s_loc.reshape(B, B_HEADS, n_cb, B_QCOLS, n_loc)
        s_ctx = jnp.einsum('bnqhd,bchd->bhnqc', q_r, k_ctx).astype(jnp.float32) * ATTN_SCALE
        p = jax.nn.softmax(jnp.concatenate([s_loc, s_ctx], axis=-1), axis=-1).astype(v.dtype)
        p_loc = p[..., :n_loc].reshape(B, B_HEADS, n_cb, B_QCOLS, kr, B_KCOLS)
        return (jnp.einsum('bhnqrk,brnkhd->bnqhd', p_loc, v_r)
                + jnp.einsum('bhnqc,bchd->bnqhd', p[..., n_loc:], v_ctx))

    out = lax.map(row_block, (q_rows, jnp.asarray(rstart, jnp.int32), jnp.asarray(drow, jnp.int32)))
    return jnp.moveaxis(out, 0, 1).reshape(B, L, B_HEADS * HEAD_DIM)


def block_dense_attention(q, k, v):
    B, L = q.shape[:2]
    nb = L // C_BLOCK
    qb = jnp.moveaxis(q.reshape(B, nb, C_BLOCK, C_KV_HEADS, C_GROUP, HEAD_DIM), 1, 0)

    def one_block(q_i):
        s = jnp.einsum('bqhgd,bshd->bhgqs', q_i, k).astype(jnp.float32) * ATTN_SCALE
        p = jax.nn.softmax(s, axis=-1).astype(v.dtype)
        return jnp.einsum('bhgqs,bshd->bqhgd', p, v)

    out = lax.map(one_block, qb)
    return jnp.moveaxis(out, 0, 1).reshape(B, L, C_Q_HEADS * HEAD_DIM)


def mixer_ab(h_lat, h_ctx, w_in, w_out, sink, rpb, cos, sin, ctx_out):
    B, L = h_lat.shape[:2]
    C = h_ctx.shape[1]
    pts = split_points(AB_SIZES)
    qa, ka, va, qb, kb, vb = jnp.split(h_lat @ w_in, pts, axis=-1)
    qa_c, ka_c, va_c, qb_c, kb_c, vb_c = jnp.split(h_ctx @ w_in, pts, axis=-1)
    ka_c, va_c = heads(ka_c, A_KV_HEADS), heads(va_c, A_KV_HEADS)
    kb_c, vb_c = heads(kb_c, B_HEADS), heads(vb_c, B_HEADS)
    qa = apply_rope(heads(qa, A_Q_HEADS), cos, sin).reshape(B, L, A_KV_HEADS, A_GROUP, HEAD_DIM)
    ka = apply_rope(heads(ka, A_KV_HEADS), cos, sin)
    out_a = window_attention(qa, ka, heads(va, A_KV_HEADS), ka_c, va_c, sink)
    out_b = neighborhood_attention(heads(qb, B_HEADS), heads(kb, B_HEADS), heads(vb, B_HEADS), kb_c, vb_c, rpb)
    y_lat = jnp.concatenate([out_a, out_b], axis=-1) @ w_out
    y_ctx = None
    if ctx_out:
        oa_c = context_attention(heads(qa_c, A_Q_HEADS).reshape(B, C, A_KV_HEADS, A_GROUP, HEAD_DIM), ka_c, va_c, sink)
        ob_c = context_attention(heads(qb_c, B_HEADS)[:, :, :, None, :], kb_c, vb_c)
        y_ctx = jnp.concatenate([oa_c, ob_c], axis=-1) @ w_out
    return y_lat, y_ctx


def mixer_c(h_lat, h_ctx, w_in, w_out, q_gain, k_gain, cos, sin, ctx_out):
    B, L = h_lat.shape[:2]
    C = h_ctx.shape[1]
    pts = split_points(C_SIZES)

    def qkv(h):
        q, k, v = jnp.split(h @ w_in, pts, axis=-1)
        return (rms_norm(heads(q, C_Q_HEADS), q_gain), rms_norm(heads(k, C_KV_HEADS), k_gain), heads(v, C_KV_HEADS))

    q, k, v = qkv(h_lat)
    q_c, k_c, v_c = qkv(h_ctx)
    q = apply_rope(q, cos, sin).reshape(B, L, C_KV_HEADS, C_GROUP, HEAD_DIM)
    k = apply_rope(k, cos, sin)
    k_all = jnp.concatenate([k_c, k], axis=1)
    v_all = jnp.concatenate([v_c, v], axis=1)
    y_lat = block_dense_attention(q, k_all, v_all) @ w_out
    y_ctx = None
    if ctx_out:
        y_ctx = context_attention(q_c.reshape(B, C, C_KV_HEADS, C_GROUP, HEAD_DIM), k_c, v_c) @ w_out
    return y_lat, y_ctx


def peer(h, wq, subkeys, u, v):
    T, D = h.shape
    hc = h.reshape(T // PEER_CHUNK, PEER_CHUNK, D)

    def chunk(xi):
        q = (xi @ wq).reshape(PEER_CHUNK, PEER_HEADS, 2, PEER_KEY_DIM)
        s = jnp.einsum('thpk,hpnk->thpn', q, subkeys).astype(jnp.float32)
        s_top, i_top = lax.top_k(s, PEER_TOPK)
        cand = s_top[:, :, 0, :, None] + s_top[:, :, 1, None, :]
        cand_idx = i_top[:, :, 0, :, None] * PEER_N_KEYS + i_top[:, :, 1, None, :]
        best, pos = lax.top_k(cand.reshape(PEER_CHUNK, PEER_HEADS, PEER_TOPK * PEER_TOPK), PEER_TOPK)
        idx = jnp.take_along_axis(cand_idx.reshape(PEER_CHUNK, PEER_HEADS, PEER_TOPK * PEER_TOPK), pos, axis=-1)
        g = jax.nn.softmax(best, axis=-1)
        act = jax.nn.gelu(jnp.einsum('thkd,td->thk', u[idx], xi).astype(jnp.float32), approximate=False)
        return jnp.einsum('thk,thkd->td', (g * act).astype(v.dtype), v[idx])

    return lax.map(chunk, hc).reshape(T, D)


def setup_inputs(seed: int = 0) -> dict:
    key = jax.random.key(seed)
    ks = jax.random.split(key, 20)
    D = D_MODEL

    def nrm(k, shape, s):
        return jax.random.normal(k, shape, jnp.float32) * s

    return {
        'x': nrm(ks[0], (BATCH, SEQ, D), 1.0),
        'c': nrm(ks[1], (BATCH, D), 1.0),
        'ctx': nrm(ks[2], (BATCH, CTX_LEN, D), 1.0),
        'c_ctx': nrm(ks[3], (D,), 1.0),
        'mod_w': nrm(ks[4], (DEPTH, D, N_MOD * D), 0.5 * D ** -0.5),
        'mod_b': nrm(ks[5], (DEPTH, N_MOD * D), 0.02),
        'ln_g': 1.0 + nrm(ks[6], (DEPTH, 2, D), 0.05),
        'ln_b': nrm(ks[7], (DEPTH, 2, D), 0.02),
        'ab_w_in': nrm(ks[8], (N_EVEN, D, AB_IN), D ** -0.5),
        'ab_w_out': nrm(ks[9], (N_EVEN, MIX_WIDTH, D), DEEPNORM_BETA * MIX_WIDTH ** -0.5),
        'a_sink': nrm(ks[10], (N_EVEN, A_Q_HEADS), 0.5),
        'b_rpb': nrm(ks[11], (N_EVEN, B_HEADS, 2 * B_WIN_ROWS - 1, 2 * B_WIN_COLS - 1), 0.1),
        'c_w_in': nrm(ks[12], (N_ODD, D, C_IN), D ** -0.5),
        'c_w_out': nrm(ks[13], (N_ODD, MIX_WIDTH, D), DEEPNORM_BETA * MIX_WIDTH ** -0.5),
        'c_q_gain': 1.0 + nrm(ks[14], (N_ODD, HEAD_DIM), 0.05),
        'c_k_gain': 1.0 + nrm(ks[15], (N_ODD, HEAD_DIM), 0.05),
        'peer_wq': nrm(ks[16], (DEPTH, D, PEER_HEADS * 2 * PEER_KEY_DIM), D ** -0.5),
        'peer_subkeys': nrm(ks[17], (DEPTH, PEER_HEADS, 2, PEER_N_KEYS, PEER_KEY_DIM), PEER_KEY_DIM ** -0.5),
        'peer_u': nrm(ks[18], (DEPTH, PEER_N_EXPERTS, D), D ** -0.5),
        'peer_v': nrm(ks[19], (DEPTH, PEER_N_EXPERTS, D), DEEPNORM_BETA),
    }


def reference(x, c, ctx, c_ctx, mod_w, mod_b, ln_g, ln_b, ab_w_in, ab_w_out, a_sink, b_rpb,
              c_w_in, c_w_out, c_q_gain, c_k_gain, peer_wq, peer_subkeys, peer_u, peer_v):
    B, L, D = x.shape
    C = ctx.shape[1]
    cos, sin = axial_rope(L)
    xc = ctx
    for layer in range(DEPTH):
        last = layer == DEPTH - 1
        i = layer // 2
        mod_lat = (jax.nn.silu(c) @ mod_w[layer] + mod_b[layer])[:, None, :]
        mod_ctx = jax.nn.silu(c_ctx) @ mod_w[layer] + mod_b[layer]
        sh1, sc1, g1, sh2, sc2, g2 = jnp.split(mod_lat, N_MOD, axis=-1)
        csh1, csc1, cg1, csh2, csc2, cg2 = jnp.split(mod_ctx, N_MOD, axis=-1)
        h_lat = x * (1 + sc1) + sh1
        h_ctx = xc * (1 + csc1) + csh1
        if layer % 2 == 0:
            y_lat, y_ctx = mixer_ab(h_lat, h_ctx, ab_w_in[i], ab_w_out[i], a_sink[i], b_rpb[i], cos, sin, not last)
        else:
            y_lat, y_ctx = mixer_c(h_lat, h_ctx, c_w_in[i], c_w_out[i], c_q_gain[i], c_k_gain[i], cos, sin, not last)
        x = layer_norm(DEEPNORM_ALPHA * x + g1 * y_lat, ln_g[layer, 0], ln_b[layer, 0])
        h_lat = x * (1 + sc2) + sh2
        if last:
            y = peer(h_lat.reshape(B * L, D), peer_wq[layer], peer_subkeys[layer], peer_u[layer], peer_v[layer])
            x = layer_norm(DEEPNORM_ALPHA * x + g2 * y.reshape(B, L, D), ln_g[layer, 1], ln_b[layer, 1])
        else:
            xc = layer_norm(DEEPNORM_ALPHA * xc + cg1 * y_ctx, ln_g[layer, 0], ln_b[layer, 0])
            h_ctx = xc * (1 + csc2) + csh2
            tokens = jnp.concatenate([h_lat.reshape(B * L, D), h_ctx.reshape(B * C, D)], axis=0)
            y = peer(tokens, peer_wq[layer], peer_subkeys[layer], peer_u[layer], peer_v[layer])
            x = layer_norm(DEEPNORM_ALPHA * x + g2 * y[:B * L].reshape(B, L, D), ln_g[layer, 1], ln_b[layer, 1])
            xc = layer_norm(DEEPNORM_ALPHA * xc + cg2 * y[B * L:].reshape(B, C, D), ln_g[layer, 1], ln_b[layer, 1])
    return x
```

```python
import numpy as np
import ml_dtypes
import concourse.bass as bass
import concourse.mybir as mybir
from concourse.bass_utils import run_bass_kernel_spmd

F32 = mybir.dt.float32
BF16 = mybir.dt.bfloat16
U32 = mybir.dt.uint32
I32 = mybir.dt.int32
ALU = mybir.AluOpType
AF = mybir.ActivationFunctionType
AX = mybir.AxisListType
NPBF = ml_dtypes.bfloat16

NCORES = 8
D = 2048
ENGS = ["tensor", "vector", "scalar", "gpsimd", "sync"]
NDMA = 8


class Sched:
    def __init__(self, nc, sems):
        self.nc = nc
        self.sems = sems
        self.q = {e: [] for e in ENGS}
        self.cnt = {}
        self.waited = {e: {} for e in ENGS}
        self.last_w = {}
        self.readers = {}
        self.rr = {e: 0 for e in ENGS}
        self.out_dma = []

    def op(self, eng, fn, reads=(), writes=(), dma=False, is_out=False):
        deps = []
        for b in reads:
            if b in self.last_w:
                deps.append(self.last_w[b])
        for b in writes:
            if b in self.last_w:
                deps.append(self.last_w[b])
            deps.extend(self.readers.get(b, ()))
        if dma:
            key = (eng, "d", self.rr[eng])
            self.rr[eng] = (self.rr[eng] + 1) % NDMA
            inc = 16
        else:
            key = (eng, "c")
            inc = 1
        prev = self.cnt.get(key, 0)
        val = prev + inc
        self.cnt[key] = val
        waits = []
        w = self.waited[eng]
        if dma and prev > 0 and w.get(key, 0) < prev:
            w[key] = prev
            waits.append((key, prev))
        for (k, v, de, ddma) in deps:
            if de == eng and eng == "tensor" and not ddma:
                continue
            if w.get(k, 0) >= v:
                continue
            w[k] = v
            waits.append((k, v))
        self.q[eng].append((waits, fn, key, inc))
        rec = (key, val, eng, dma)
        for b in writes:
            self.last_w[b] = rec
            self.readers[b] = []
        for b in reads:
            self.readers.setdefault(b, []).append(rec)
        if is_out:
            self.out_dma.append(rec)
        return rec

    def finish(self):
        waits = []
        for (k, v, de, ddma) in self.out_dma:
            waits.append((k, v))
        self.q["sync"].append((waits, None, None, 0))

    def emit(self, block):
        nc = self.nc
        sems = self.sems

        def body(engname):
            def f(engine):
                for (waits, fn, key, inc) in self.q[engname]:
                    for (k, v) in waits:
                        engine.wait_ge(sems[k], v)
                    if fn is not None:
                        fn(engine).then_inc(sems[key], inc)
            return f

        block.tensor(body("tensor"))
        block.vector(body("vector"))
        block.scalar(body("scalar"))
        block.gpsimd(body("gpsimd"))
        block.sync(body("sync"))


def sem_keys():
    keys = []
    for e in ENGS:
        keys.append((e, "c"))
        for i in range(NDMA):
            keys.append((e, "d", i))
    return keys


class Ctx:
    def __init__(self, nc):
        self.nc = nc
        self.stack = []
        self.prefix = ""

    def mark(self):
        return len(self.stack)

    def release(self, mark):
        while len(self.stack) > mark:
            self.stack.pop().__exit__(None, None, None)

    def enter(self, cm):
        v = cm.__enter__()
        self.stack.append(cm)
        return v

    def close(self):
        while self.stack:
            self.stack.pop().__exit__(None, None, None)

    def sb(self, name, shape, dt):
        return self.enter(self.nc.sbuf_tensor(self.prefix + name, list(shape), dt))

    def ps(self, name, shape, dt):
        return self.enter(self.nc.psum_tensor(self.prefix + name, list(shape), dt))


def new_prog():
    nc = bass.Bass("TRN2", target_bir_lowering=False)
    cx = Ctx(nc)
    sems = {}
    for k in sem_keys():
        sems[k] = cx.enter(nc.semaphore("s_" + "_".join(str(x) for x in k)))
    S = Sched(nc, sems)
    return nc, cx, S


LAST_S = [None]


def finish_prog(nc, cx, S):
    LAST_S[0] = S
    S.finish()
    block = cx.enter(nc.Block())
    S.emit(block)
    cx.close()
    return nc


def run(nc, in_maps):
    res = run_bass_kernel_spmd(nc, in_maps, core_ids=list(range(NCORES)))
    return res.results


MOD_CPC = 6 * D // NCORES


def build_mod():
    nc, cx, S = new_prog()
    cT = nc.dram_tensor("cT", [128, 16, 3], F32, kind="ExternalInput").ap()
    w = nc.dram_tensor("w", [4, D, MOD_CPC], F32, kind="ExternalInput").ap()
    b = nc.dram_tensor("b", [3, 4 * MOD_CPC], F32, kind="ExternalInput").ap()
    out = nc.dram_tensor("out", [3, 4 * MOD_CPC], F32, kind="ExternalOutput").ap()
    c_sb = cx.sb("c_sb", [128, 16, 3], F32)
    s_sb = cx.sb("s_sb", [128, 16, 3], F32)
    b_sb = cx.sb("b_sb", [3, 4 * MOD_CPC], F32)
    o_sb = cx.sb("o_sb", [3, 4 * MOD_CPC], F32)
    wt = [cx.sb("wt%d" % i, [128, MOD_CPC], F32) for i in range(4)]
    ps = [cx.ps("ps%d" % i, [128, 512], F32) for i in range(3)]

    S.op("sync", lambda e: e.dma_start(out=c_sb[:], in_=cT), writes=["c_sb"], dma=True)
    S.op("sync", lambda e: e.dma_start(out=b_sb[:], in_=b), writes=["b_sb"], dma=True)
    S.op("scalar", lambda e: e.activation(out=s_sb[:], in_=c_sb[:], func=AF.Silu),
         reads=["c_sb"], writes=["s_sb"])
    n = 0
    for l in range(4):
        for ch in range(16):
            slot = n % 4
            n += 1
            S.op("sync", lambda e, slot=slot, l=l, ch=ch: e.dma_start(
                out=wt[slot][:], in_=w[l, ch * 128:(ch + 1) * 128, :]),
                writes=[("wt", slot)], dma=True)
            for j in range(3):
                S.op("tensor", lambda e, slot=slot, j=j, ch=ch: e.matmul(
                    ps[j][0:3, :], lhsT=s_sb[:, ch, :], rhs=wt[slot][:, j * 512:(j + 1) * 512],
                    start=(ch == 0), stop=(ch == 15)),
                    reads=[("wt", slot), "s_sb"], writes=[("ps", j)])
        for j in range(3):
            c0 = l * MOD_CPC + j * 512
            S.op("vector", lambda e, j=j, c0=c0: e.tensor_tensor(
                out=o_sb[:, c0:c0 + 512], in0=ps[j][0:3, :], in1=b_sb[:, c0:c0 + 512], op=ALU.add),
                reads=[("ps", j), "b_sb"], writes=["o_sb"])
    S.op("sync", lambda e: e.dma_start(out=out, in_=o_sb[:]), reads=["o_sb"], dma=True, is_out=True)
    return finish_prog(nc, cx, S)


def run_mod(c, c_ctx, mod_w, mod_b):
    cv = np.concatenate([c, c_ctx[None]], 0)
    cT = np.ascontiguousarray(cv.T.reshape(16, 128, 3).transpose(1, 0, 2))
    nc = build_mod()
    in_maps = []
    for r in range(NCORES):
        cs = slice(r * MOD_CPC, (r + 1) * MOD_CPC)
        wr = np.ascontiguousarray(mod_w[:, :, cs])
        br = np.ascontiguousarray(
            np.broadcast_to(mod_b[:, cs].reshape(1, 4 * MOD_CPC), (3, 4 * MOD_CPC)))
        in_maps.append({"cT": cT, "w": wr, "b": br})
    res = run(nc, in_maps)
    mod = np.zeros((4, 3, 6 * D), np.float32)
    for r in range(NCORES):
        o = res[r]["out"].reshape(3, 4, MOD_CPC)
        mod[:, :, r * MOD_CPC:(r + 1) * MOD_CPC] = o.transpose(1, 0, 2)
    return mod


def barrier(S):
    allw = [(k, v) for k, v in S.cnt.items()]
    for e in ENGS:
        waits = []
        for (k, v) in allw:
            if S.waited[e].get(k, 0) < v:
                S.waited[e][k] = v
                waits.append((k, v))
        S.q[e].append((waits, None, None, 0))


NBLK = 17
NROW = NBLK * 128
GW = 768
RMS_EPS = 1e-6


def p1_groups(kind):
    if kind == "ab":
        spec = [(0, 8, None, True), (8, 10, None, True)]
        nh = 36
    else:
        spec = [(0, 16, 0, True), (16, 20, 1, True)]
        nh = 24
    groups = []
    for g in range(nh // 6):
        lo, hi = g * 6, g * 6 + 6
        items = []
        for (a, b, gi, rp) in spec:
            s, e = max(a, lo), min(b, hi)
            if s < e:
                items.append((s - lo, e - lo, gi, rp))
        groups.append(items)
    return groups


def build_p1(kind):
    ncols = 4608 if kind == "ab" else 3072
    ngrp = ncols // GW
    groups = p1_groups(kind)
    nc, cx, S = new_prog()
    x = nc.dram_tensor("x", [NROW, D], F32, kind="ExternalInput").ap()
    modv = nc.dram_tensor("modv", [2, 2, D], F32, kind="ExternalInput").ap()
    w = nc.dram_tensor("w", [D, ncols], F32, kind="ExternalInput").ap()
    cosd = nc.dram_tensor("cos", [NROW, 128], F32, kind="ExternalInput").ap()
    sind = nc.dram_tensor("sin", [NROW, 128], F32, kind="ExternalInput").ap()
    gain = nc.dram_tensor("gain", [2, 128], F32, kind="ExternalInput").ap()
    ident = nc.dram_tensor("ident", [128, 128], BF16, kind="ExternalInput").ap()
    out = nc.dram_tensor("out", [NROW, ncols], BF16, kind="ExternalOutput").ap()

    id_sb = cx.sb("id_sb", [128, 128], BF16)
    hT = cx.sb("hT", [128, NBLK, 16, 128], BF16)
    Mt = cx.sb("Mt", [128, D], F32)
    SHt = cx.sb("SHt", [128, D], F32)
    xt = [cx.sb("xt%d" % i, [128, D], F32) for i in range(2)]
    hb = [cx.sb("hb%d" % i, [128, D], BF16) for i in range(2)]
    pT = [cx.ps("pT%d" % i, [128, 1024], BF16) for i in range(2)]
    pq = [cx.ps("pq%d" % i, [128, 512], F32) for i in range(4)]
    wf = [cx.sb("wf%d" % i, [128, GW], F32) for i in range(3)]
    wb = [cx.sb("wb%d" % i, [128, 16, GW], BF16) for i in range(2)]
    cst = [cx.sb("cst%d" % i, [128, 128], F32) for i in range(2)]
    snt = [cx.sb("snt%d" % i, [128, 128], F32) for i in range(2)]
    gn = cx.sb("gn", [128, 2, 128], F32)
    o = [cx.sb("o%d" % i, [128, GW], F32) for i in range(2)]
    t1 = cx.sb("t1", [128, GW], F32)
    t2 = cx.sb("t2", [128, GW], F32)
    ss = cx.sb("ss", [128, 8], F32)
    ob = [cx.sb("ob%d" % i, [128, GW], BF16) for i in range(2)]

    S.op("sync", lambda e: e.dma_start(out=id_sb[:], in_=ident), writes=["id"], dma=True)
    S.op("sync", lambda e: e.dma_start(
        out=gn[:], in_=gain.unsqueeze(0).broadcast_to([128, 2, 128])), writes=["gn"], dma=True)

    for blk in range(NBLK):
        if blk == 0 or blk == NBLK - 1:
            mi = 0 if blk == 0 else 1
            S.op("sync", lambda e, mi=mi: e.dma_start(
                out=Mt[:], in_=modv[mi, 0:1, :].broadcast_to([128, D])), writes=["Mt"], dma=True)
            S.op("sync", lambda e, mi=mi: e.dma_start(
                out=SHt[:], in_=modv[mi, 1:2, :].broadcast_to([128, D])), writes=["SHt"], dma=True)
            S.op("gpsimd", lambda e: e.tensor_scalar(
                out=Mt[:], in0=Mt[:], scalar1=1.0, scalar2=None, op0=ALU.add),
                reads=["Mt"], writes=["Mt"])
        b = blk % 2
        S.op("sync", lambda e, b=b, blk=blk: e.dma_start(
            out=xt[b][:], in_=x[blk * 128:(blk + 1) * 128, :]), writes=[("xt", b)], dma=True)
        S.op("gpsimd", lambda e, b=b: e.tensor_tensor(
            out=xt[b][:], in0=xt[b][:], in1=Mt[:], op=ALU.mult),
            reads=[("xt", b), "Mt"], writes=[("xt", b)])
        S.op("vector", lambda e, b=b: e.tensor_tensor(
            out=hb[b][:], in0=xt[b][:], in1=SHt[:], op=ALU.add),
            reads=[("xt", b), "SHt"], writes=[("hb", b)])
        for half in range(2):
            for j in range(8):
                ch = half * 8 + j
                S.op("tensor", lambda e, b=b, half=half, j=j, ch=ch: e.transpose(
                    out=pT[half][:, j * 128:(j + 1) * 128], in_=hb[b][:, ch * 128:(ch + 1) * 128],
                    identity=id_sb[:]),
                    reads=[("hb", b), "id"], writes=[("pT", half)])
            S.op("scalar", lambda e, half=half, blk=blk: e.copy(
                out=hT[:, blk, half * 8:(half + 1) * 8, :],
                in_=pT[half][:].rearrange("p (c t) -> p c t", c=8)),
                reads=[("pT", half)], writes=[("hT", blk)])

    nit = 0
    for g in range(ngrp):
        wslot = g % 2
        for ch in range(16):
            fs = (g * 16 + ch) % 3
            S.op("sync", lambda e, fs=fs, ch=ch, g=g: e.dma_start(
                out=wf[fs][:], in_=w[ch * 128:(ch + 1) * 128, g * GW:(g + 1) * GW]),
                writes=[("wf", fs)], dma=True)
            S.op("gpsimd", lambda e, fs=fs, ch=ch, wslot=wslot: e.tensor_copy(
                out=wb[wslot][:, ch, :], in_=wf[fs][:]),
                reads=[("wf", fs)], writes=[("wb", wslot)])
        for blk in range(NBLK):
            it = nit % 2
            nit += 1
            S.op("sync", lambda e, it=it, blk=blk: e.dma_start(
                out=cst[it][:], in_=cosd[blk * 128:(blk + 1) * 128, :]), writes=[("cs", it)], dma=True)
            S.op("sync", lambda e, it=it, blk=blk: e.dma_start(
                out=snt[it][:], in_=sind[blk * 128:(blk + 1) * 128, :]), writes=[("sn", it)], dma=True)
            for half in range(2):
                pb = it * 2 + half
                for ch in range(16):
                    S.op("tensor", lambda e, pb=pb, ch=ch, blk=blk, half=half, wslot=wslot: e.matmul(
                        pq[pb][:, 0:384], lhsT=hT[:, blk, ch, :],
                        rhs=wb[wslot][:, ch, half * 384:(half + 1) * 384],
                        start=(ch == 0), stop=(ch == 15)),
                        reads=[("hT", blk), ("wb", wslot)], writes=[("pq", pb)])
                S.op("scalar", lambda e, pb=pb, it=it, half=half: e.copy(
                    out=o[it][:, half * 384:(half + 1) * 384], in_=pq[pb][:, 0:384]),
                    reads=[("pq", pb)], writes=[("o", it)])
            for (h0, h1, gi, rp) in groups[g]:
                nh = h1 - h0
                osl = o[it][:, h0 * 128:h1 * 128]
                o3 = osl.rearrange("p (h d) -> p h d", d=128)
                if gi is not None:
                    t13 = t1[:, h0 * 128:h1 * 128].rearrange("p (h d) -> p h d", d=128)
                    S.op("scalar", lambda e, osl=osl, h0=h0, h1=h1: e.activation(
                        out=t1[:, h0 * 128:h1 * 128], in_=osl, func=AF.Square),
                        reads=[("o", it)], writes=["t1"])
                    S.op("vector", lambda e, t13=t13, nh=nh: e.tensor_reduce(
                        out=ss[:, 0:nh], in_=t13, axis=AX.X, op=ALU.add),
                        reads=["t1"], writes=["ss"])
                    S.op("vector", lambda e, nh=nh: e.tensor_scalar(
                        out=ss[:, 0:nh], in0=ss[:, 0:nh], scalar1=1.0 / 128, scalar2=RMS_EPS,
                        op0=ALU.mult, op1=ALU.add), reads=["ss"], writes=["ss"])
                    S.op("scalar", lambda e, nh=nh: e.activation(
                        out=ss[:, 0:nh], in_=ss[:, 0:nh], func=AF.Sqrt), reads=["ss"], writes=["ss"])
                    S.op("vector", lambda e, nh=nh: e.reciprocal(
                        out=ss[:, 0:nh], in_=ss[:, 0:nh]), reads=["ss"], writes=["ss"])
                    S.op("vector", lambda e, o3=o3, nh=nh: e.tensor_tensor(
                        out=o3, in0=o3, in1=ss[:, 0:nh].unsqueeze(2).broadcast_to([128, nh, 128]),
                        op=ALU.mult), reads=[("o", it), "ss"], writes=[("o", it)])
                    S.op("vector", lambda e, o3=o3, nh=nh, gi=gi: e.tensor_tensor(
                        out=o3, in0=o3, in1=gn[:, gi:gi + 1, :].broadcast_to([128, nh, 128]),
                        op=ALU.mult), reads=[("o", it), "gn"], writes=[("o", it)])
                if rp:
                    t13 = t1[:, h0 * 128:h1 * 128].rearrange("p (h d) -> p h d", d=128)
                    S.op("vector", lambda e, o3=o3, t13=t13, nh=nh, it=it: e.tensor_tensor(
                        out=t13, in0=o3, in1=cst[it][:].unsqueeze(1).broadcast_to([128, nh, 128]),
                        op=ALU.mult), reads=[("o", it), ("cs", it)], writes=["t1"])
                    o5 = osl.rearrange("p (h a b c) -> p h a b c", a=2, b=2, c=32)
                    t25 = t2[:, h0 * 128:h1 * 128].rearrange("p (h a b c) -> p h a b c", a=2, b=2, c=32)
                    s4 = snt[it][:].rearrange("p (a b c) -> p a b c", a=2, b=2, c=32)
                    for wh in range(2):
                        S.op("gpsimd", lambda e, o5=o5, t25=t25, s4=s4, wh=wh, nh=nh: e.tensor_tensor(
                            out=t25[:, :, :, wh, :], in0=o5[:, :, :, 1 - wh, :],
                            in1=s4[:, :, wh, :].unsqueeze(1).broadcast_to([128, nh, 2, 32]),
                            op=ALU.mult), reads=[("o", it), ("sn", it)], writes=["t2"])
                    S.op("vector", lambda e, h0=h0, h1=h1, it=it: e.tensor_tensor(
                        out=ob[it][:, h0 * 128:h1 * 128], in0=t1[:, h0 * 128:h1 * 128],
                        in1=t2[:, h0 * 128:h1 * 128], op=ALU.add),
                        reads=["t1", "t2"], writes=[("ob", it)])
            covered = [False] * 6
            for (h0, h1, gi, rp) in groups[g]:
                if rp:
                    for hh in range(h0, h1):
                        covered[hh] = True
            hh = 0
            while hh < 6:
                if covered[hh]:
                    hh += 1
                    continue
                h2 = hh
                while h2 < 6 and not covered[h2]:
                    h2 += 1
                S.op("vector", lambda e, hh=hh, h2=h2, it=it: e.tensor_copy(
                    out=ob[it][:, hh * 128:h2 * 128], in_=o[it][:, hh * 128:h2 * 128]),
                    reads=[("o", it)], writes=[("ob", it)])
                hh = h2
            S.op("sync", lambda e, it=it, blk=blk, g=g: e.dma_start(
                out=out[blk * 128:(blk + 1) * 128, g * GW:(g + 1) * GW], in_=ob[it][:]),
                reads=[("ob", it)], dma=True, is_out=True)
    return finish_prog(nc, cx, S)


_PROG_CACHE = {}


def get_prog(name, builder, *args):
    key = (name,) + tuple(args)
    if key not in _PROG_CACHE:
        _PROG_CACHE[key] = builder(*args)
    return _PROG_CACHE[key]


GRID_W = 64
SEQ = 8192
CTX = 256


def rope_tables():
    t = np.arange(SEQ)
    row = (t // GRID_W).astype(np.float32)
    col = (t % GRID_W).astype(np.float32)
    n_freq = 32
    inv_freq = (10000.0 ** (-np.arange(n_freq, dtype=np.float32) / n_freq)).astype(np.float32)
    ang_r = row[:, None] * inv_freq[None, :]
    ang_c = col[:, None] * inv_freq[None, :]
    ang = np.concatenate([ang_r, ang_r, ang_c, ang_c], axis=-1)
    cos = np.cos(ang).astype(np.float32)
    sin = np.sin(ang).astype(np.float32)
    sgn = np.concatenate([-np.ones(32), np.ones(32), -np.ones(32), np.ones(32)]).astype(np.float32)
    return cos, sin * sgn[None, :]


_ROPE = None


def core_rows(x, xc, r):
    b, c4 = r // 4, r % 4
    return np.concatenate([x[b, c4 * 2048:(c4 + 1) * 2048], xc[b, (c4 % 2) * 128:(c4 % 2 + 1) * 128]], 0)


def split_mod(mod_l):
    names = ["sh1", "sc1", "g1", "sh2", "sc2", "g2"]
    return {n: mod_l[:, i * D:(i + 1) * D] for i, n in enumerate(names)}


def run_p1(kind, x, xc, mod_l, w_in, gains):
    global _ROPE
    if _ROPE is None:
        _ROPE = rope_tables()
    cos, sinS = _ROPE
    m = split_mod(mod_l)
    ncols = w_in.shape[1]
    nc = get_prog("p1", build_p1, kind)
    ident = np.eye(128, dtype=np.float32).astype(NPBF)
    in_maps = []
    for r in range(NCORES):
        b, c4 = r // 4, r % 4
        modv = np.stack([np.stack([m["sc1"][b], m["sh1"][b]]), np.stack([m["sc1"][2], m["sh1"][2]])])
        cs = np.concatenate([cos[c4 * 2048:(c4 + 1) * 2048], np.ones((128, 128), np.float32)], 0)
        sn = np.concatenate([sinS[c4 * 2048:(c4 + 1) * 2048], np.zeros((128, 128), np.float32)], 0)
        in_maps.append({"x": core_rows(x, xc, r), "modv": np.ascontiguousarray(modv), "w": w_in,
                        "cos": cs, "sin": sn, "gain": gains, "ident": ident})
    res = run(nc, in_maps)
    q_lat = np.zeros((2, SEQ, ncols), NPBF)
    q_ctx = np.zeros((2, CTX, ncols), NPBF)
    for r in range(NCORES):
        b, c4 = r // 4, r % 4
        o = res[r]["out"]
        q_lat[b, c4 * 2048:(c4 + 1) * 2048] = o[:2048]
        if c4 < 2:
            q_ctx[b, c4 * 128:(c4 + 1) * 128] = o[2048:]
    return q_lat, q_ctx


ATTN_SCALE = 128 ** -0.5
NKB_C = 66


def emit_attn_unit(S, qmov, kblocks, P, Sps, Ops, nsub, uid, post_exp=None):
    n = len(kblocks)
    W = nsub * 128

    def s_mm(i):
        kT, kTk, _, _ = kblocks[i]
        sb = i % 2
        S.op("tensor", lambda e, kT=kT, sb=sb: e.matmul(
            Sps[sb][:, 0:W], lhsT=kT, rhs=qmov[0], start=True, stop=True),
            reads=kTk + qmov[1], writes=[("Sps", sb)])

    s_mm(0)
    for i in range(n):
        if i + 1 < n:
            s_mm(i + 1)
        sb = i % 2
        pb = (uid * 131 + i) % len(P)
        S.op("scalar", lambda e, sb=sb, pb=pb: e.activation(
            out=P[pb][:, 0:W], in_=Sps[sb][:, 0:W], func=AF.Exp, scale=ATTN_SCALE),
            reads=[("Sps", sb)], writes=[("P", pb)])
        if post_exp is not None:
            post_exp(i, pb)
        _, _, v, vk = kblocks[i]
        for j in range(nsub):
            S.op("tensor", lambda e, pb=pb, j=j, v=v, i=i: e.matmul(
                Ops[j][:, 0:129], lhsT=P[pb][:, j * 128:(j + 1) * 128], rhs=v,
                start=(i == 0), stop=(i == n - 1)),
                reads=[("P", pb)] + vk, writes=[("Ops", j)])


def build_p2a_c():
    nc, cx, S = new_prog()
    QT = nc.dram_tensor("QT", [4, 128, 4, NROW], BF16, kind="ExternalInput").ap()
    KT = nc.dram_tensor("KT", [4, 128, NKB_C * 128], BF16, kind="ExternalInput").ap()
    V = nc.dram_tensor("V", [4, 128, NKB_C, 128], BF16, kind="ExternalInput").ap()
    att = nc.dram_tensor("att", [NROW, D], BF16, kind="ExternalOutput").ap()

    KTs = [cx.sb("KTs%d" % i, [128, NKB_C * 128], BF16) for i in range(2)]
    Vs = [cx.sb("Vs%d" % i, [128, NKB_C, 129], BF16) for i in range(2)]
    QTs = [cx.sb("QTs%d" % i, [128, 4, NROW], BF16) for i in range(2)]
    P = [cx.sb("P%d" % i, [128, 512], BF16) for i in range(3)]
    at = [cx.sb("at%d" % i, [128, 512], BF16) for i in range(2)]
    rec = cx.sb("rec", [128, 4], F32)
    Sps = [cx.ps("Sps%d" % i, [128, 512], F32) for i in range(2)]
    Ops = [cx.ps("Ops%d" % i, [128, 512], F32) for i in range(4)]

    for sl in range(2):
        S.op("gpsimd", lambda e, sl=sl: e.memset(Vs[sl][:, :, 128:129], 1.0), writes=[("Vone", sl)])
    uid = 0
    for g in range(4):
        sl = g % 2
        S.op("sync", lambda e, sl=sl, g=g: e.dma_start(out=KTs[sl][:], in_=KT[g]),
             writes=[("KT", sl)], dma=True)
        S.op("sync", lambda e, sl=sl, g=g: e.dma_start(out=Vs[sl][:, :, 0:128], in_=V[g]),
             writes=[("V", sl)], dma=True)
        S.op("sync", lambda e, sl=sl, g=g: e.dma_start(out=QTs[sl][:], in_=QT[g]),
             writes=[("QT", sl)], dma=True)
        for qb in range(NBLK):
            nkb = NKB_C if qb < 16 else 2
            qmov = (QTs[sl][:, :, qb * 128:(qb + 1) * 128], [("QT", sl)])
            kblocks = [(KTs[sl][:, kb * 128:(kb + 1) * 128], [("KT", sl)],
                        Vs[sl][:, kb, :], [("V", sl), ("Vone", sl)]) for kb in range(nkb)]
            emit_attn_unit(S, qmov, kblocks, P, Sps, Ops, 4, uid)
            uid += 1
            ab = uid % 2
            for j in range(4):
                S.op("vector", lambda e, j=j: e.reciprocal(out=rec[:, j:j + 1], in_=Ops[j][:, 128:129]),
                     reads=[("Ops", j)], writes=[("rec", j)])
                S.op("vector", lambda e, j=j, ab=ab: e.tensor_scalar(
                    out=at[ab][:, j * 128:(j + 1) * 128], in0=Ops[j][:, 0:128], scalar1=rec[:, j:j + 1],
                    scalar2=None, op0=ALU.mult),
                    reads=[("Ops", j), ("rec", j)], writes=[("at", ab)])
            S.op("sync", lambda e, ab=ab, qb=qb, g=g: e.dma_start(
                out=att[qb * 128:(qb + 1) * 128, g * 512:(g + 1) * 512], in_=at[ab][:]),
                reads=[("at", ab)], dma=True, is_out=True)
    return finish_prog(nc, cx, S)


def to_headT(q_lat, q_ctx, r, h0, nh):
    rows = core_rows(q_lat, q_ctx, r)
    sub = rows[:, h0 * 128:(h0 + nh) * 128].reshape(NROW, nh, 128)
    return np.ascontiguousarray(sub.transpose(1, 2, 0))


def run_p2a_c(q_lat, q_ctx):
    nc = get_prog("p2a_c", build_p2a_c)
    in_maps = []
    kv = {}
    for b in range(2):
        allr = np.concatenate([q_ctx[b], q_lat[b]], 0)
        k = allr[:, 2048:2560].reshape(NKB_C * 128, 4, 128)
        v = allr[:, 2560:3072].reshape(NKB_C, 128, 4, 128)
        KTh = np.ascontiguousarray(k.transpose(1, 2, 0))
        Vh = np.ascontiguousarray(v.transpose(2, 1, 0, 3))
        kv[b] = (KTh, Vh)
    for r in range(NCORES):
        b = r // 4
        qt = to_headT(q_lat, q_ctx, r, 0, 16).reshape(4, 4, 128, NROW)
        qt = np.ascontiguousarray(qt.transpose(0, 2, 1, 3))
        in_maps.append({"QT": qt, "KT": kv[b][0], "V": kv[b][1]})
    res = run(nc, in_maps)
    return gather_rows([res[r]["att"] for r in range(NCORES)], D, NPBF)


def gather_rows(outs, ncols, dt):
    lat = np.zeros((2, SEQ, ncols), dt)
    ctx = np.zeros((2, CTX, ncols), dt)
    for r in range(NCORES):
        b, c4 = r // 4, r % 4
        o = outs[r]
        lat[b, c4 * 2048:(c4 + 1) * 2048] = o[:2048]
        if c4 < 2:
            ctx[b, c4 * 128:(c4 + 1) * 128] = o[2048:]
    return lat, ctx


NPAT = 25
PAT_CLASS = {0: 5, 1: 10, 14: 15, 15: 20}


def pat_base(lb):
    return PAT_CLASS.get(lb, 0)


def build_p2a_ab():
    nc, cx, S = new_prog()
    QTa = nc.dram_tensor("QTa", [2, 128, 4, NROW], BF16, kind="ExternalInput").ap()
    KTa = nc.dram_tensor("KTa", [2, 128, 20 * 128], BF16, kind="ExternalInput").ap()
    Va = nc.dram_tensor("Va", [2, 128, 20, 128], BF16, kind="ExternalInput").ap()
    QTb = nc.dram_tensor("QTb", [8, 128, NROW], BF16, kind="ExternalInput").ap()
    KTb = nc.dram_tensor("KTb", [8, 128, 22 * 128], BF16, kind="ExternalInput").ap()
    Vb = nc.dram_tensor("Vb", [8, 128, 22, 128], BF16, kind="ExternalInput").ap()
    maskA = nc.dram_tensor("maskA", [128, 4, 128], BF16, kind="ExternalInput").ap()
    sink = nc.dram_tensor("sink", [1, 8], F32, kind="ExternalInput").ap()
    biasB = nc.dram_tensor("biasB", [8, 128, NPAT * 128], F32, kind="ExternalInput").ap()
    maskB = nc.dram_tensor("maskB", [128, NPAT * 128], BF16, kind="ExternalInput").ap()
    att = nc.dram_tensor("att", [NROW, D], BF16, kind="ExternalOutput").ap()

    QTas = [cx.sb("QTas%d" % i, [128, 4, NROW], BF16) for i in range(2)]
    KTas = [cx.sb("KTas%d" % i, [128, 20 * 128], BF16) for i in range(2)]
    Vas = [cx.sb("Vas%d" % i, [128, 20, 129], BF16) for i in range(2)]
    QTbs = [cx.sb("QTbs%d" % i, [128, NROW], BF16) for i in range(2)]
    KTbs = [cx.sb("KTbs%d" % i, [128, 22 * 128], BF16) for i in range(2)]
    Vbs = [cx.sb("Vbs%d" % i, [128, 22, 129], BF16) for i in range(2)]
    mA = cx.sb("mA", [128, 4, 128], BF16)
    mB = cx.sb("mB", [128, NPAT * 128], BF16)
    bB = cx.sb("bB", [128, NPAT * 128], F32)
    eB = cx.sb("eB", [128, NPAT * 128], F32)
    E = [cx.sb("E%d" % i, [128, NPAT * 128], BF16) for i in range(2)]
    snk = cx.sb("snk", [128, 8], F32)
    esnk = cx.sb("esnk", [128, 8], F32)
    P = [cx.sb("P%d" % i, [128, 512], BF16) for i in range(3)]
    at = [cx.sb("at%d" % i, [128, 512], BF16) for i in range(2)]
    rec = cx.sb("rec", [128, 4], F32)
    Sps = [cx.ps("Sps%d" % i, [128, 512], F32) for i in range(2)]
    Ops = [cx.ps("Ops%d" % i, [128, 512], F32) for i in range(4)]

    for sl in range(2):
        S.op("gpsimd", lambda e, sl=sl: e.memset(Vas[sl][:, :, 128:129], 1.0), writes=[("Vaone", sl)])
        S.op("gpsimd", lambda e, sl=sl: e.memset(Vbs[sl][:, :, 128:129], 1.0), writes=[("Vbone", sl)])
    S.op("sync", lambda e: e.dma_start(out=mA[:], in_=maskA), writes=["mA"], dma=True)
    S.op("sync", lambda e: e.dma_start(out=mB[:], in_=maskB), writes=["mB"], dma=True)
    S.op("sync", lambda e: e.dma_start(out=snk[:], in_=sink.broadcast_to([128, 8])), writes=["snk"], dma=True)
    S.op("scalar", lambda e: e.activation(out=esnk[:], in_=snk[:], func=AF.Exp), reads=["snk"], writes=["esnk"])

    uid = 0
    for g in range(2):
        sl = g % 2
        S.op("sync", lambda e, sl=sl, g=g: e.dma_start(out=KTas[sl][:], in_=KTa[g]), writes=[("KTa", sl)], dma=True)
        S.op("sync", lambda e, sl=sl, g=g: e.dma_start(out=Vas[sl][:, :, 0:128], in_=Va[g]), writes=[("Va", sl)], dma=True)
        S.op("sync", lambda e, sl=sl, g=g: e.dma_start(out=QTas[sl][:], in_=QTa[g]), writes=[("QTa", sl)], dma=True)
        for qb in range(NBLK):
            pos = [qb, qb + 1, qb + 2, 18, 19] if qb < 16 else [18, 19]
            qmov = (QTas[sl][:, :, qb * 128:(qb + 1) * 128], [("QTa", sl)])
            kblocks = [(KTas[sl][:, p * 128:(p + 1) * 128], [("KTa", sl)],
                        Vas[sl][:, p, :], [("Va", sl), ("Vaone", sl)]) for p in pos]

            def post_exp(i, pb, qb=qb):
                if qb >= 16 or i not in (0, 2):
                    return
                mi = (0 if qb == 0 else 1) if i == 0 else (3 if qb == 15 else 2)
                S.op("vector", lambda e, pb=pb, mi=mi: e.tensor_tensor(
                    out=P[pb][:].rearrange("p (g q) -> p g q", g=4),
                    in0=P[pb][:].rearrange("p (g q) -> p g q", g=4),
                    in1=mA[:, mi:mi + 1, :].broadcast_to([128, 4, 128]), op=ALU.mult),
                    reads=[("P", pb), "mA"], writes=[("P", pb)])

            emit_attn_unit(S, qmov, kblocks, P, Sps, Ops, 4, uid, post_exp)
            uid += 1
            ab = uid % 2
            for j in range(4):
                hd = g * 4 + j
                S.op("vector", lambda e, j=j, hd=hd: e.tensor_tensor(
                    out=rec[:, j:j + 1], in0=Ops[j][:, 128:129], in1=esnk[:, hd:hd + 1], op=ALU.add),
                    reads=[("Ops", j), "esnk"], writes=[("rec", j)])
                S.op("vector", lambda e, j=j: e.reciprocal(out=rec[:, j:j + 1], in_=rec[:, j:j + 1]),
                     reads=[("rec", j)], writes=[("rec", j)])
                S.op("vector", lambda e, j=j, ab=ab: e.tensor_scalar(
                    out=at[ab][:, j * 128:(j + 1) * 128], in0=Ops[j][:, 0:128], scalar1=rec[:, j:j + 1],
                    scalar2=None, op0=ALU.mult),
                    reads=[("Ops", j), ("rec", j)], writes=[("at", ab)])
            S.op("sync", lambda e, ab=ab, qb=qb, g=g: e.dma_start(
                out=att[qb * 128:(qb + 1) * 128, g * 512:(g + 1) * 512], in_=at[ab][:]),
                reads=[("at", ab)], dma=True, is_out=True)

    for h in range(8):
        sl = h % 2
        S.op("sync", lambda e, sl=sl, h=h: e.dma_start(out=KTbs[sl][:], in_=KTb[h]), writes=[("KTb", sl)], dma=True)
        S.op("sync", lambda e, sl=sl, h=h: e.dma_start(out=Vbs[sl][:, :, 0:128], in_=Vb[h]), writes=[("Vb", sl)], dma=True)
        S.op("sync", lambda e, sl=sl, h=h: e.dma_start(out=QTbs[sl][:], in_=QTb[h]), writes=[("QTb", sl)], dma=True)
        S.op("sync", lambda e, h=h: e.dma_start(out=bB[:], in_=biasB[h]), writes=["bB"], dma=True)
        S.op("scalar", lambda e: e.activation(out=eB[:], in_=bB[:], func=AF.Exp), reads=["bB"], writes=["eB"])
        S.op("gpsimd", lambda e, sl=sl: e.tensor_tensor(out=E[sl][:], in0=eB[:], in1=mB[:], op=ALU.mult),
             reads=["eB", "mB"], writes=[("E", sl)])
        for qb in range(NBLK):
            pos = [qb + s for s in range(5)] + [20, 21] if qb < 16 else [20, 21]
            qmov = (QTbs[sl][:, qb * 128:(qb + 1) * 128], [("QTb", sl)])
            kblocks = [(KTbs[sl][:, p * 128:(p + 1) * 128], [("KTb", sl)],
                        Vbs[sl][:, p, :], [("Vb", sl), ("Vbone", sl)]) for p in pos]

            def post_exp(i, pb, qb=qb, sl=sl):
                if qb >= 16 or i >= 5:
                    return
                pi = pat_base(qb) + i
                S.op("vector", lambda e, pb=pb, pi=pi, sl=sl: e.tensor_tensor(
                    out=P[pb][:, 0:128], in0=P[pb][:, 0:128], in1=E[sl][:, pi * 128:(pi + 1) * 128],
                    op=ALU.mult), reads=[("P", pb), ("E", sl)], writes=[("P", pb)])

            emit_attn_unit(S, qmov, kblocks, P, Sps, Ops, 1, uid, post_exp)
            uid += 1
            ab = uid % 2
            S.op("vector", lambda e: e.reciprocal(out=rec[:, 0:1], in_=Ops[0][:, 128:129]),
                 reads=[("Ops", 0)], writes=[("rec", 0)])
            S.op("vector", lambda e, ab=ab: e.tensor_scalar(
                out=at[ab][:, 0:128], in0=Ops[0][:, 0:128], scalar1=rec[:, 0:1],
                scalar2=None, op0=ALU.mult),
                reads=[("Ops", 0), ("rec", 0)], writes=[("at", ab)])
            S.op("sync", lambda e, ab=ab, qb=qb, h=h: e.dma_start(
                out=att[qb * 128:(qb + 1) * 128, 1024 + h * 128:1024 + (h + 1) * 128], in_=at[ab][:, 0:128]),
                reads=[("at", ab)], dma=True, is_out=True)
    return finish_prog(nc, cx, S)


def nbr_geometry(c4):
    halo = [16 * c4 - 2 + p for p in range(20)]
    if c4 == 0:
        halo[0], halo[1] = 3, None
    if c4 == 3:
        halo[18], halo[19] = 60, None
    mask = np.zeros((128, NPAT, 128), bool)
    drow = np.zeros((128, NPAT, 128), np.int64)
    dcol = np.zeros((128, NPAT, 128), np.int64)
    k = np.arange(128)
    q = np.arange(128)
    for lb in [2, 0, 1, 14, 15]:
        m = 16 * c4 + lb
        seen = []
        for s in range(5):
            gb = halo[lb + s]
            pi = pat_base(lb) + s
            if gb is None or gb in seen or gb < 0 or gb > 63:
                continue
            seen.append(gb)
            kr = (2 * gb + k // 64)[:, None]
            kc = (k % 64)[:, None]
            qr = (2 * m + q // 64)[None, :]
            qc = (q % 64)[None, :]
            rstart = np.clip(qr - 4, 0, 120)
            wstart = np.clip(qc - 8, 0, 48)
            ok = (kr >= rstart) & (kr < rstart + 8) & (kc >= wstart) & (kc < wstart + 16)
            mask[:, pi, :] = ok
            drow[:, pi, :] = np.where(ok, kr - qr + 7, 0)
            dcol[:, pi, :] = np.clip(kc - qc + 15, 0, 30) * ok
    return halo, mask, drow, dcol


def run_p2a_ab(q_lat, q_ctx, sink, rpb):
    nc = get_prog("p2a_ab", build_p2a_ab)
    tri_prev = (np.arange(128)[:, None] >= np.arange(128)[None, :])
    tri_next = (np.arange(128)[:, None] <= np.arange(128)[None, :])
    zeros = np.zeros((128, 128), bool)
    in_maps = []
    for r in range(NCORES):
        b, c4 = r // 4, r % 4
        qt = to_headT(q_lat, q_ctx, r, 0, 8).reshape(2, 4, 128, NROW)
        QTa = np.ascontiguousarray(qt.transpose(0, 2, 1, 3))
        QTb = to_headT(q_lat, q_ctx, r, 12, 8)
        lat = q_lat[b].reshape(64, 128, 4608)
        cxb = q_ctx[b].reshape(2, 128, 4608)
        zb = np.zeros((128, 4608), NPBF)
        blocksA = []
        for p in range(18):
            gb = 16 * c4 - 1 + p
            blocksA.append(lat[gb] if 0 <= gb < 64 else zb)
        blocksA += [cxb[0], cxb[1]]
        A = np.stack(blocksA)
        ka = A[:, :, 1024:1280].reshape(20, 128, 2, 128)
        va = A[:, :, 1280:1536].reshape(20, 128, 2, 128)
        KTa = np.ascontiguousarray(ka.transpose(2, 3, 0, 1).reshape(2, 128, 20 * 128))
        Va = np.ascontiguousarray(va.transpose(2, 1, 0, 3))
        halo, mask, drow, dcol = nbr_geometry(c4)
        blocksB = [lat[gb] if gb is not None and 0 <= gb < 64 else zb for gb in halo] + [cxb[0], cxb[1]]
        Bk = np.stack(blocksB)
        kb = Bk[:, :, 2560:3584].reshape(22, 128, 8, 128)
        vb = Bk[:, :, 3584:4608].reshape(22, 128, 8, 128)
        KTb = np.ascontiguousarray(kb.transpose(2, 3, 0, 1).reshape(8, 128, 22 * 128))
        Vb = np.ascontiguousarray(vb.transpose(2, 1, 0, 3))
        mA = np.stack([zeros if c4 == 0 else tri_prev, tri_prev, tri_next, zeros if c4 == 3 else tri_next], 1)
        biasB = rpb[:, drow, dcol].astype(np.float32)
        in_maps.append({
            "QTa": QTa, "KTa": KTa, "Va": Va, "QTb": QTb, "KTb": KTb, "Vb": Vb,
            "maskA": np.ascontiguousarray(mA).astype(np.float32).astype(NPBF),
            "sink": np.ascontiguousarray(sink.reshape(1, 8)),
            "biasB": np.ascontiguousarray(biasB.reshape(8, 128, NPAT * 128)),
            "maskB": mask.reshape(128, NPAT * 128).astype(np.float32).astype(NPBF)})
    res = run(nc, in_maps)
    return gather_rows([res[r]["att"] for r in range(NCORES)], D, NPBF)


ALPHA = float((2 * 4) ** 0.25)
LN_EPS = 1e-5


def emit_ln(S, zt, zk, stats, mv, lng, lnb, outt, outk, tag):
    for c in range(4):
        S.op("vector", lambda e, c=c: e.bn_stats(out=stats[:, c, :], in_=zt[:, c * 512:(c + 1) * 512]),
             reads=[zk], writes=[("stats", tag)])
    S.op("vector", lambda e: e.bn_aggr(out=mv[:, 0:2], in_=stats[:].rearrange("p c s -> p (c s)")),
         reads=[("stats", tag)], writes=[("mv", tag)])
    S.op("vector", lambda e: e.tensor_scalar(out=mv[:, 2:3], in0=mv[:, 1:2], scalar1=LN_EPS, scalar2=None,
                                             op0=ALU.add), reads=[("mv", tag)], writes=[("mv", tag)])
    S.op("scalar", lambda e: e.activation(out=mv[:, 2:3], in_=mv[:, 2:3], func=AF.Sqrt),
         reads=[("mv", tag)], writes=[("mv", tag)])
    S.op("vector", lambda e: e.reciprocal(out=mv[:, 2:3], in_=mv[:, 2:3]),
         reads=[("mv", tag)], writes=[("mv", tag)])
    S.op("vector", lambda e: e.tensor_scalar(out=zt[:], in0=zt[:], scalar1=mv[:, 0:1], scalar2=mv[:, 2:3],
                                             op0=ALU.subtract, op1=ALU.mult),
         reads=[zk, ("mv", tag)], writes=[zk])
    S.op("gpsimd", lambda e: e.tensor_tensor(out=zt[:], in0=zt[:], in1=lng[:], op=ALU.mult),
         reads=[zk, "lng"], writes=[zk])
    S.op("vector", lambda e: e.tensor_tensor(out=outt[:], in0=zt[:], in1=lnb[:], op=ALU.add),
         reads=[zk, "lnb"], writes=[outk])


def build_p2b():
    nc, cx, S = new_prog()
    emit_p2b(nc, cx, S)
    return finish_prog(nc, cx, S)


def emit_p2b(nc, cx, S):
    attT = nc.dram_tensor("attT", [NBLK, 128, 16, 128], BF16, kind="ExternalInput").ap()
    x = nc.dram_tensor("x", [NROW, D], F32, kind="ExternalInput").ap()
    w = nc.dram_tensor("wout", [D, D], F32, kind="ExternalInput").ap()
    g1 = nc.dram_tensor("g1", [2, D], F32, kind="ExternalInput").ap()
    lngd = nc.dram_tensor("lng", [1, D], F32, kind="ExternalInput").ap()
    lnbd = nc.dram_tensor("lnb", [1, D], F32, kind="ExternalInput").ap()
    out = nc.dram_tensor("out", [NROW, D], F32, kind="ExternalOutput").ap()

    wob = cx.sb("wob", [128, 16, D], BF16)
    wf = [cx.sb("wf%d" % i, [128, D], F32) for i in range(2)]
    aT = [cx.sb("aT%d" % i, [128, 16, 128], BF16) for i in range(2)]
    xt = [cx.sb("xt%d" % i, [128, D], F32) for i in range(2)]
    zt = [cx.sb("zt%d" % i, [128, D], F32) for i in range(2)]
    G1 = cx.sb("G1", [128, D], F32)
    lng = cx.sb("lng_sb", [128, D], F32)
    lnb = cx.sb("lnb_sb", [128, D], F32)
    stats = cx.sb("stats", [128, 4, 6], F32)
    mv = cx.sb("mv", [128, 4], F32)
    yps = [cx.ps("yps%d" % i, [128, D], F32) for i in range(2)]

    S.op("sync", lambda e: e.dma_start(out=lng[:], in_=lngd.broadcast_to([128, D])), writes=["lng"], dma=True)
    S.op("sync", lambda e: e.dma_start(out=lnb[:], in_=lnbd.broadcast_to([128, D])), writes=["lnb"], dma=True)
    for ch in range(16):
        fs = ch % 2
        S.op("sync", lambda e, fs=fs, ch=ch: e.dma_start(out=wf[fs][:], in_=w[ch * 128:(ch + 1) * 128, :]),
             writes=[("wf", fs)], dma=True)
        S.op("gpsimd", lambda e, fs=fs, ch=ch: e.tensor_copy(out=wob[:, ch, :], in_=wf[fs][:]),
             reads=[("wf", fs)], writes=["wob"])
    for blk in range(NBLK):
        b = blk % 2
        if blk == 0 or blk == NBLK - 1:
            mi = 0 if blk == 0 else 1
            S.op("sync", lambda e, mi=mi: e.dma_start(out=G1[:], in_=g1[mi:mi + 1, :].broadcast_to([128, D])),
                 writes=["G1"], dma=True)
        S.op("sync", lambda e, b=b, blk=blk: e.dma_start(out=aT[b][:], in_=attT[blk]), writes=[("aT", b)], dma=True)
        S.op("sync", lambda e, b=b, blk=blk: e.dma_start(out=xt[b][:], in_=x[blk * 128:(blk + 1) * 128, :]),
             writes=[("xt", b)], dma=True)
        for cb in range(4):
            for h in range(16):
                S.op("tensor", lambda e, b=b, cb=cb, h=h: e.matmul(
                    yps[b][:, cb * 512:(cb + 1) * 512], lhsT=aT[b][:, h, :],
                    rhs=wob[:, h, cb * 512:(cb + 1) * 512], start=(h == 0), stop=(h == 15)),
                    reads=[("aT", b), "wob"], writes=[("yps", b, cb)])
        for cb in range(4):
            S.op("vector", lambda e, b=b, cb=cb: e.tensor_tensor(
                out=zt[b][:, cb * 512:(cb + 1) * 512], in0=yps[b][:, cb * 512:(cb + 1) * 512],
                in1=G1[:, cb * 512:(cb + 1) * 512], op=ALU.mult),
                reads=[("yps", b, cb), "G1"], writes=[("zt", b)])
        S.op("vector", lambda e, b=b: e.scalar_tensor_tensor(
            out=zt[b][:], in0=xt[b][:], scalar=ALPHA, in1=zt[b][:], op0=ALU.mult, op1=ALU.add),
            reads=[("xt", b), ("zt", b)], writes=[("zt", b)])
        emit_ln(S, zt[b], ("zt", b), stats, mv, lng, lnb, xt[b], ("xt", b), 0)
        S.op("sync", lambda e, b=b, blk=blk: e.dma_start(out=out[blk * 128:(blk + 1) * 128, :], in_=xt[b][:]),
             reads=[("xt", b)], writes=[("x1dram", blk)], dma=True, is_out=True)
    return out


def run_p2b(att_lat, att_ctx, x, xc, mod_l, w_out, ln_g, ln_b):
    nc = get_prog("p2b", build_p2b)
    m = split_mod(mod_l)
    in_maps = []
    for r in range(NCORES):
        b = r // 4
        rows = core_rows(att_lat, att_ctx, r)
        aT = np.ascontiguousarray(rows.reshape(NBLK, 128, 16, 128).transpose(0, 3, 2, 1))
        in_maps.append({"attT": aT, "x": core_rows(x, xc, r), "wout": w_out,
                        "g1": np.ascontiguousarray(np.stack([m["g1"][b], m["g1"][2]])),
                        "lng": np.ascontiguousarray(ln_g.reshape(1, D)),
                        "lnb": np.ascontiguousarray(ln_b.reshape(1, D))})
    res = run(nc, in_maps)
    return gather_rows([res[r]["out"] for r in range(NCORES)], D, np.float32)


import os
P3A_STOP = int(os.environ.get('P3A_STOP', '0'))
P3A_VAR = int(os.environ.get('P3A_VAR', '0'))
P3A_NB = int(os.environ.get('P3A_NB', '17'))


def build_p3a():
    nc, cx, S = new_prog()
    emit_p3a(nc, cx, S, None, True)
    return finish_prog(nc, cx, S)


def build_p23():
    nc, cx, S = new_prog()
    mark = cx.mark()
    cx.prefix = "a_"
    out = emit_p2b(nc, cx, S)
    barrier(S)
    cx.release(mark)
    cx.prefix = "b_"
    emit_p3a(nc, cx, S, out, False)
    return finish_prog(nc, cx, S)


def emit_p3a(nc, cx, S, x, want_h2):
    if x is None:
        x = nc.dram_tensor("x", [NROW, D], F32, kind="ExternalInput").ap()
    modv = nc.dram_tensor("modv", [2, 2, D], F32, kind="ExternalInput").ap()
    w = nc.dram_tensor("pwq", [D, D], F32, kind="ExternalInput").ap()
    skt = nc.dram_tensor("skt", [128, 16 * 128], F32, kind="ExternalInput").ap()
    ident = nc.dram_tensor("ident", [128, 128], BF16, kind="ExternalInput").ap()
    iot = nc.dram_tensor("iot", [1, 16], F32, kind="ExternalInput").ap()
    h2o = nc.dram_tensor("h2", [NROW, D], F32, kind="ExternalOutput").ap() if want_h2 else None
    idxo = nc.dram_tensor("idx", [NROW, 128], I32, kind="ExternalOutput").ap()
    gwo = nc.dram_tensor("gw", [NROW, 128], F32, kind="ExternalOutput").ap()

    id_sb = cx.sb("id_sb", [128, 128], BF16)
    io16 = cx.sb("io16", [128, 16], F32)
    wqb = cx.sb("wqb", [128, 16, D], BF16)
    wf = cx.sb("wf", [128, D], F32)
    skb = cx.sb("skb", [128, 16, 128], BF16)
    Mt = cx.sb("Mt", [128, D], F32)
    SHt = cx.sb("SHt", [128, D], F32)
    xt = [cx.sb("xt%d" % i, [128, D], F32) for i in range(2)]
    hb = cx.sb("hb", [128, D], BF16)
    hT = cx.sb("hT", [128, 16, 128], BF16)
    qsb = cx.sb("qsb", [128, D], BF16)
    qT = cx.sb("qT", [128, 16, 128], BF16)
    s1 = cx.sb("s1", [128, D], F32)
    s2 = cx.sb("s2", [128, D], F32)
    tv = cx.sb("tv", [128, 16, 16], F32)
    ti = cx.sb("ti", [128, 16, 16], U32)
    tif = cx.sb("tif", [128, 16, 16], F32)
    cand = cx.sb("cand", [128, 8, 256], F32)
    cand2 = cx.sb("cand2", [128, 8, 256], F32)
    cv = cx.sb("cv", [128, 8, 16], F32)
    cpos = cx.sb("cpos", [128, 8, 16], U32)
    pu = cx.sb("pu", [128, 8, 16], U32)
    pa = cx.sb("pa", [128, 8, 16], F32)
    pbf = cx.sb("pbf", [128, 8, 16], F32)
    sel = cx.sb("sel", [128, 8, 16, 16], F32)
    I1 = cx.sb("I1", [128, 8, 16], F32)
    I2 = cx.sb("I2", [128, 8, 16], F32)
    ef = cx.sb("ef", [128, 8, 16], F32)
    ei = [cx.sb("ei%d" % i, [128, 128], I32) for i in range(2)]
    gs = cx.sb("gs", [128, 8], F32)
    gt = [cx.sb("gt%d" % i, [128, 8, 16], F32) for i in range(2)]
    pT = [cx.ps("pT%d" % i, [128, 1024], BF16) for i in range(2)]
    qps = cx.ps("qps", [128, D], F32)

    S.op("sync", lambda e: e.dma_start(out=id_sb[:], in_=ident), writes=["id"], dma=True)
    S.op("sync", lambda e: e.dma_start(out=io16[:], in_=iot.broadcast_to([128, 16])), writes=["io16"], dma=True)
    S.op("sync", lambda e: e.dma_start(out=wf[:], in_=skt), writes=["wf"], dma=True)
    S.op("gpsimd", lambda e: e.tensor_copy(out=skb[:].rearrange("p a b -> p (a b)"), in_=wf[:]),
         reads=["wf"], writes=["skb"])
    for ch in range(16):
        S.op("sync", lambda e, ch=ch: e.dma_start(out=wf[:], in_=w[ch * 128:(ch + 1) * 128, :]),
             writes=["wf"], dma=True)
        S.op("gpsimd", lambda e, ch=ch: e.tensor_copy(out=wqb[:, ch, :], in_=wf[:]),
             reads=["wf"], writes=["wqb"])

    for blk in range(min(NBLK, P3A_NB)):
        b = blk % 2
        if blk == 0 or blk == NBLK - 1:
            mi = 0 if blk == 0 else 1
            S.op("sync", lambda e, mi=mi: e.dma_start(
                out=Mt[:], in_=modv[mi, 0:1, :].broadcast_to([128, D])), writes=["Mt"], dma=True)
            S.op("sync", lambda e, mi=mi: e.dma_start(
                out=SHt[:], in_=modv[mi, 1:2, :].broadcast_to([128, D])), writes=["SHt"], dma=True)
            S.op("gpsimd", lambda e: e.tensor_scalar(
                out=Mt[:], in0=Mt[:], scalar1=1.0, scalar2=None, op0=ALU.add),
                reads=["Mt"], writes=["Mt"])
        S.op("sync", lambda e, b=b, blk=blk: e.dma_start(
            out=xt[b][:], in_=x[blk * 128:(blk + 1) * 128, :]), reads=[("x1dram", blk)], writes=[("xt", b)], dma=True)
        S.op("gpsimd", lambda e, b=b: e.tensor_tensor(out=xt[b][:], in0=xt[b][:], in1=Mt[:], op=ALU.mult),
             reads=[("xt", b), "Mt"], writes=[("xt", b)])
        if P3A_VAR == 2:
            S.op("vector", lambda e, b=b: e.tensor_tensor(out=hb[:], in0=xt[b][:], in1=SHt[:], op=ALU.add),
                 reads=[("xt", b), "SHt"], writes=["hb"])
        else:
            S.op("vector", lambda e, b=b: e.tensor_tensor(out=xt[b][:], in0=xt[b][:], in1=SHt[:], op=ALU.add),
                 reads=[("xt", b), "SHt"], writes=[("xt", b)])
        if want_h2 and P3A_VAR != 2:
            S.op("sync", lambda e, b=b, blk=blk: e.dma_start(out=h2o[blk * 128:(blk + 1) * 128, :], in_=xt[b][:]),
                 reads=[("xt", b)], dma=True, is_out=True)
        if P3A_STOP == 1:
            continue
        if P3A_VAR == 2:
            pass
        elif P3A_VAR == 1:
            S.op("vector", lambda e, b=b: e.tensor_copy(out=hb[:], in_=xt[b][:]), reads=[("xt", b)], writes=["hb"])
        else:
            S.op("scalar", lambda e, b=b: e.copy(out=hb[:], in_=xt[b][:]), reads=[("xt", b)], writes=["hb"])
        if P3A_STOP == 10:
            continue
        for half in range(2):
            for j in range(8):
                ch = half * 8 + j
                S.op("tensor", lambda e, half=half, j=j, ch=ch: e.transpose(
                    out=pT[half][:, j * 128:(j + 1) * 128], in_=hb[:, ch * 128:(ch + 1) * 128],
                    identity=id_sb[:]), reads=["hb", "id"], writes=[("pT", half)])
            S.op("scalar", lambda e, half=half: e.copy(
                out=hT[:, half * 8:(half + 1) * 8, :], in_=pT[half][:].rearrange("p (c t) -> p c t", c=8)),
                reads=[("pT", half)], writes=["hT"])
        if P3A_STOP == 2:
            continue
        for cb in range(4):
            for ch in range(16):
                S.op("tensor", lambda e, cb=cb, ch=ch: e.matmul(
                    qps[:, cb * 512:(cb + 1) * 512], lhsT=hT[:, ch, :],
                    rhs=wqb[:, ch, cb * 512:(cb + 1) * 512], start=(ch == 0), stop=(ch == 15)),
                    reads=["hT", "wqb"], writes=[("qps", cb)])
            S.op("scalar", lambda e, cb=cb: e.copy(out=qsb[:, cb * 512:(cb + 1) * 512],
                                                   in_=qps[:, cb * 512:(cb + 1) * 512]),
                 reads=[("qps", cb)], writes=["qsb"])
        if P3A_STOP == 3:
            continue
        for half in range(2):
            for j in range(8):
                hp = half * 8 + j
                S.op("tensor", lambda e, half=half, j=j, hp=hp: e.transpose(
                    out=pT[half][:, j * 128:(j + 1) * 128], in_=qsb[:, hp * 128:(hp + 1) * 128],
                    identity=id_sb[:]), reads=["qsb", "id"], writes=[("pT", half)])
            S.op("scalar", lambda e, half=half: e.copy(
                out=qT[:, half * 8:(half + 1) * 8, :], in_=pT[half][:].rearrange("p (c t) -> p c t", c=8)),
                reads=[("pT", half)], writes=["qT"])
        if P3A_STOP == 4:
            continue
        for hp in range(16):
            cb = hp // 4
            S.op("tensor", lambda e, hp=hp: e.matmul(
                qps[:, hp * 128:(hp + 1) * 128], lhsT=qT[:, hp, :], rhs=skb[:, hp, :], start=True, stop=True),
                reads=["qT", "skb"], writes=[("qps", cb)])
        for cb in range(4):
            S.op("scalar", lambda e, cb=cb: e.copy(out=s1[:, cb * 512:(cb + 1) * 512],
                                                   in_=qps[:, cb * 512:(cb + 1) * 512]),
                 reads=[("qps", cb)], writes=["s1"])
        if P3A_STOP == 5:
            continue
        for hp in range(16):
            sl = slice(hp * 128, (hp + 1) * 128)
            S.op("vector", lambda e, hp=hp, sl=sl: e.max(out=tv[:, hp, 0:8], in_=s1[:, sl]),
                 reads=["s1"], writes=["tv"])
            S.op("vector", lambda e, hp=hp, sl=sl: e.max_index(out=ti[:, hp, 0:8], in_max=tv[:, hp, 0:8],
                                                               in_values=s1[:, sl]),
                 reads=["s1", "tv"], writes=["ti"])
            S.op("vector", lambda e, hp=hp, sl=sl: e.match_replace(
                out=s2[:, sl], in_to_replace=tv[:, hp, 0:8], in_values=s1[:, sl], imm_value=-1e30),
                reads=["s1", "tv"], writes=["s2"])
            S.op("vector", lambda e, hp=hp, sl=sl: e.max(out=tv[:, hp, 8:16], in_=s2[:, sl]),
                 reads=["s2"], writes=["tv"])
            S.op("vector", lambda e, hp=hp, sl=sl: e.max_index(out=ti[:, hp, 8:16], in_max=tv[:, hp, 8:16],
                                                               in_values=s2[:, sl]),
                 reads=["s2", "tv"], writes=["ti"])
        if P3A_STOP == 6:
            continue
        tv4 = tv[:].rearrange("p (h two) k -> p h two k", two=2)
        c4v = cand[:].rearrange("p h (i j) -> p h i j", j=16)
        S.op("vector", lambda e: e.tensor_tensor(
            out=c4v, in0=tv4[:, :, 0, :].unsqueeze(3).broadcast_to([128, 8, 16, 16]),
            in1=tv4[:, :, 1, :].unsqueeze(2).broadcast_to([128, 8, 16, 16]), op=ALU.add),
            reads=["tv"], writes=["cand"])
        for h in range(8):
            S.op("vector", lambda e, h=h: e.max(out=cv[:, h, 0:8], in_=cand[:, h, :]),
                 reads=["cand"], writes=["cv"])
            S.op("vector", lambda e, h=h: e.max_index(out=cpos[:, h, 0:8], in_max=cv[:, h, 0:8],
                                                      in_values=cand[:, h, :]),
                 reads=["cand", "cv"], writes=["cpos"])
            S.op("vector", lambda e, h=h: e.match_replace(
                out=cand2[:, h, :], in_to_replace=cv[:, h, 0:8], in_values=cand[:, h, :], imm_value=-1e30),
                reads=["cand", "cv"], writes=["cand2"])
            S.op("vector", lambda e, h=h: e.max(out=cv[:, h, 8:16], in_=cand2[:, h, :]),
                 reads=["cand2"], writes=["cv"])
            S.op("vector", lambda e, h=h: e.max_index(out=cpos[:, h, 8:16], in_max=cv[:, h, 8:16],
                                                      in_values=cand2[:, h, :]),
                 reads=["cand2", "cv"], writes=["cpos"])
        if P3A_STOP == 7:
            continue
        gb = blk % 2
        S.op("vector", lambda e, gb=gb: e.tensor_tensor(
            out=gt[gb][:], in0=cv[:], in1=cv[:, :, 0:1].broadcast_to([128, 8, 16]), op=ALU.subtract),
            reads=["cv"], writes=[("gt", gb)])
        S.op("scalar", lambda e, gb=gb: e.activation(out=gt[gb][:], in_=gt[gb][:], func=AF.Exp),
             reads=[("gt", gb)], writes=[("gt", gb)])
        S.op("vector", lambda e, gb=gb: e.tensor_reduce(out=gs[:], in_=gt[gb][:], axis=AX.X, op=ALU.add),
             reads=[("gt", gb)], writes=["gs"])
        S.op("vector", lambda e: e.reciprocal(out=gs[:], in_=gs[:]), reads=["gs"], writes=["gs"])
        S.op("vector", lambda e, gb=gb: e.tensor_tensor(
            out=gt[gb][:], in0=gt[gb][:], in1=gs[:].unsqueeze(2).broadcast_to([128, 8, 16]), op=ALU.mult),
            reads=[("gt", gb), "gs"], writes=[("gt", gb)])
        S.op("sync", lambda e, gb=gb, blk=blk: e.dma_start(
            out=gwo[blk * 128:(blk + 1) * 128, :], in_=gt[gb][:].rearrange("p h k -> p (h k)")),
            reads=[("gt", gb)], dma=True, is_out=True)
        if P3A_STOP == 8:
            continue
        S.op("vector", lambda e: e.tensor_copy(out=tif[:], in_=ti[:]), reads=["ti"], writes=["tif"])
        tif4 = tif[:].rearrange("p (h two) k -> p h two k", two=2)
        for (which, dst) in [(0, I1), (1, I2)]:
            if which == 0:
                S.op("vector", lambda e: e.tensor_single_scalar(out=pu[:], in_=cpos[:], scalar=4,
                                                                op=ALU.logical_shift_right),
                     reads=["cpos"], writes=["pu"])
            else:
                S.op("vector", lambda e: e.tensor_single_scalar(out=pu[:], in_=cpos[:], scalar=15,
                                                                op=ALU.bitwise_and),
                     reads=["cpos"], writes=["pu"])
            S.op("vector", lambda e: e.tensor_copy(out=pa[:], in_=pu[:]), reads=["pu"], writes=["pa"])
            S.op("vector", lambda e: e.tensor_tensor(
                out=sel[:], in0=io16[:].unsqueeze(1).unsqueeze(1).broadcast_to([128, 8, 16, 16]),
                in1=pa[:].unsqueeze(3).broadcast_to([128, 8, 16, 16]), op=ALU.is_equal),
                reads=["io16", "pa"], writes=["sel"])
            S.op("vector", lambda e, which=which: e.tensor_tensor(
                out=sel[:], in0=sel[:], in1=tif4[:, :, which, :].unsqueeze(2).broadcast_to([128, 8, 16, 16]),
                op=ALU.mult), reads=["sel", "tif"], writes=["sel"])
            S.op("vector", lambda e, dst=dst: e.tensor_reduce(out=dst[:], in_=sel[:], axis=AX.X, op=ALU.add),
                 reads=["sel"], writes=["I%d" % which])
        S.op("vector", lambda e: e.scalar_tensor_tensor(
            out=ef[:].rearrange("p h k -> p (h k)"), in0=I1[:].rearrange("p h k -> p (h k)"), scalar=128.0,
            in1=I2[:].rearrange("p h k -> p (h k)"), op0=ALU.mult, op1=ALU.add),
            reads=["I0", "I1"], writes=["ef"])
        S.op("vector", lambda e, gb=gb: e.tensor_copy(out=ei[gb][:], in_=ef[:].rearrange("p h k -> p (h k)")),
             reads=["ef"], writes=[("ei", gb)])
        S.op("sync", lambda e, gb=gb, blk=blk: e.dma_start(out=idxo[blk * 128:(blk + 1) * 128, :], in_=ei[gb][:]),
             reads=[("ei", gb)], dma=True, is_out=True)


def run_p3a(x1, x1c, mod_l, wq, subkeys):
    nc = get_prog("p3a", build_p3a)
    m = split_mod(mod_l)
    ident = np.eye(128, dtype=np.float32).astype(NPBF)
    skt = np.ascontiguousarray(subkeys.reshape(16, 128, 128).transpose(2, 0, 1).reshape(128, 16 * 128))
    iot = np.arange(16, dtype=np.float32).reshape(1, 16)
    in_maps = []
    for r in range(NCORES):
        b = r // 4
        modv = np.stack([np.stack([m["sc2"][b], m["sh2"][b]]), np.stack([m["sc2"][2], m["sh2"][2]])])
        in_maps.append({"x": core_rows(x1, x1c, r), "modv": np.ascontiguousarray(modv), "pwq": wq,
                        "skt": skt, "ident": ident, "iot": iot})
    res = run(nc, in_maps)
    return res


NEXP = 16384
NUB = 4


def build_p3b(NBLK=NBLK, NEXP=NEXP):
    NROW = NBLK * 128
    nc, cx, S = new_prog()
    x1d = nc.dram_tensor("x1", [NROW, D], F32, kind="ExternalInput").ap()
    idxd = nc.dram_tensor("idx", [NROW, 128], I32, kind="ExternalInput").ap()
    gwd = nc.dram_tensor("gw", [NROW, 128], F32, kind="ExternalInput").ap()
    ud = nc.dram_tensor("u", [NEXP, D], F32, kind="ExternalInput").ap()
    vd = nc.dram_tensor("v", [NEXP, D], F32, kind="ExternalInput").ap()
    modv = nc.dram_tensor("modv", [2, 3, D], F32, kind="ExternalInput").ap()
    lngd = nc.dram_tensor("lng", [1, D], F32, kind="ExternalInput").ap()
    lnbd = nc.dram_tensor("lnb", [1, D], F32, kind="ExternalInput").ap()
    out = nc.dram_tensor("out", [NROW, D], F32, kind="ExternalOutput").ap()

    h2 = [cx.sb("h2t%d" % i, [128, D], F32) for i in range(2)]
    x1 = [cx.sb("x1t%d" % i, [128, D], F32) for i in range(2)]
    it = [cx.sb("it%d" % i, [128, 128], I32) for i in range(2)]
    gw = [cx.sb("gw%d" % i, [128, 128], F32) for i in range(2)]
    U = [cx.sb("U%d" % i, [128, D], F32) for i in range(NUB)]
    Vt = [cx.sb("V%d" % i, [128, D], F32) for i in range(NUB)]
    junk = cx.sb("junk", [128, D], F32)
    act = cx.sb("act", [128, 128], F32)
    wt = cx.sb("wt", [128, 128], F32)
    y = cx.sb("y", [128, D], F32)
    Mt = cx.sb("Mt", [128, D], F32)
    SHt = cx.sb("SHt", [128, D], F32)
    G2 = cx.sb("G2", [128, D], F32)
    lng = cx.sb("lng_sb", [128, D], F32)
    lnb = cx.sb("lnb_sb", [128, D], F32)
    stats = cx.sb("stats", [128, 4, 6], F32)
    mv = cx.sb("mv", [128, 4], F32)

    S.op("sync", lambda e: e.dma_start(out=lng[:], in_=lngd.broadcast_to([128, D])), writes=["lng"], dma=True)
    S.op("sync", lambda e: e.dma_start(out=lnb[:], in_=lnbd.broadcast_to([128, D])), writes=["lnb"], dma=True)
    nu = 0
    nv = 0
    cur = None
    for blk in range(NBLK):
        b = blk % 2
        mi = 1 if blk % 17 == 16 else 0
        if mi != cur:
            cur = mi
            S.op("sync", lambda e, mi=mi: e.dma_start(out=Mt[:], in_=modv[mi, 0:1, :].broadcast_to([128, D])),
                 writes=["Mt"], dma=True)
            S.op("sync", lambda e, mi=mi: e.dma_start(out=SHt[:], in_=modv[mi, 1:2, :].broadcast_to([128, D])),
                 writes=["SHt"], dma=True)
            S.op("sync", lambda e, mi=mi: e.dma_start(out=G2[:], in_=modv[mi, 2:3, :].broadcast_to([128, D])),
                 writes=["G2"], dma=True)
            S.op("vector", lambda e: e.tensor_scalar(out=Mt[:], in0=Mt[:], scalar1=1.0, scalar2=None, op0=ALU.add),
                 reads=["Mt"], writes=["Mt"])
        rs = slice(blk * 128, (blk + 1) * 128)
        S.op("sync", lambda e, b=b, rs=rs: e.dma_start(out=x1[b][:], in_=x1d[rs, :]), writes=[("x1", b)], dma=True)
        S.op("sync", lambda e, b=b, rs=rs: e.dma_start(out=it[b][:], in_=idxd[rs, :]), writes=[("it", b)], dma=True)
        S.op("sync", lambda e, b=b, rs=rs: e.dma_start(out=gw[b][:], in_=gwd[rs, :]), writes=[("gw", b)], dma=True)
        S.op("vector", lambda e, b=b: e.tensor_tensor(out=h2[b][:], in0=x1[b][:], in1=Mt[:], op=ALU.mult),
             reads=[("x1", b), "Mt"], writes=[("h2", b)])
        S.op("vector", lambda e, b=b: e.tensor_tensor(out=h2[b][:], in0=h2[b][:], in1=SHt[:], op=ALU.add),
             reads=[("h2", b), "SHt"], writes=[("h2", b)])
        for s in range(128):
            ub = nu % NUB
            nu += 1
            S.op("gpsimd", lambda e, ub=ub, b=b, s=s: e.indirect_dma_start(
                out=U[ub][:], out_offset=None, in_=ud,
                in_offset=bass.IndirectOffsetOnAxis(ap=it[b][:, s:s + 1], axis=0)),
                reads=[("it", b)], writes=[("U", ub)], dma=True)
            S.op("vector", lambda e, ub=ub, b=b, s=s: e.scalar_tensor_tensor(
                out=junk[:], in0=U[ub][:], scalar=1.0, in1=h2[b][:], op0=ALU.mult, op1=ALU.mult,
                accum_out=act[:, s:s + 1]),
                reads=[("U", ub), ("h2", b)], writes=["junk", "act"])
        S.op("scalar", lambda e: e.activation(out=wt[:], in_=act[:], func=AF.Gelu), reads=["act"], writes=["wt"])
        S.op("vector", lambda e, b=b: e.tensor_tensor(out=wt[:], in0=wt[:], in1=gw[b][:], op=ALU.mult),
             reads=["wt", ("gw", b)], writes=["wt"])
        for s in range(128):
            vb = nv % NUB
            nv += 1
            S.op("gpsimd", lambda e, vb=vb, b=b, s=s: e.indirect_dma_start(
                out=Vt[vb][:], out_offset=None, in_=vd,
                in_offset=bass.IndirectOffsetOnAxis(ap=it[b][:, s:s + 1], axis=0)),
                reads=[("it", b)], writes=[("V", vb)], dma=True)
            if s == 0:
                S.op("vector", lambda e, vb=vb, s=s: e.tensor_scalar(
                    out=y[:], in0=Vt[vb][:], scalar1=wt[:, s:s + 1], scalar2=None, op0=ALU.mult),
                    reads=[("V", vb), "wt"], writes=["y"])
            else:
                S.op("vector", lambda e, vb=vb, s=s: e.scalar_tensor_tensor(
                    out=y[:], in0=Vt[vb][:], scalar=wt[:, s:s + 1], in1=y[:], op0=ALU.mult, op1=ALU.add),
                    reads=[("V", vb), "wt", "y"], writes=["y"])
        S.op("vector", lambda e: e.tensor_tensor(out=y[:], in0=y[:], in1=G2[:], op=ALU.mult),
             reads=["y", "G2"], writes=["y"])
        S.op("vector", lambda e, b=b: e.scalar_tensor_tensor(
            out=y[:], in0=x1[b][:], scalar=ALPHA, in1=y[:], op0=ALU.mult, op1=ALU.add),
            reads=[("x1", b), "y"], writes=["y"])
        emit_ln(S, y, "y", stats, mv, lng, lnb, x1[b], ("x1", b), 0)
        S.op("sync", lambda e, b=b, rs=rs: e.dma_start(out=out[rs, :], in_=x1[b][:]),
             reads=[("x1", b)], dma=True, is_out=True)
    return finish_prog(nc, cx, S)


P3B_CORES = 2


def run_p3b(idx_list, gw_list, x1, x1c, mod_l, u, v, ln_g, ln_b):
    per = NCORES // P3B_CORES
    nc = get_prog("p3b", build_p3b, NBLK * per, NEXP)
    m = split_mod(mod_l)
    in_maps = []
    for p in range(P3B_CORES):
        lcs = list(range(p * per, (p + 1) * per))
        b = lcs[0] // 4
        modv = np.stack([np.stack([m["sc2"][b], m["sh2"][b], m["g2"][b]]),
                         np.stack([m["sc2"][2], m["sh2"][2], m["g2"][2]])])
        in_maps.append({"x1": np.concatenate([core_rows(x1, x1c, r) for r in lcs], 0),
                        "idx": np.concatenate([idx_list[r] for r in lcs], 0),
                        "gw": np.concatenate([gw_list[r] for r in lcs], 0),
                        "u": u, "v": v, "modv": np.ascontiguousarray(modv),
                        "lng": np.ascontiguousarray(ln_g.reshape(1, D)),
                        "lnb": np.ascontiguousarray(ln_b.reshape(1, D))})
    res = run_bass_kernel_spmd(nc, in_maps, core_ids=list(range(P3B_CORES))).results
    outs = []
    for p in range(P3B_CORES):
        o = res[p]["out"]
        for j in range(per):
            outs.append(o[j * NROW:(j + 1) * NROW])
    return gather_rows(outs, D, np.float32)


def run_p23(att_lat, att_ctx, x, xc, mod_l, w_out, ln_g, ln_b, wq, subkeys):
    nc = get_prog("p23", build_p23)
    m = split_mod(mod_l)
    ident = np.eye(128, dtype=np.float32).astype(NPBF)
    skt = np.ascontiguousarray(subkeys.reshape(16, 128, 128).transpose(2, 0, 1).reshape(128, 16 * 128))
    iot = np.arange(16, dtype=np.float32).reshape(1, 16)
    in_maps = []
    for r in range(NCORES):
        b = r // 4
        rows = core_rows(att_lat, att_ctx, r)
        aT = np.ascontiguousarray(rows.reshape(NBLK, 128, 16, 128).transpose(0, 3, 2, 1))
        modv = np.stack([np.stack([m["sc2"][b], m["sh2"][b]]), np.stack([m["sc2"][2], m["sh2"][2]])])
        in_maps.append({"attT": aT, "x": core_rows(x, xc, r), "wout": w_out,
                        "g1": np.ascontiguousarray(np.stack([m["g1"][b], m["g1"][2]])),
                        "lng": np.ascontiguousarray(ln_g.reshape(1, D)),
                        "lnb": np.ascontiguousarray(ln_b.reshape(1, D)),
                        "modv": np.ascontiguousarray(modv), "pwq": wq, "skt": skt, "ident": ident, "iot": iot})
    res = run(nc, in_maps)
    x1, x1c = gather_rows([res[r]["out"] for r in range(NCORES)], D, np.float32)
    return x1, x1c, [res[r]["idx"] for r in range(NCORES)], [res[r]["gw"] for r in range(NCORES)]


def kernel(x, c, ctx, c_ctx, mod_w, mod_b, ln_g, ln_b, ab_w_in, ab_w_out, a_sink, b_rpb,
           c_w_in, c_w_out, c_q_gain, c_k_gain, peer_wq, peer_subkeys, peer_u, peer_v):
    f = lambda a: np.ascontiguousarray(np.asarray(a, dtype=np.float32))
    x, c, ctx, c_ctx = f(x), f(c), f(ctx), f(c_ctx)
    mod = run_mod(c, c_ctx, f(mod_w), f(mod_b))
    xc = ctx
    dummy_gain = np.zeros((2, 128), np.float32)
    for layer in range(4):
        i = layer // 2
        mod_l = mod[layer]
        if layer % 2 == 0:
            q_lat, q_ctx = run_p1("ab", x, xc, mod_l, f(ab_w_in[i]), dummy_gain)
            att_lat, att_ctx = run_p2a_ab(q_lat, q_ctx, f(a_sink[i]), f(b_rpb[i]))
            w_out = f(ab_w_out[i])
        else:
            gains = np.ascontiguousarray(np.stack([f(c_q_gain[i]), f(c_k_gain[i])]))
            q_lat, q_ctx = run_p1("c", x, xc, mod_l, f(c_w_in[i]), gains)
            att_lat, att_ctx = run_p2a_c(q_lat, q_ctx)
            w_out = f(c_w_out[i])
        x1, x1c, idx_l, gw_l = run_p23(att_lat, att_ctx, x, xc, mod_l, w_out, f(ln_g[layer, 0]), f(ln_b[layer, 0]),
                                       f(peer_wq[layer]), f(peer_subkeys[layer]))
        x, xc = run_p3b(idx_l, gw_l, x1, x1c, mod_l, f(peer_u[layer]), f(peer_v[layer]),
                        f(ln_g[layer, 1]), f(ln_b[layer, 1]))
    return x
```

```python
import numpy as np
import ml_dtypes
import concourse.bass as bass
import concourse.mybir as mybir
from concourse.bass_utils import run_bass_kernel_spmd

F32 = mybir.dt.float32
BF16 = mybir.dt.bfloat16
U32 = mybir.dt.uint32
I32 = mybir.dt.int32
ALU = mybir.AluOpType
AF = mybir.ActivationFunctionType
AX = mybir.AxisListType
NPBF = ml_dtypes.bfloat16

NCORES = 8
D = 2048
ENGS = ["tensor", "vector", "scalar", "gpsimd", "sync"]
NDMA = 8


class Sched:
    def __init__(self, nc, sems):
        self.nc = nc
        self.sems = sems
        self.q = {e: [] for e in ENGS}
        self.cnt = {}
        self.waited = {e: {} for e in ENGS}
        self.last_w = {}
        self.readers = {}
        self.rr = {e: 0 for e in ENGS}
        self.out_dma = []

    def op(self, eng, fn, reads=(), writes=(), dma=False, is_out=False):
        deps = []
        for b in reads:
            if b in self.last_w:
                deps.append(self.last_w[b])
        for b in writes:
            if b in self.last_w:
                deps.append(self.last_w[b])
            deps.extend(self.readers.get(b, ()))
        if dma:
            key = (eng, "d", self.rr[eng])
            self.rr[eng] = (self.rr[eng] + 1) % NDMA
            inc = 16
        else:
            key = (eng, "c")
            inc = 1
        prev = self.cnt.get(key, 0)
        val = prev + inc
        self.cnt[key] = val
        waits = []
        w = self.waited[eng]
        if dma and prev > 0 and w.get(key, 0) < prev:
            w[key] = prev
            waits.append((key, prev))
        for (k, v, de, ddma) in deps:
            if de == eng and eng == "tensor" and not ddma:
                continue
            if w.get(k, 0) >= v:
                continue
            w[k] = v
            waits.append((k, v))
        self.q[eng].append((waits, fn, key, inc))
        rec = (key, val, eng, dma)
        for b in writes:
            self.last_w[b] = rec
            self.readers[b] = []
        for b in reads:
            self.readers.setdefault(b, []).append(rec)
        if is_out:
            self.out_dma.append(rec)
        return rec

    def finish(self):
        waits = []
        for (k, v, de, ddma) in self.out_dma:
            waits.append((k, v))
        self.q["sync"].append((waits, None, None, 0))

    def emit(self, block):
        nc = self.nc
        sems = self.sems

        def body(engname):
            def f(engine):
                for (waits, fn, key, inc) in self.q[engname]:
                    for (k, v) in waits:
                        engine.wait_ge(sems[k], v)
                    if fn is not None:
                        fn(engine).then_inc(sems[key], inc)
            return f

        block.tensor(body("tensor"))
        block.vector(body("vector"))
        block.scalar(body("scalar"))
        block.gpsimd(body("gpsimd"))
        block.sync(body("sync"))


def sem_keys():
    keys = []
    for e in ENGS:
        keys.append((e, "c"))
        for i in range(NDMA):
            keys.append((e, "d", i))
    return keys


class Ctx:
    def __init__(self, nc):
        self.nc = nc
        self.stack = []
        self.prefix = ""

    def mark(self):
        return len(self.stack)

    def release(self, mark):
        while len(self.stack) > mark:
            self.stack.pop().__exit__(None, None, None)

    def enter(self, cm):
        v = cm.__enter__()
        self.stack.append(cm)
        return v

    def close(self):
        while self.stack:
            self.stack.pop().__exit__(None, None, None)

    def sb(self, name, shape, dt):
        return self.enter(self.nc.sbuf_tensor(self.prefix + name, list(shape), dt))

    def ps(self, name, shape, dt):
        return self.enter(self.nc.psum_tensor(self.prefix + name, list(shape), dt))


def new_prog():
    nc = bass.Bass("TRN2", target_bir_lowering=False)
    cx = Ctx(nc)
    sems = {}
    for k in sem_keys():
        sems[k] = cx.enter(nc.semaphore("s_" + "_".join(str(x) for x in k)))
    S = Sched(nc, sems)
    return nc, cx, S


LAST_S = [None]


def finish_prog(nc, cx, S):
    LAST_S[0] = S
    S.finish()
    block = cx.enter(nc.Block())
    S.emit(block)
    cx.close()
    return nc


def run(nc, in_maps):
    res = run_bass_kernel_spmd(nc, in_maps, core_ids=list(range(NCORES)))
    return res.results


MOD_CPC = 6 * D // NCORES


def build_mod():
    nc, cx, S = new_prog()
    cT = nc.dram_tensor("cT", [128, 16, 3], F32, kind="ExternalInput").ap()
    w = nc.dram_tensor("w", [4, D, MOD_CPC], F32, kind="ExternalInput").ap()
    b = nc.dram_tensor("b", [3, 4 * MOD_CPC], F32, kind="ExternalInput").ap()
    out = nc.dram_tensor("out", [3, 4 * MOD_CPC], F32, kind="ExternalOutput").ap()
    c_sb = cx.sb("c_sb", [128, 16, 3], F32)
    s_sb = cx.sb("s_sb", [128, 16, 3], F32)
    b_sb = cx.sb("b_sb", [3, 4 * MOD_CPC], F32)
    o_sb = cx.sb("o_sb", [3, 4 * MOD_CPC], F32)
    wt = [cx.sb("wt%d" % i, [128, MOD_CPC], F32) for i in range(4)]
    ps = [cx.ps("ps%d" % i, [128, 512], F32) for i in range(3)]

    S.op("sync", lambda e: e.dma_start(out=c_sb[:], in_=cT), writes=["c_sb"], dma=True)
    S.op("sync", lambda e: e.dma_start(out=b_sb[:], in_=b), writes=["b_sb"], dma=True)
    S.op("scalar", lambda e: e.activation(out=s_sb[:], in_=c_sb[:], func=AF.Silu),
         reads=["c_sb"], writes=["s_sb"])
    n = 0
    for l in range(4):
        for ch in range(16):
            slot = n % 4
            n += 1
            S.op("sync", lambda e, slot=slot, l=l, ch=ch: e.dma_start(
                out=wt[slot][:], in_=w[l, ch * 128:(ch + 1) * 128, :]),
                writes=[("wt", slot)], dma=True)
            for j in range(3):
                S.op("tensor", lambda e, slot=slot, j=j, ch=ch: e.matmul(
                    ps[j][0:3, :], lhsT=s_sb[:, ch, :], rhs=wt[slot][:, j * 512:(j + 1) * 512],
                    start=(ch == 0), stop=(ch == 15)),
                    reads=[("wt", slot), "s_sb"], writes=[("ps", j)])
        for j in range(3):
            c0 = l * MOD_CPC + j * 512
            S.op("vector", lambda e, j=j, c0=c0: e.tensor_tensor(
                out=o_sb[:, c0:c0 + 512], in0=ps[j][0:3, :], in1=b_sb[:, c0:c0 + 512], op=ALU.add),
                reads=[("ps", j), "b_sb"], writes=["o_sb"])
    S.op("sync", lambda e: e.dma_start(out=out, in_=o_sb[:]), reads=["o_sb"], dma=True, is_out=True)
    return finish_prog(nc, cx, S)


def run_mod(c, c_ctx, mod_w, mod_b):
    cv = np.concatenate([c, c_ctx[None]], 0)
    cT = np.ascontiguousarray(cv.T.reshape(16, 128, 3).transpose(1, 0, 2))
    nc = build_mod()
    in_maps = []
    for r in range(NCORES):
        cs = slice(r * MOD_CPC, (r + 1) * MOD_CPC)
        wr = np.ascontiguousarray(mod_w[:, :, cs])
        br = np.ascontiguousarray(
            np.broadcast_to(mod_b[:, cs].reshape(1, 4 * MOD_CPC), (3, 4 * MOD_CPC)))
        in_maps.append({"cT": cT, "w": wr, "b": br})
    res = run(nc, in_maps)
    mod = np.zeros((4, 3, 6 * D), np.float32)
    for r in range(NCORES):
        o = res[r]["out"].reshape(3, 4, MOD_CPC)
        mod[:, :, r * MOD_CPC:(r + 1) * MOD_CPC] = o.transpose(1, 0, 2)
    return mod


def barrier(S):
    allw = [(k, v) for k, v in S.cnt.items()]
    for e in ENGS:
        waits = []
        for (k, v) in allw:
            if S.waited[e].get(k, 0) < v:
                S.waited[e][k] = v
                waits.append((k, v))
        S.q[e].append((waits, None, None, 0))


NBLK = 17
NROW = NBLK * 128
GW = 768
RMS_EPS = 1e-6


def p1_groups(kind):
    if kind == "ab":
        spec = [(0, 8, None, True), (8, 10, None, True)]
        nh = 36
    else:
        spec = [(0, 16, 0, True), (16, 20, 1, True)]
        nh = 24
    groups = []
    for g in range(nh // 6):
        lo, hi = g * 6, g * 6 + 6
        items = []
        for (a, b, gi, rp) in spec:
            s, e = max(a, lo), min(b, hi)
            if s < e:
                items.append((s - lo, e - lo, gi, rp))
        groups.append(items)
    return groups


def build_p1(kind):
    ncols = 4608 if kind == "ab" else 3072
    ngrp = ncols // GW
    groups = p1_groups(kind)
    nc, cx, S = new_prog()
    x = nc.dram_tensor("x", [NROW, D], F32, kind="ExternalInput").ap()
    modv = nc.dram_tensor("modv", [2, 2, D], F32, kind="ExternalInput").ap()
    w = nc.dram_tensor("w", [D, ncols], F32, kind="ExternalInput").ap()
    cosd = nc.dram_tensor("cos", [NROW, 128], F32, kind="ExternalInput").ap()
    sind = nc.dram_tensor("sin", [NROW, 128], F32, kind="ExternalInput").ap()
    gain = nc.dram_tensor("gain", [2, 128], F32, kind="ExternalInput").ap()
    ident = nc.dram_tensor("ident", [128, 128], BF16, kind="ExternalInput").ap()
    out = nc.dram_tensor("out", [NROW, ncols], BF16, kind="ExternalOutput").ap()

    id_sb = cx.sb("id_sb", [128, 128], BF16)
    hT = cx.sb("hT", [128, NBLK, 16, 128], BF16)
    Mt = cx.sb("Mt", [128, D], F32)
    SHt = cx.sb("SHt", [128, D], F32)
    xt = [cx.sb("xt%d" % i, [128, D], F32) for i in range(2)]
    hb = [cx.sb("hb%d" % i, [128, D], BF16) for i in range(2)]
    pT = [cx.ps("pT%d" % i, [128, 1024], BF16) for i in range(2)]
    pq = [cx.ps("pq%d" % i, [128, 512], F32) for i in range(4)]
    wf = [cx.sb("wf%d" % i, [128, GW], F32) for i in range(3)]
    wb = [cx.sb("wb%d" % i, [128, 16, GW], BF16) for i in range(2)]
    cst = [cx.sb("cst%d" % i, [128, 128], F32) for i in range(2)]
    snt = [cx.sb("snt%d" % i, [128, 128], F32) for i in range(2)]
    gn = cx.sb("gn", [128, 2, 128], F32)
    o = [cx.sb("o%d" % i, [128, GW], F32) for i in range(2)]
    t1 = cx.sb("t1", [128, GW], F32)
    t2 = cx.sb("t2", [128, GW], F32)
    ss = cx.sb("ss", [128, 8], F32)
    ob = [cx.sb("ob%d" % i, [128, GW], BF16) for i in range(2)]

    S.op("sync", lambda e: e.dma_start(out=id_sb[:], in_=ident), writes=["id"], dma=True)
    S.op("sync", lambda e: e.dma_start(
        out=gn[:], in_=gain.unsqueeze(0).broadcast_to([128, 2, 128])), writes=["gn"], dma=True)

    for blk in range(NBLK):
        if blk == 0 or blk == NBLK - 1:
            mi = 0 if blk == 0 else 1
            S.op("sync", lambda e, mi=mi: e.dma_start(
                out=Mt[:], in_=modv[mi, 0:1, :].broadcast_to([128, D])), writes=["Mt"], dma=True)
            S.op("sync", lambda e, mi=mi: e.dma_start(
                out=SHt[:], in_=modv[mi, 1:2, :].broadcast_to([128, D])), writes=["SHt"], dma=True)
            S.op("gpsimd", lambda e: e.tensor_scalar(
                out=Mt[:], in0=Mt[:], scalar1=1.0, scalar2=None, op0=ALU.add),
                reads=["Mt"], writes=["Mt"])
        b = blk % 2
        S.op("sync", lambda e, b=b, blk=blk: e.dma_start(
            out=xt[b][:], in_=x[blk * 128:(blk + 1) * 128, :]), writes=[("xt", b)], dma=True)
        S.op("gpsimd", lambda e, b=b: e.tensor_tensor(
            out=xt[b][:], in0=xt[b][:], in1=Mt[:], op=ALU.mult),
            reads=[("xt", b), "Mt"], writes=[("xt", b)])
        S.op("vector", lambda e, b=b: e.tensor_tensor(
            out=hb[b][:], in0=xt[b][:], in1=SHt[:], op=ALU.add),
            reads=[("xt", b), "SHt"], writes=[("hb", b)])
        for half in range(2):
            for j in range(8):
                ch = half * 8 + j
                S.op("tensor", lambda e, b=b, half=half, j=j, ch=ch: e.transpose(
                    out=pT[half][:, j * 128:(j + 1) * 128], in_=hb[b][:, ch * 128:(ch + 1) * 128],
                    identity=id_sb[:]),
                    reads=[("hb", b), "id"], writes=[("pT", half)])
            S.op("scalar", lambda e, half=half, blk=blk: e.copy(
                out=hT[:, blk, half * 8:(half + 1) * 8, :],
                in_=pT[half][:].rearrange("p (c t) -> p c t", c=8)),
                reads=[("pT", half)], writes=[("hT", blk)])

    nit = 0
    for g in range(ngrp):
        wslot = g % 2
        for ch in range(16):
            fs = (g * 16 + ch) % 3
            S.op("sync", lambda e, fs=fs, ch=ch, g=g: e.dma_start(
                out=wf[fs][:], in_=w[ch * 128:(ch + 1) * 128, g * GW:(g + 1) * GW]),
                writes=[("wf", fs)], dma=True)
            S.op("gpsimd", lambda e, fs=fs, ch=ch, wslot=wslot: e.tensor_copy(
                out=wb[wslot][:, ch, :], in_=wf[fs][:]),
                reads=[("wf", fs)], writes=[("wb", wslot)])
        for blk in range(NBLK):
            it = nit % 2
            nit += 1
            S.op("sync", lambda e, it=it, blk=blk: e.dma_start(
                out=cst[it][:], in_=cosd[blk * 128:(blk + 1) * 128, :]), writes=[("cs", it)], dma=True)
            S.op("sync", lambda e, it=it, blk=blk: e.dma_start(
                out=snt[it][:], in_=sind[blk * 128:(blk + 1) * 128, :]), writes=[("sn", it)], dma=True)
            for half in range(2):
                pb = it * 2 + half
                for ch in range(16):
                    S.op("tensor", lambda e, pb=pb, ch=ch, blk=blk, half=half, wslot=wslot: e.matmul(
                        pq[pb][:, 0:384], lhsT=hT[:, blk, ch, :],
                        rhs=wb[wslot][:, ch, half * 384:(half + 1) * 384],
                        start=(ch == 0), stop=(ch == 15)),
                        reads=[("hT", blk), ("wb", wslot)], writes=[("pq", pb)])
                S.op("scalar", lambda e, pb=pb, it=it, half=half: e.copy(
                    out=o[it][:, half * 384:(half + 1) * 384], in_=pq[pb][:, 0:384]),
                    reads=[("pq", pb)], writes=[("o", it)])
            for (h0, h1, gi, rp) in groups[g]:
                nh = h1 - h0
                osl = o[it][:, h0 * 128:h1 * 128]
                o3 = osl.rearrange("p (h d) -> p h d", d=128)
                if gi is not None:
                    t13 = t1[:, h0 * 128:h1 * 128].rearrange("p (h d) -> p h d", d=128)
                    S.op("scalar", lambda e, osl=osl, h0=h0, h1=h1: e.activation(
                        out=t1[:, h0 * 128:h1 * 128], in_=osl, func=AF.Square),
                        reads=[("o", it)], writes=["t1"])
                    S.op("vector", lambda e, t13=t13, nh=nh: e.tensor_reduce(
                        out=ss[:, 0:nh], in_=t13, axis=AX.X, op=ALU.add),
                        reads=["t1"], writes=["ss"])
                    S.op("vector", lambda e, nh=nh: e.tensor_scalar(
                        out=ss[:, 0:nh], in0=ss[:, 0:nh], scalar1=1.0 / 128, scalar2=RMS_EPS,
                        op0=ALU.mult, op1=ALU.add), reads=["ss"], writes=["ss"])
                    S.op("scalar", lambda e, nh=nh: e.activation(
                        out=ss[:, 0:nh], in_=ss[:, 0:nh], func=AF.Sqrt), reads=["ss"], writes=["ss"])
                    S.op("vector", lambda e, nh=nh: e.reciprocal(
                        out=ss[:, 0:nh], in_=ss[:, 0:nh]), reads=["ss"], writes=["ss"])
                    S.op("vector", lambda e, o3=o3, nh=nh: e.tensor_tensor(
                        out=o3, in0=o3, in1=ss[:, 0:nh].unsqueeze(2).broadcast_to([128, nh, 128]),
                        op=ALU.mult), reads=[("o", it), "ss"], writes=[("o", it)])
                    S.op("vector", lambda e, o3=o3, nh=nh, gi=gi: e.tensor_tensor(
                        out=o3, in0=o3, in1=gn[:, gi:gi + 1, :].broadcast_to([128, nh, 128]),
                        op=ALU.mult), reads=[("o", it), "gn"], writes=[("o", it)])
                if rp:
                    t13 = t1[:, h0 * 128:h1 * 128].rearrange("p (h d) -> p h d", d=128)
                    S.op("vector", lambda e, o3=o3, t13=t13, nh=nh, it=it: e.tensor_tensor(
                        out=t13, in0=o3, in1=cst[it][:].unsqueeze(1).broadcast_to([128, nh, 128]),
                        op=ALU.mult), reads=[("o", it), ("cs", it)], writes=["t1"])
                    o5 = osl.rearrange("p (h a b c) -> p h a b c", a=2, b=2, c=32)
                    t25 = t2[:, h0 * 128:h1 * 128].rearrange("p (h a b c) -> p h a b c", a=2, b=2, c=32)
                    s4 = snt[it][:].rearrange("p (a b c) -> p a b c", a=2, b=2, c=32)
                    for wh in range(2):
                        S.op("gpsimd", lambda e, o5=o5, t25=t25, s4=s4, wh=wh, nh=nh: e.tensor_tensor(
                            out=t25[:, :, :, wh, :], in0=o5[:, :, :, 1 - wh, :],
                            in1=s4[:, :, wh, :].unsqueeze(1).broadcast_to([128, nh, 2, 32]),
                            op=ALU.mult), reads=[("o", it), ("sn", it)], writes=["t2"])
                    S.op("vector", lambda e, h0=h0, h1=h1, it=it: e.tensor_tensor(
                        out=ob[it][:, h0 * 128:h1 * 128], in0=t1[:, h0 * 128:h1 * 128],
                        in1=t2[:, h0 * 128:h1 * 128], op=ALU.add),
                        reads=["t1", "t2"], writes=[("ob", it)])
            covered = [False] * 6
            for (h0, h1, gi, rp) in groups[g]:
                if rp:
                    for hh in range(h0, h1):
                        covered[hh] = True
            hh = 0
            while hh < 6:
                if covered[hh]:
                    hh += 1
                    continue
                h2 = hh
                while h2 < 6 and not covered[h2]:
                    h2 += 1
                S.op("vector", lambda e, hh=hh, h2=h2, it=it: e.tensor_copy(
                    out=ob[it][:, hh * 128:h2 * 128], in_=o[it][:, hh * 128:h2 * 128]),
                    reads=[("o", it)], writes=[("ob", it)])
                hh = h2
            S.op("sync", lambda e, it=it, blk=blk, g=g: e.dma_start(
                out=out[blk * 128:(blk + 1) * 128, g * GW:(g + 1) * GW], in_=ob[it][:]),
                reads=[("ob", it)], dma=True, is_out=True)
    return finish_prog(nc, cx, S)


_PROG_CACHE = {}


def get_prog(name, builder, *args):
    key = (name,) + tuple(args)
    if key not in _PROG_CACHE:
        _PROG_CACHE[key] = builder(*args)
    return _PROG_CACHE[key]


GRID_W = 64
SEQ = 8192
CTX = 256


def rope_tables():
    t = np.arange(SEQ)
    row = (t // GRID_W).astype(np.float32)
    col = (t % GRID_W).astype(np.float32)
    n_freq = 32
    inv_freq = (10000.0 ** (-np.arange(n_freq, dtype=np.float32) / n_freq)).astype(np.float32)
    ang_r = row[:, None] * inv_freq[None, :]
    ang_c = col[:, None] * inv_freq[None, :]
    ang = np.concatenate([ang_r, ang_r, ang_c, ang_c], axis=-1)
    cos = np.cos(ang).astype(np.float32)
    sin = np.sin(ang).astype(np.float32)
    sgn = np.concatenate([-np.ones(32), np.ones(32), -np.ones(32), np.ones(32)]).astype(np.float32)
    return cos, sin * sgn[None, :]


_ROPE = None


def core_rows(x, xc, r):
    b, c4 = r // 4, r % 4
    return np.concatenate([x[b, c4 * 2048:(c4 + 1) * 2048], xc[b, (c4 % 2) * 128:(c4 % 2 + 1) * 128]], 0)


def split_mod(mod_l):
    names = ["sh1", "sc1", "g1", "sh2", "sc2", "g2"]
    return {n: mod_l[:, i * D:(i + 1) * D] for i, n in enumerate(names)}


def run_p1(kind, x, xc, mod_l, w_in, gains):
    global _ROPE
    if _ROPE is None:
        _ROPE = rope_tables()
    cos, sinS = _ROPE
    m = split_mod(mod_l)
    ncols = w_in.shape[1]
    nc = get_prog("p1", build_p1, kind)
    ident = np.eye(128, dtype=np.float32).astype(NPBF)
    in_maps = []
    for r in range(NCORES):
        b, c4 = r // 4, r % 4
        modv = np.stack([np.stack([m["sc1"][b], m["sh1"][b]]), np.stack([m["sc1"][2], m["sh1"][2]])])
        cs = np.concatenate([cos[c4 * 2048:(c4 + 1) * 2048], np.ones((128, 128), np.float32)], 0)
        sn = np.concatenate([sinS[c4 * 2048:(c4 + 1) * 2048], np.zeros((128, 128), np.float32)], 0)
        in_maps.append({"x": core_rows(x, xc, r), "modv": np.ascontiguousarray(modv), "w": w_in,
                        "cos": cs, "sin": sn, "gain": gains, "ident": ident})
    res = run(nc, in_maps)
    q_lat = np.zeros((2, SEQ, ncols), NPBF)
    q_ctx = np.zeros((2, CTX, ncols), NPBF)
    for r in range(NCORES):
        b, c4 = r // 4, r % 4
        o = res[r]["out"]
        q_lat[b, c4 * 2048:(c4 + 1) * 2048] = o[:2048]
        if c4 < 2:
            q_ctx[b, c4 * 128:(c4 + 1) * 128] = o[2048:]
    return q_lat, q_ctx


ATTN_SCALE = 128 ** -0.5
NKB_C = 66


def emit_attn_unit(S, qmov, kblocks, P, Sps, Ops, nsub, uid, post_exp=None):
    n = len(kblocks)
    W = nsub * 128

    def s_mm(i):
        kT, kTk, _, _ = kblocks[i]
        sb = i % 2
        S.op("tensor", lambda e, kT=kT, sb=sb: e.matmul(
            Sps[sb][:, 0:W], lhsT=kT, rhs=qmov[0], start=True, stop=True),
            reads=kTk + qmov[1], writes=[("Sps", sb)])

    s_mm(0)
    for i in range(n):
        if i + 1 < n:
            s_mm(i + 1)
        sb = i % 2
        pb = (uid * 131 + i) % len(P)
        S.op("scalar", lambda e, sb=sb, pb=pb: e.activation(
            out=P[pb][:, 0:W], in_=Sps[sb][:, 0:W], func=AF.Exp, scale=ATTN_SCALE),
            reads=[("Sps", sb)], writes=[("P", pb)])
        if post_exp is not None:
            post_exp(i, pb)
        _, _, v, vk = kblocks[i]
        for j in range(nsub):
            S.op("tensor", lambda e, pb=pb, j=j, v=v, i=i: e.matmul(
                Ops[j][:, 0:129], lhsT=P[pb][:, j * 128:(j + 1) * 128], rhs=v,
                start=(i == 0), stop=(i == n - 1)),
                reads=[("P", pb)] + vk, writes=[("Ops", j)])


def build_p2a_c():
    nc, cx, S = new_prog()
    QT = nc.dram_tensor("QT", [4, 128, 4, NROW], BF16, kind="ExternalInput").ap()
    KT = nc.dram_tensor("KT", [4, 128, NKB_C * 128], BF16, kind="ExternalInput").ap()
    V = nc.dram_tensor("V", [4, 128, NKB_C, 128], BF16, kind="ExternalInput").ap()
    att = nc.dram_tensor("att", [NROW, D], BF16, kind="ExternalOutput").ap()

    KTs = [cx.sb("KTs%d" % i, [128, NKB_C * 128], BF16) for i in range(2)]
    Vs = [cx.sb("Vs%d" % i, [128, NKB_C, 129], BF16) for i in range(2)]
    QTs = [cx.sb("QTs%d" % i, [128, 4, NROW], BF16) for i in range(2)]
    P = [cx.sb("P%d" % i, [128, 512], BF16) for i in range(3)]
    at = [cx.sb("at%d" % i, [128, 512], BF16) for i in range(2)]
    rec = cx.sb("rec", [128, 4], F32)
    Sps = [cx.ps("Sps%d" % i, [128, 512], F32) for i in range(2)]
    Ops = [cx.ps("Ops%d" % i, [128, 512], F32) for i in range(4)]

    for sl in range(2):
        S.op("gpsimd", lambda e, sl=sl: e.memset(Vs[sl][:, :, 128:129], 1.0), writes=[("Vone", sl)])
    uid = 0
    for g in range(4):
        sl = g % 2
        S.op("sync", lambda e, sl=sl, g=g: e.dma_start(out=KTs[sl][:], in_=KT[g]),
             writes=[("KT", sl)], dma=True)
        S.op("sync", lambda e, sl=sl, g=g: e.dma_start(out=Vs[sl][:, :, 0:128], in_=V[g]),
             writes=[("V", sl)], dma=True)
        S.op("sync", lambda e, sl=sl, g=g: e.dma_start(out=QTs[sl][:], in_=QT[g]),
             writes=[("QT", sl)], dma=True)
        for qb in range(NBLK):
            nkb = NKB_C if qb < 16 else 2
            qmov = (QTs[sl][:, :, qb * 128:(qb + 1) * 128], [("QT", sl)])
            kblocks = [(KTs[sl][:, kb * 128:(kb + 1) * 128], [("KT", sl)],
                        Vs[sl][:, kb, :], [("V", sl), ("Vone", sl)]) for kb in range(nkb)]
            emit_attn_unit(S, qmov, kblocks, P, Sps, Ops, 4, uid)
            uid += 1
            ab = uid % 2
            for j in range(4):
                S.op("vector", lambda e, j=j: e.reciprocal(out=rec[:, j:j + 1], in_=Ops[j][:, 128:129]),
                     reads=[("Ops", j)], writes=[("rec", j)])
                S.op("vector", lambda e, j=j, ab=ab: e.tensor_scalar(
                    out=at[ab][:, j * 128:(j + 1) * 128], in0=Ops[j][:, 0:128], scalar1=rec[:, j:j + 1],
                    scalar2=None, op0=ALU.mult),
                    reads=[("Ops", j), ("rec", j)], writes=[("at", ab)])
            S.op("sync", lambda e, ab=ab, qb=qb, g=g: e.dma_start(
                out=att[qb * 128:(qb + 1) * 128, g * 512:(g + 1) * 512], in_=at[ab][:]),
                reads=[("at", ab)], dma=True, is_out=True)
    return finish_prog(nc, cx, S)


def to_headT(q_lat, q_ctx, r, h0, nh):
    rows = core_rows(q_lat, q_ctx, r)
    sub = rows[:, h0 * 128:(h0 + nh) * 128].reshape(NROW, nh, 128)
    return np.ascontiguousarray(sub.transpose(1, 2, 0))


def run_p2a_c(q_lat, q_ctx):
    nc = get_prog("p2a_c", build_p2a_c)
    in_maps = []
    kv = {}
    for b in range(2):
        allr = np.concatenate([q_ctx[b], q_lat[b]], 0)
        k = allr[:, 2048:2560].reshape(NKB_C * 128, 4, 128)
        v = allr[:, 2560:3072].reshape(NKB_C, 128, 4, 128)
        KTh = np.ascontiguousarray(k.transpose(1, 2, 0))
        Vh = np.ascontiguousarray(v.transpose(2, 1, 0, 3))
        kv[b] = (KTh, Vh)
    for r in range(NCORES):
        b = r // 4
        qt = to_headT(q_lat, q_ctx, r, 0, 16).reshape(4, 4, 128, NROW)
        qt = np.ascontiguousarray(qt.transpose(0, 2, 1, 3))
        in_maps.append({"QT": qt, "KT": kv[b][0], "V": kv[b][1]})
    res = run(nc, in_maps)
    return gather_rows([res[r]["att"] for r in range(NCORES)], D, NPBF)


def gather_rows(outs, ncols, dt):
    lat = np.zeros((2, SEQ, ncols), dt)
    ctx = np.zeros((2, CTX, ncols), dt)
    for r in range(NCORES):
        b, c4 = r // 4, r % 4
        o = outs[r]
        lat[b, c4 * 2048:(c4 + 1) * 2048] = o[:2048]
        if c4 < 2:
            ctx[b, c4 * 128:(c4 + 1) * 128] = o[2048:]
    return lat, ctx


NPAT = 25
PAT_CLASS = {0: 5, 1: 10, 14: 15, 15: 20}


def pat_base(lb):
    return PAT_CLASS.get(lb, 0)


def build_p2a_ab():
    nc, cx, S = new_prog()
    QTa = nc.dram_tensor("QTa", [2, 128, 4, NROW], BF16, kind="ExternalInput").ap()
    KTa = nc.dram_tensor("KTa", [2, 128, 20 * 128], BF16, kind="ExternalInput").ap()
    Va = nc.dram_tensor("Va", [2, 128, 20, 128], BF16, kind="ExternalInput").ap()
    QTb = nc.dram_tensor("QTb", [8, 128, NROW], BF16, kind="ExternalInput").ap()
    KTb = nc.dram_tensor("KTb", [8, 128, 22 * 128], BF16, kind="ExternalInput").ap()
    Vb = nc.dram_tensor("Vb", [8, 128, 22, 128], BF16, kind="ExternalInput").ap()
    maskA = nc.dram_tensor("maskA", [128, 4, 128], BF16, kind="ExternalInput").ap()
    sink = nc.dram_tensor("sink", [1, 8], F32, kind="ExternalInput").ap()
    biasB = nc.dram_tensor("biasB", [8, 128, NPAT * 128], F32, kind="ExternalInput").ap()
    maskB = nc.dram_tensor("maskB", [128, NPAT * 128], BF16, kind="ExternalInput").ap()
    att = nc.dram_tensor("att", [NROW, D], BF16, kind="ExternalOutput").ap()

    QTas = [cx.sb("QTas%d" % i, [128, 4, NROW], BF16) for i in range(2)]
    KTas = [cx.sb("KTas%d" % i, [128, 20 * 128], BF16) for i in range(2)]
    Vas = [cx.sb("Vas%d" % i, [128, 20, 129], BF16) for i in range(2)]
    QTbs = [cx.sb("QTbs%d" % i, [128, NROW], BF16) for i in range(2)]
    KTbs = [cx.sb("KTbs%d" % i, [128, 22 * 128], BF16) for i in range(2)]
    Vbs = [cx.sb("Vbs%d" % i, [128, 22, 129], BF16) for i in range(2)]
    mA = cx.sb("mA", [128, 4, 128], BF16)
    mB = cx.sb("mB", [128, NPAT * 128], BF16)
    bB = cx.sb("bB", [128, NPAT * 128], F32)
    eB = cx.sb("eB", [128, NPAT * 128], F32)
    E = [cx.sb("E%d" % i, [128, NPAT * 128], BF16) for i in range(2)]
    snk = cx.sb("snk", [128, 8], F32)
    esnk = cx.sb("esnk", [128, 8], F32)
    P = [cx.sb("P%d" % i, [128, 512], BF16) for i in range(3)]
    at = [cx.sb("at%d" % i, [128, 512], BF16) for i in range(2)]
    rec = cx.sb("rec", [128, 4], F32)
    Sps = [cx.ps("Sps%d" % i, [128, 512], F32) for i in range(2)]
    Ops = [cx.ps("Ops%d" % i, [128, 512], F32) for i in range(4)]

    for sl in range(2):
        S.op("gpsimd", lambda e, sl=sl: e.memset(Vas[sl][:, :, 128:129], 1.0), writes=[("Vaone", sl)])
        S.op("gpsimd", lambda e, sl=sl: e.memset(Vbs[sl][:, :, 128:129], 1.0), writes=[("Vbone", sl)])
    S.op("sync", lambda e: e.dma_start(out=mA[:], in_=maskA), writes=["mA"], dma=True)
    S.op("sync", lambda e: e.dma_start(out=mB[:], in_=maskB), writes=["mB"], dma=True)
    S.op("sync", lambda e: e.dma_start(out=snk[:], in_=sink.broadcast_to([128, 8])), writes=["snk"], dma=True)
    S.op("scalar", lambda e: e.activation(out=esnk[:], in_=snk[:], func=AF.Exp), reads=["snk"], writes=["esnk"])

    uid = 0
    for g in range(2):
        sl = g % 2
        S.op("sync", lambda e, sl=sl, g=g: e.dma_start(out=KTas[sl][:], in_=KTa[g]), writes=[("KTa", sl)], dma=True)
        S.op("sync", lambda e, sl=sl, g=g: e.dma_start(out=Vas[sl][:, :, 0:128], in_=Va[g]), writes=[("Va", sl)], dma=True)
        S.op("sync", lambda e, sl=sl, g=g: e.dma_start(out=QTas[sl][:], in_=QTa[g]), writes=[("QTa", sl)], dma=True)
        for qb in range(NBLK):
            pos = [qb, qb + 1, qb + 2, 18, 19] if qb < 16 else [18, 19]
            qmov = (QTas[sl][:, :, qb * 128:(qb + 1) * 128], [("QTa", sl)])
            kblocks = [(KTas[sl][:, p * 128:(p + 1) * 128], [("KTa", sl)],
                        Vas[sl][:, p, :], [("Va", sl), ("Vaone", sl)]) for p in pos]

            def post_exp(i, pb, qb=qb):
                if qb >= 16 or i not in (0, 2):
                    return
                mi = (0 if qb == 0 else 1) if i == 0 else (3 if qb == 15 else 2)
                S.op("vector", lambda e, pb=pb, mi=mi: e.tensor_tensor(
                    out=P[pb][:].rearrange("p (g q) -> p g q", g=4),
                    in0=P[pb][:].rearrange("p (g q) -> p g q", g=4),
                    in1=mA[:, mi:mi + 1, :].broadcast_to([128, 4, 128]), op=ALU.mult),
                    reads=[("P", pb), "mA"], writes=[("P", pb)])

            emit_attn_unit(S, qmov, kblocks, P, Sps, Ops, 4, uid, post_exp)
            uid += 1
            ab = uid % 2
            for j in range(4):
                hd = g * 4 + j
                S.op("vector", lambda e, j=j, hd=hd: e.tensor_tensor(
                    out=rec[:, j:j + 1], in0=Ops[j][:, 128:129], in1=esnk[:, hd:hd + 1], op=ALU.add),
                    reads=[("Ops", j), "esnk"], writes=[("rec", j)])
                S.op("vector", lambda e, j=j: e.reciprocal(out=rec[:, j:j + 1], in_=rec[:, j:j + 1]),
                     reads=[("rec", j)], writes=[("rec", j)])
                S.op("vector", lambda e, j=j, ab=ab: e.tensor_scalar(
                    out=at[ab][:, j * 128:(j + 1) * 128], in0=Ops[j][:, 0:128], scalar1=rec[:, j:j + 1],
                    scalar2=None, op0=ALU.mult),
                    reads=[("Ops", j), ("rec", j)], writes=[("at", ab)])
            S.op("sync", lambda e, ab=ab, qb=qb, g=g: e.dma_start(
                out=att[qb * 128:(qb + 1) * 128, g * 512:(g + 1) * 512], in_=at[ab][:]),
                reads=[("at", ab)], dma=True, is_out=True)

    for h in range(8):
        sl = h % 2
        S.op("sync", lambda e, sl=sl, h=h: e.dma_start(out=KTbs[sl][:], in_=KTb[h]), writes=[("KTb", sl)], dma=True)
        S.op("sync", lambda e, sl=sl, h=h: e.dma_start(out=Vbs[sl][:, :, 0:128], in_=Vb[h]), writes=[("Vb", sl)], dma=True)
        S.op("sync", lambda e, sl=sl, h=h: e.dma_start(out=QTbs[sl][:], in_=QTb[h]), writes=[("QTb", sl)], dma=True)
        S.op("sync", lambda e, h=h: e.dma_start(out=bB[:], in_=biasB[h]), writes=["bB"], dma=True)
        S.op("scalar", lambda e: e.activation(out=eB[:], in_=bB[:], func=AF.Exp), reads=["bB"], writes=["eB"])
        S.op("gpsimd", lambda e, sl=sl: e.tensor_tensor(out=E[sl][:], in0=eB[:], in1=mB[:], op=ALU.mult),
             reads=["eB", "mB"], writes=[("E", sl)])
        for qb in range(NBLK):
            pos = [qb + s for s in range(5)] + [20, 21] if qb < 16 else [20, 21]
            qmov = (QTbs[sl][:, qb * 128:(qb + 1) * 128], [("QTb", sl)])
            kblocks = [(KTbs[sl][:, p * 128:(p + 1) * 128], [("KTb", sl)],
                        Vbs[sl][:, p, :], [("Vb", sl), ("Vbone", sl)]) for p in pos]

            def post_exp(i, pb, qb=qb, sl=sl):
                if qb >= 16 or i >= 5:
                    return
                pi = pat_base(qb) + i
                S.op("vector", lambda e, pb=pb, pi=pi, sl=sl: e.tensor_tensor(
                    out=P[pb][:, 0:128], in0=P[pb][:, 0:128], in1=E[sl][:, pi * 128:(pi + 1) * 128],
                    op=ALU.mult), reads=[("P", pb), ("E", sl)], writes=[("P", pb)])

            emit_attn_unit(S, qmov, kblocks, P, Sps, Ops, 1, uid, post_exp)
            uid += 1
            ab = uid % 2
            S.op("vector", lambda e: e.reciprocal(out=rec[:, 0:1], in_=Ops[0][:, 128:129]),
                 reads=[("Ops", 0)], writes=[("rec", 0)])
            S.op("vector", lambda e, ab=ab: e.tensor_scalar(
                out=at[ab][:, 0:128], in0=Ops[0][:, 0:128], scalar1=rec[:, 0:1],
                scalar2=None, op0=ALU.mult),
                reads=[("Ops", 0), ("rec", 0)], writes=[("at", ab)])
            S.op("sync", lambda e, ab=ab, qb=qb, h=h: e.dma_start(
                out=att[qb * 128:(qb + 1) * 128, 1024 + h * 128:1024 + (h + 1) * 128], in_=at[ab][:, 0:128]),
                reads=[("at", ab)], dma=True, is_out=True)
    return finish_prog(nc, cx, S)


def nbr_geometry(c4):
    halo = [16 * c4 - 2 + p for p in range(20)]
    if c4 == 0:
        halo[0], halo[1] = 3, None
    if c4 == 3:
        halo[18], halo[19] = 60, None
    mask = np.zeros((128, NPAT, 128), bool)
    drow = np.zeros((128, NPAT, 128), np.int64)
    dcol = np.zeros((128, NPAT, 128), np.int64)
    k = np.arange(128)
    q = np.arange(128)
    for lb in [2, 0, 1, 14, 15]:
        m = 16 * c4 + lb
        seen = []
        for s in range(5):
            gb = halo[lb + s]
            pi = pat_base(lb) + s
            if gb is None or gb in seen or gb < 0 or gb > 63:
                continue
            seen.append(gb)
            kr = (2 * gb + k // 64)[:, None]
            kc = (k % 64)[:, None]
            qr = (2 * m + q // 64)[None, :]
            qc = (q % 64)[None, :]
            rstart = np.clip(qr - 4, 0, 120)
            wstart = np.clip(qc - 8, 0, 48)
            ok = (kr >= rstart) & (kr < rstart + 8) & (kc >= wstart) & (kc < wstart + 16)
            mask[:, pi, :] = ok
            drow[:, pi, :] = np.where(ok, kr - qr + 7, 0)
            dcol[:, pi, :] = np.clip(kc - qc + 15, 0, 30) * ok
    return halo, mask, drow, dcol


def run_p2a_ab(q_lat, q_ctx, sink, rpb):
    nc = get_prog("p2a_ab", build_p2a_ab)
    tri_prev = (np.arange(128)[:, None] >= np.arange(128)[None, :])
    tri_next = (np.arange(128)[:, None] <= np.arange(128)[None, :])
    zeros = np.zeros((128, 128), bool)
    in_maps = []
    for r in range(NCORES):
        b, c4 = r // 4, r % 4
        qt = to_headT(q_lat, q_ctx, r, 0, 8).reshape(2, 4, 128, NROW)
        QTa = np.ascontiguousarray(qt.transpose(0, 2, 1, 3))
        QTb = to_headT(q_lat, q_ctx, r, 12, 8)
        lat = q_lat[b].reshape(64, 128, 4608)
        cxb = q_ctx[b].reshape(2, 128, 4608)
        zb = np.zeros((128, 4608), NPBF)
        blocksA = []
        for p in range(18):
            gb = 16 * c4 - 1 + p
            blocksA.append(lat[gb] if 0 <= gb < 64 else zb)
        blocksA += [cxb[0], cxb[1]]
        A = np.stack(blocksA)
        ka = A[:, :, 1024:1280].reshape(20, 128, 2, 128)
        va = A[:, :, 1280:1536].reshape(20, 128, 2, 128)
        KTa = np.ascontiguousarray(ka.transpose(2, 3, 0, 1).reshape(2, 128, 20 * 128))
        Va = np.ascontiguousarray(va.transpose(2, 1, 0, 3))
        halo, mask, drow, dcol = nbr_geometry(c4)
        blocksB = [lat[gb] if gb is not None and 0 <= gb < 64 else zb for gb in halo] + [cxb[0], cxb[1]]
        Bk = np.stack(blocksB)
        kb = Bk[:, :, 2560:3584].reshape(22, 128, 8, 128)
        vb = Bk[:, :, 3584:4608].reshape(22, 128, 8, 128)
        KTb = np.ascontiguousarray(kb.transpose(2, 3, 0, 1).reshape(8, 128, 22 * 128))
        Vb = np.ascontiguousarray(vb.transpose(2, 1, 0, 3))
        mA = np.stack([zeros if c4 == 0 else tri_prev, tri_prev, tri_next, zeros if c4 == 3 else tri_next], 1)
        biasB = rpb[:, drow, dcol].astype(np.float32)
        in_maps.append({
            "QTa": QTa, "KTa": KTa, "Va": Va, "QTb": QTb, "KTb": KTb, "Vb": Vb,
            "maskA": np.ascontiguousarray(mA).astype(np.float32).astype(NPBF),
            "sink": np.ascontiguousarray(sink.reshape(1, 8)),
            "biasB": np.ascontiguousarray(biasB.reshape(8, 128, NPAT * 128)),
            "maskB": mask.reshape(128, NPAT * 128).astype(np.float32).astype(NPBF)})
    res = run(nc, in_maps)
    return gather_rows([res[r]["att"] for r in range(NCORES)], D, NPBF)


ALPHA = float((2 * 4) ** 0.25)
LN_EPS = 1e-5


def emit_ln(S, zt, zk, stats, mv, lng, lnb, outt, outk, tag):
    for c in range(4):
        S.op("vector", lambda e, c=c: e.bn_stats(out=stats[:, c, :], in_=zt[:, c * 512:(c + 1) * 512]),
             reads=[zk], writes=[("stats", tag)])
    S.op("vector", lambda e: e.bn_aggr(out=mv[:, 0:2], in_=stats[:].rearrange("p c s -> p (c s)")),
         reads=[("stats", tag)], writes=[("mv", tag)])
    S.op("vector", lambda e: e.tensor_scalar(out=mv[:, 2:3], in0=mv[:, 1:2], scalar1=LN_EPS, scalar2=None,
                                             op0=ALU.add), reads=[("mv", tag)], writes=[("mv", tag)])
    S.op("scalar", lambda e: e.activation(out=mv[:, 2:3], in_=mv[:, 2:3], func=AF.Sqrt),
         reads=[("mv", tag)], writes=[("mv", tag)])
    S.op("vector", lambda e: e.reciprocal(out=mv[:, 2:3], in_=mv[:, 2:3]),
         reads=[("mv", tag)], writes=[("mv", tag)])
    S.op("vector", lambda e: e.tensor_scalar(out=zt[:], in0=zt[:], scalar1=mv[:, 0:1], scalar2=mv[:, 2:3],
                                             op0=ALU.subtract, op1=ALU.mult),
         reads=[zk, ("mv", tag)], writes=[zk])
    S.op("gpsimd", lambda e: e.tensor_tensor(out=zt[:], in0=zt[:], in1=lng[:], op=ALU.mult),
         reads=[zk, "lng"], writes=[zk])
    S.op("vector", lambda e: e.tensor_tensor(out=outt[:], in0=zt[:], in1=lnb[:], op=ALU.add),
         reads=[zk, "lnb"], writes=[outk])


def build_p2b():
    nc, cx, S = new_prog()
    emit_p2b(nc, cx, S)
    return finish_prog(nc, cx, S)


def emit_p2b(nc, cx, S):
    attT = nc.dram_tensor("attT", [NBLK, 128, 16, 128], BF16, kind="ExternalInput").ap()
    x = nc.dram_tensor("x", [NROW, D], F32, kind="ExternalInput").ap()
    w = nc.dram_tensor("wout", [D, D], F32, kind="ExternalInput").ap()
    g1 = nc.dram_tensor("g1", [2, D], F32, kind="ExternalInput").ap()
    lngd = nc.dram_tensor("lng", [1, D], F32, kind="ExternalInput").ap()
    lnbd = nc.dram_tensor("lnb", [1, D], F32, kind="ExternalInput").ap()
    out = nc.dram_tensor("out", [NROW, D], F32, kind="ExternalOutput").ap()

    wob = cx.sb("wob", [128, 16, D], BF16)
    wf = [cx.sb("wf%d" % i, [128, D], F32) for i in range(2)]
    aT = [cx.sb("aT%d" % i, [128, 16, 128], BF16) for i in range(2)]
    xt = [cx.sb("xt%d" % i, [128, D], F32) for i in range(2)]
    zt = [cx.sb("zt%d" % i, [128, D], F32) for i in range(2)]
    G1 = cx.sb("G1", [128, D], F32)
    lng = cx.sb("lng_sb", [128, D], F32)
    lnb = cx.sb("lnb_sb", [128, D], F32)
    stats = cx.sb("stats", [128, 4, 6], F32)
    mv = cx.sb("mv", [128, 4], F32)
    yps = [cx.ps("yps%d" % i, [128, D], F32) for i in range(2)]

    S.op("sync", lambda e: e.dma_start(out=lng[:], in_=lngd.broadcast_to([128, D])), writes=["lng"], dma=True)
    S.op("sync", lambda e: e.dma_start(out=lnb[:], in_=lnbd.broadcast_to([128, D])), writes=["lnb"], dma=True)
    for ch in range(16):
        fs = ch % 2
        S.op("sync", lambda e, fs=fs, ch=ch: e.dma_start(out=wf[fs][:], in_=w[ch * 128:(ch + 1) * 128, :]),
             writes=[("wf", fs)], dma=True)
        S.op("gpsimd", lambda e, fs=fs, ch=ch: e.tensor_copy(out=wob[:, ch, :], in_=wf[fs][:]),
             reads=[("wf", fs)], writes=["wob"])
    for blk in range(NBLK):
        b = blk % 2
        if blk == 0 or blk == NBLK - 1:
            mi = 0 if blk == 0 else 1
            S.op("sync", lambda e, mi=mi: e.dma_start(out=G1[:], in_=g1[mi:mi + 1, :].broadcast_to([128, D])),
                 writes=["G1"], dma=True)
        S.op("sync", lambda e, b=b, blk=blk: e.dma_start(out=aT[b][:], in_=attT[blk]), writes=[("aT", b)], dma=True)
        S.op("sync", lambda e, b=b, blk=blk: e.dma_start(out=xt[b][:], in_=x[blk * 128:(blk + 1) * 128, :]),
             writes=[("xt", b)], dma=True)
        for cb in range(4):
            for h in range(16):
                S.op("tensor", lambda e, b=b, cb=cb, h=h: e.matmul(
                    yps[b][:, cb * 512:(cb + 1) * 512], lhsT=aT[b][:, h, :],
                    rhs=wob[:, h, cb * 512:(cb + 1) * 512], start=(h == 0), stop=(h == 15)),
                    reads=[("aT", b), "wob"], writes=[("yps", b, cb)])
        for cb in range(4):
            S.op("vector", lambda e, b=b, cb=cb: e.tensor_tensor(
                out=zt[b][:, cb * 512:(cb + 1) * 512], in0=yps[b][:, cb * 512:(cb + 1) * 512],
                in1=G1[:, cb * 512:(cb + 1) * 512], op=ALU.mult),
                reads=[("yps", b, cb), "G1"], writes=[("zt", b)])
        S.op("vector", lambda e, b=b: e.scalar_tensor_tensor(
            out=zt[b][:], in0=xt[b][:], scalar=ALPHA, in1=zt[b][:], op0=ALU.mult, op1=ALU.add),
            reads=[("xt", b), ("zt", b)], writes=[("zt", b)])
        emit_ln(S, zt[b], ("zt", b), stats, mv, lng, lnb, xt[b], ("xt", b), 0)
        S.op("sync", lambda e, b=b, blk=blk: e.dma_start(out=out[blk * 128:(blk + 1) * 128, :], in_=xt[b][:]),
             reads=[("xt", b)], writes=[("x1dram", blk)], dma=True, is_out=True)
    return out


def run_p2b(att_lat, att_ctx, x, xc, mod_l, w_out, ln_g, ln_b):
    nc = get_prog("p2b", build_p2b)
    m = split_mod(mod_l)
    in_maps = []
    for r in range(NCORES):
        b = r // 4
        rows = core_rows(att_lat, att_ctx, r)
        aT = np.ascontiguousarray(rows.reshape(NBLK, 128, 16, 128).transpose(0, 3, 2, 1))
        in_maps.append({"attT": aT, "x": core_rows(x, xc, r), "wout": w_out,
                        "g1": np.ascontiguousarray(np.stack([m["g1"][b], m["g1"][2]])),
                        "lng": np.ascontiguousarray(ln_g.reshape(1, D)),
                        "lnb": np.ascontiguousarray(ln_b.reshape(1, D))})
    res = run(nc, in_maps)
    return gather_rows([res[r]["out"] for r in range(NCORES)], D, np.float32)


import os
P3A_STOP = int(os.environ.get('P3A_STOP', '0'))
P3A_VAR = int(os.environ.get('P3A_VAR', '0'))
P3A_NB = int(os.environ.get('P3A_NB', '17'))


def build_p3a():
    nc, cx, S = new_prog()
    emit_p3a(nc, cx, S, None, True)
    return finish_prog(nc, cx, S)


def build_p23():
    nc, cx, S = new_prog()
    mark = cx.mark()
    cx.prefix = "a_"
    out = emit_p2b(nc, cx, S)
    barrier(S)
    cx.release(mark)
    cx.prefix = "b_"
    emit_p3a(nc, cx, S, out, False)
    return finish_prog(nc, cx, S)


def emit_p3a(nc, cx, S, x, want_h2):
    if x is None:
        x = nc.dram_tensor("x", [NROW, D], F32, kind="ExternalInput").ap()
    modv = nc.dram_tensor("modv", [2, 2, D], F32, kind="ExternalInput").ap()
    w = nc.dram_tensor("pwq", [D, D], F32, kind="ExternalInput").ap()
    skt = nc.dram_tensor("skt", [128, 16 * 128], F32, kind="ExternalInput").ap()
    ident = nc.dram_tensor("ident", [128, 128], BF16, kind="ExternalInput").ap()
    iot = nc.dram_tensor("iot", [1, 16], F32, kind="ExternalInput").ap()
    h2o = nc.dram_tensor("h2", [NROW, D], F32, kind="ExternalOutput").ap() if want_h2 else None
    idxo = nc.dram_tensor("idx", [NROW, 128], I32, kind="ExternalOutput").ap()
    gwo = nc.dram_tensor("gw", [NROW, 128], F32, kind="ExternalOutput").ap()

    id_sb = cx.sb("id_sb", [128, 128], BF16)
    io16 = cx.sb("io16", [128, 16], F32)
    wqb = cx.sb("wqb", [128, 16, D], BF16)
    wf = cx.sb("wf", [128, D], F32)
    skb = cx.sb("skb", [128, 16, 128], BF16)
    Mt = cx.sb("Mt", [128, D], F32)
    SHt = cx.sb("SHt", [128, D], F32)
    xt = [cx.sb("xt%d" % i, [128, D], F32) for i in range(2)]
    hb = cx.sb("hb", [128, D], BF16)
    hT = cx.sb("hT", [128, 16, 128], BF16)
    qsb = cx.sb("qsb", [128, D], BF16)
    qT = cx.sb("qT", [128, 16, 128], BF16)
    s1 = cx.sb("s1", [128, D], F32)
    s2 = cx.sb("s2", [128, D], F32)
    tv = cx.sb("tv", [128, 16, 16], F32)
    ti = cx.sb("ti", [128, 16, 16], U32)
    tif = cx.sb("tif", [128, 16, 16], F32)
    cand = cx.sb("cand", [128, 8, 256], F32)
    cand2 = cx.sb("cand2", [128, 8, 256], F32)
    cv = cx.sb("cv", [128, 8, 16], F32)
    cpos = cx.sb("cpos", [128, 8, 16], U32)
    pu = cx.sb("pu", [128, 8, 16], U32)
    pa = cx.sb("pa", [128, 8, 16], F32)
    pbf = cx.sb("pbf", [128, 8, 16], F32)
    sel = cx.sb("sel", [128, 8, 16, 16], F32)
    I1 = cx.sb("I1", [128, 8, 16], F32)
    I2 = cx.sb("I2", [128, 8, 16], F32)
    ef = cx.sb("ef", [128, 8, 16], F32)
    ei = [cx.sb("ei%d" % i, [128, 128], I32) for i in range(2)]
    gs = cx.sb("gs", [128, 8], F32)
    gt = [cx.sb("gt%d" % i, [128, 8, 16], F32) for i in range(2)]
    pT = [cx.ps("pT%d" % i, [128, 1024], BF16) for i in range(2)]
    qps = cx.ps("qps", [128, D], F32)

    S.op("sync", lambda e: e.dma_start(out=id_sb[:], in_=ident), writes=["id"], dma=True)
    S.op("sync", lambda e: e.dma_start(out=io16[:], in_=iot.broadcast_to([128, 16])), writes=["io16"], dma=True)
    S.op("sync", lambda e: e.dma_start(out=wf[:], in_=skt), writes=["wf"], dma=True)
    S.op("gpsimd", lambda e: e.tensor_copy(out=skb[:].rearrange("p a b -> p (a b)"), in_=wf[:]),
         reads=["wf"], writes=["skb"])
    for ch in range(16):
        S.op("sync", lambda e, ch=ch: e.dma_start(out=wf[:], in_=w[ch * 128:(ch + 1) * 128, :]),
             writes=["wf"], dma=True)
        S.op("gpsimd", lambda e, ch=ch: e.tensor_copy(out=wqb[:, ch, :], in_=wf[:]),
             reads=["wf"], writes=["wqb"])

    for blk in range(min(NBLK, P3A_NB)):
        b = blk % 2
        if blk == 0 or blk == NBLK - 1:
            mi = 0 if blk == 0 else 1
            S.op("sync", lambda e, mi=mi: e.dma_start(
                out=Mt[:], in_=modv[mi, 0:1, :].broadcast_to([128, D])), writes=["Mt"], dma=True)
            S.op("sync", lambda e, mi=mi: e.dma_start(
                out=SHt[:], in_=modv[mi, 1:2, :].broadcast_to([128, D])), writes=["SHt"], dma=True)
            S.op("gpsimd", lambda e: e.tensor_scalar(
                out=Mt[:], in0=Mt[:], scalar1=1.0, scalar2=None, op0=ALU.add),
                reads=["Mt"], writes=["Mt"])
        S.op("sync", lambda e, b=b, blk=blk: e.dma_start(
            out=xt[b][:], in_=x[blk * 128:(blk + 1) * 128, :]), reads=[("x1dram", blk)], writes=[("xt", b)], dma=True)
        S.op("gpsimd", lambda e, b=b: e.tensor_tensor(out=xt[b][:], in0=xt[b][:], in1=Mt[:], op=ALU.mult),
             reads=[("xt", b), "Mt"], writes=[("xt", b)])
        if P3A_VAR == 2:
            S.op("vector", lambda e, b=b: e.tensor_tensor(out=hb[:], in0=xt[b][:], in1=SHt[:], op=ALU.add),
                 reads=[("xt", b), "SHt"], writes=["hb"])
        else:
            S.op("vector", lambda e, b=b: e.tensor_tensor(out=xt[b][:], in0=xt[b][:], in1=SHt[:], op=ALU.add),
                 reads=[("xt", b), "SHt"], writes=[("xt", b)])
        if want_h2 and P3A_VAR != 2:
            S.op("sync", lambda e, b=b, blk=blk: e.dma_start(out=h2o[blk * 128:(blk + 1) * 128, :], in_=xt[b][:]),
                 reads=[("xt", b)], dma=True, is_out=True)
        if P3A_STOP == 1:
            continue
        if P3A_VAR == 2:
            pass
        elif P3A_VAR == 1:
            S.op("vector", lambda e, b=b: e.tensor_copy(out=hb[:], in_=xt[b][:]), reads=[("xt", b)], writes=["hb"])
        else:
            S.op("scalar", lambda e, b=b: e.copy(out=hb[:], in_=xt[b][:]), reads=[("xt", b)], writes=["hb"])
        if P3A_STOP == 10:
            continue
        for half in range(2):
            for j in range(8):
                ch = half * 8 + j
                S.op("tensor", lambda e, half=half, j=j, ch=ch: e.transpose(
                    out=pT[half][:, j * 128:(j + 1) * 128], in_=hb[:, ch * 128:(ch + 1) * 128],
                    identity=id_sb[:]), reads=["hb", "id"], writes=[("pT", half)])
            S.op("scalar", lambda e, half=half: e.copy(
                out=hT[:, half * 8:(half + 1) * 8, :], in_=pT[half][:].rearrange("p (c t) -> p c t", c=8)),
                reads=[("pT", half)], writes=["hT"])
        if P3A_STOP == 2:
            continue
        for cb in range(4):
            for ch in range(16):
                S.op("tensor", lambda e, cb=cb, ch=ch: e.matmul(
                    qps[:, cb * 512:(cb + 1) * 512], lhsT=hT[:, ch, :],
                    rhs=wqb[:, ch, cb * 512:(cb + 1) * 512], start=(ch == 0), stop=(ch == 15)),
                    reads=["hT", "wqb"], writes=[("qps", cb)])
            S.op("scalar", lambda e, cb=cb: e.copy(out=qsb[:, cb * 512:(cb + 1) * 512],
                                                   in_=qps[:, cb * 512:(cb + 1) * 512]),
                 reads=[("qps", cb)], writes=["qsb"])
        if P3A_STOP == 3:
            continue
        for half in range(2):
            for j in range(8):
                hp = half * 8 + j
                S.op("tensor", lambda e, half=half, j=j, hp=hp: e.transpose(
                    out=pT[half][:, j * 128:(j + 1) * 128], in_=qsb[:, hp * 128:(hp + 1) * 128],
                    identity=id_sb[:]), reads=["qsb", "id"], writes=[("pT", half)])
            S.op("scalar", lambda e, half=half: e.copy(
                out=qT[:, half * 8:(half + 1) * 8, :], in_=pT[half][:].rearrange("p (c t) -> p c t", c=8)),
                reads=[("pT", half)], writes=["qT"])
        if P3A_STOP == 4:
            continue
        for hp in range(16):
            cb = hp // 4
            S.op("tensor", lambda e, hp=hp: e.matmul(
                qps[:, hp * 128:(hp + 1) * 128], lhsT=qT[:, hp, :], rhs=skb[:, hp, :], start=True, stop=True),
                reads=["qT", "skb"], writes=[("qps", cb)])
        for cb in range(4):
            S.op("scalar", lambda e, cb=cb: e.copy(out=s1[:, cb * 512:(cb + 1) * 512],
                                                   in_=qps[:, cb * 512:(cb + 1) * 512]),
                 reads=[("qps", cb)], writes=["s1"])
        if P3A_STOP == 5:
            continue
        for hp in range(16):
            sl = slice(hp * 128, (hp + 1) * 128)
            S.op("vector", lambda e, hp=hp, sl=sl: e.max(out=tv[:, hp, 0:8], in_=s1[:, sl]),
                 reads=["s1"], writes=["tv"])
            S.op("vector", lambda e, hp=hp, sl=sl: e.max_index(out=ti[:, hp, 0:8], in_max=tv[:, hp, 0:8],
                                                               in_values=s1[:, sl]),
                 reads=["s1", "tv"], writes=["ti"])
            S.op("vector", lambda e, hp=hp, sl=sl: e.match_replace(
                out=s2[:, sl], in_to_replace=tv[:, hp, 0:8], in_values=s1[:, sl], imm_value=-1e30),
                reads=["s1", "tv"], writes=["s2"])
            S.op("vector", lambda e, hp=hp, sl=sl: e.max(out=tv[:, hp, 8:16], in_=s2[:, sl]),
                 reads=["s2"], writes=["tv"])
            S.op("vector", lambda e, hp=hp, sl=sl: e.max_index(out=ti[:, hp, 8:16], in_max=tv[:, hp, 8:16],
                                                               in_values=s2[:, sl]),
                 reads=["s2", "tv"], writes=["ti"])
        if P3A_STOP == 6:
            continue
        tv4 = tv[:].rearrange("p (h two) k -> p h two k", two=2)
        c4v = cand[:].rearrange("p h (i j) -> p h i j", j=16)
        S.op("vector", lambda e: e.tensor_tensor(
            out=c4v, in0=tv4[:, :, 0, :].unsqueeze(3).broadcast_to([128, 8, 16, 16]),
            in1=tv4[:, :, 1, :].unsqueeze(2).broadcast_to([128, 8, 16, 16]), op=ALU.add),
            reads=["tv"], writes=["cand"])
        for h in range(8):
            S.op("vector", lambda e, h=h: e.max(out=cv[:, h, 0:8], in_=cand[:, h, :]),
                 reads=["cand"], writes=["cv"])
            S.op("vector", lambda e, h=h: e.max_index(out=cpos[:, h, 0:8], in_max=cv[:, h, 0:8],
                                                      in_values=cand[:, h, :]),
                 reads=["cand", "cv"], writes=["cpos"])
            S.op("vector", lambda e, h=h: e.match_replace(
                out=cand2[:, h, :], in_to_replace=cv[:, h, 0:8], in_values=cand[:, h, :], imm_value=-1e30),
                reads=["cand", "cv"], writes=["cand2"])
            S.op("vector", lambda e, h=h: e.max(out=cv[:, h, 8:16], in_=cand2[:, h, :]),
                 reads=["cand2"], writes=["cv"])
            S.op("vector", lambda e, h=h: e.max_index(out=cpos[:, h, 8:16], in_max=cv[:, h, 8:16],
                                                      in_values=cand2[:, h, :]),
                 reads=["cand2", "cv"], writes=["cpos"])
        if P3A_STOP == 7:
            continue
        gb = blk % 2
        S.op("vector", lambda e, gb=gb: e.tensor_tensor(
            out=gt[gb][:], in0=cv[:], in1=cv[:, :, 0:1].broadcast_to([128, 8, 16]), op=ALU.subtract),
            reads=["cv"], writes=[("gt", gb)])
        S.op("scalar", lambda e, gb=gb: e.activation(out=gt[gb][:], in_=gt[gb][:], func=AF.Exp),
             reads=[("gt", gb)], writes=[("gt", gb)])
        S.op("vector", lambda e, gb=gb: e.tensor_reduce(out=gs[:], in_=gt[gb][:], axis=AX.X, op=ALU.add),
             reads=[("gt", gb)], writes=["gs"])
        S.op("vector", lambda e: e.reciprocal(out=gs[:], in_=gs[:]), reads=["gs"], writes=["gs"])
        S.op("vector", lambda e, gb=gb: e.tensor_tensor(
            out=gt[gb][:], in0=gt[gb][:], in1=gs[:].unsqueeze(2).broadcast_to([128, 8, 16]), op=ALU.mult),
            reads=[("gt", gb), "gs"], writes=[("gt", gb)])
        S.op("sync", lambda e, gb=gb, blk=blk: e.dma_start(
            out=gwo[blk * 128:(blk + 1) * 128, :], in_=gt[gb][:].rearrange("p h k -> p (h k)")),
            reads=[("gt", gb)], dma=True, is_out=True)
        if P3A_STOP == 8:
            continue
        S.op("vector", lambda e: e.tensor_copy(out=tif[:], in_=ti[:]), reads=["ti"], writes=["tif"])
        tif4 = tif[:].rearrange("p (h two) k -> p h two k", two=2)
        for (which, dst) in [(0, I1), (1, I2)]:
            if which == 0:
                S.op("vector", lambda e: e.tensor_single_scalar(out=pu[:], in_=cpos[:], scalar=4,
                                                                op=ALU.logical_shift_right),
                     reads=["cpos"], writes=["pu"])
            else:
                S.op("vector", lambda e: e.tensor_single_scalar(out=pu[:], in_=cpos[:], scalar=15,
                                                                op=ALU.bitwise_and),
                     reads=["cpos"], writes=["pu"])
            S.op("vector", lambda e: e.tensor_copy(out=pa[:], in_=pu[:]), reads=["pu"], writes=["pa"])
            S.op("vector", lambda e: e.tensor_tensor(
                out=sel[:], in0=io16[:].unsqueeze(1).unsqueeze(1).broadcast_to([128, 8, 16, 16]),
                in1=pa[:].unsqueeze(3).broadcast_to([128, 8, 16, 16]), op=ALU.is_equal),
                reads=["io16", "pa"], writes=["sel"])
            S.op("vector", lambda e, which=which: e.tensor_tensor(
                out=sel[:], in0=sel[:], in1=tif4[:, :, which, :].unsqueeze(2).broadcast_to([128, 8, 16, 16]),
                op=ALU.mult), reads=["sel", "tif"], writes=["sel"])
            S.op("vector", lambda e, dst=dst: e.tensor_reduce(out=dst[:], in_=sel[:], axis=AX.X, op=ALU.add),
                 reads=["sel"], writes=["I%d" % which])
        S.op("vector", lambda e: e.scalar_tensor_tensor(
            out=ef[:].rearrange("p h k -> p (h k)"), in0=I1[:].rearrange("p h k -> p (h k)"), scalar=128.0,
            in1=I2[:].rearrange("p h k -> p (h k)"), op0=ALU.mult, op1=ALU.add),
            reads=["I0", "I1"], writes=["ef"])
        S.op("vector", lambda e, gb=gb: e.tensor_copy(out=ei[gb][:], in_=ef[:].rearrange("p h k -> p (h k)")),
             reads=["ef"], writes=[("ei", gb)])
        S.op("sync", lambda e, gb=gb, blk=blk: e.dma_start(out=idxo[blk * 128:(blk + 1) * 128, :], in_=ei[gb][:]),
             reads=[("ei", gb)], dma=True, is_out=True)


def run_p3a(x1, x1c, mod_l, wq, subkeys):
    nc = get_prog("p3a", build_p3a)
    m = split_mod(mod_l)
    ident = np.eye(128, dtype=np.float32).astype(NPBF)
    skt = np.ascontiguousarray(subkeys.reshape(16, 128, 128).transpose(2, 0, 1).reshape(128, 16 * 128))
    iot = np.arange(16, dtype=np.float32).reshape(1, 16)
    in_maps = []
    for r in range(NCORES):
        b = r // 4
        modv = np.stack([np.stack([m["sc2"][b], m["sh2"][b]]), np.stack([m["sc2"][2], m["sh2"][2]])])
        in_maps.append({"x": core_rows(x1, x1c, r), "modv": np.ascontiguousarray(modv), "pwq": wq,
                        "skt": skt, "ident": ident, "iot": iot})
    res = run(nc, in_maps)
    return res


NEXP = 16384
NUB = 4


def build_p3b(NBLK=NBLK, NEXP=NEXP):
    NROW = NBLK * 128
    nc, cx, S = new_prog()
    x1d = nc.dram_tensor("x1", [NROW, D], F32, kind="ExternalInput").ap()
    idxd = nc.dram_tensor("idx", [NROW, 128], I32, kind="ExternalInput").ap()
    gwd = nc.dram_tensor("gw", [NROW, 128], F32, kind="ExternalInput").ap()
    ud = nc.dram_tensor("u", [NEXP, D], F32, kind="ExternalInput").ap()
    vd = nc.dram_tensor("v", [NEXP, D], F32, kind="ExternalInput").ap()
    modv = nc.dram_tensor("modv", [2, 3, D], F32, kind="ExternalInput").ap()
    lngd = nc.dram_tensor("lng", [1, D], F32, kind="ExternalInput").ap()
    lnbd = nc.dram_tensor("lnb", [1, D], F32, kind="ExternalInput").ap()
    out = nc.dram_tensor("out", [NROW, D], F32, kind="ExternalOutput").ap()

    h2 = [cx.sb("h2t%d" % i, [128, D], F32) for i in range(2)]
    x1 = [cx.sb("x1t%d" % i, [128, D], F32) for i in range(2)]
    it = [cx.sb("it%d" % i, [128, 128], I32) for i in range(2)]
    gw = [cx.sb("gw%d" % i, [128, 128], F32) for i in range(2)]
    U = [cx.sb("U%d" % i, [128, D], F32) for i in range(NUB)]
    Vt = [cx.sb("V%d" % i, [128, D], F32) for i in range(NUB)]
    junk = cx.sb("junk", [128, D], F32)
    act = cx.sb("act", [128, 128], F32)
    wt = cx.sb("wt", [128, 128], F32)
    y = cx.sb("y", [128, D], F32)
    Mt = cx.sb("Mt", [128, D], F32)
    SHt = cx.sb("SHt", [128, D], F32)
    G2 = cx.sb("G2", [128, D], F32)
    lng = cx.sb("lng_sb", [128, D], F32)
    lnb = cx.sb("lnb_sb", [128, D], F32)
    stats = cx.sb("stats", [128, 4, 6], F32)
    mv = cx.sb("mv", [128, 4], F32)

    S.op("sync", lambda e: e.dma_start(out=lng[:], in_=lngd.broadcast_to([128, D])), writes=["lng"], dma=True)
    S.op("sync", lambda e: e.dma_start(out=lnb[:], in_=lnbd.broadcast_to([128, D])), writes=["lnb"], dma=True)
    nu = 0
    nv = 0
    cur = None
    for blk in range(NBLK):
        b = blk % 2
        mi = 1 if blk % 17 == 16 else 0
        if mi != cur:
            cur = mi
            S.op("sync", lambda e, mi=mi: e.dma_start(out=Mt[:], in_=modv[mi, 0:1, :].broadcast_to([128, D])),
                 writes=["Mt"], dma=True)
            S.op("sync", lambda e, mi=mi: e.dma_start(out=SHt[:], in_=modv[mi, 1:2, :].broadcast_to([128, D])),
                 writes=["SHt"], dma=True)
            S.op("sync", lambda e, mi=mi: e.dma_start(out=G2[:], in_=modv[mi, 2:3, :].broadcast_to([128, D])),
                 writes=["G2"], dma=True)
            S.op("vector", lambda e: e.tensor_scalar(out=Mt[:], in0=Mt[:], scalar1=1.0, scalar2=None, op0=ALU.add),
                 reads=["Mt"], writes=["Mt"])
        rs = slice(blk * 128, (blk + 1) * 128)
        S.op("sync", lambda e, b=b, rs=rs: e.dma_start(out=x1[b][:], in_=x1d[rs, :]), writes=[("x1", b)], dma=True)
        S.op("sync", lambda e, b=b, rs=rs: e.dma_start(out=it[b][:], in_=idxd[rs, :]), writes=[("it", b)], dma=True)
        S.op("sync", lambda e, b=b, rs=rs: e.dma_start(out=gw[b][:], in_=gwd[rs, :]), writes=[("gw", b)], dma=True)
        S.op("vector", lambda e, b=b: e.tensor_tensor(out=h2[b][:], in0=x1[b][:], in1=Mt[:], op=ALU.mult),
             reads=[("x1", b), "Mt"], writes=[("h2", b)])
        S.op("vector", lambda e, b=b: e.tensor_tensor(out=h2[b][:], in0=h2[b][:], in1=SHt[:], op=ALU.add),
             reads=[("h2", b), "SHt"], writes=[("h2", b)])
        for s in range(128):
            ub = nu % NUB
            nu += 1
            S.op("gpsimd", lambda e, ub=ub, b=b, s=s: e.indirect_dma_start(
                out=U[ub][:], out_offset=None, in_=ud,
                in_offset=bass.IndirectOffsetOnAxis(ap=it[b][:, s:s + 1], axis=0)),
                reads=[("it", b)], writes=[("U", ub)], dma=True)
            S.op("vector", lambda e, ub=ub, b=b, s=s: e.scalar_tensor_tensor(
                out=junk[:], in0=U[ub][:], scalar=1.0, in1=h2[b][:], op0=ALU.mult, op1=ALU.mult,
                accum_out=act[:, s:s + 1]),
                reads=[("U", ub), ("h2", b)], writes=["junk", "act"])
        S.op("scalar", lambda e: e.activation(out=wt[:], in_=act[:], func=AF.Gelu), reads=["act"], writes=["wt"])
        S.op("vector", lambda e, b=b: e.tensor_tensor(out=wt[:], in0=wt[:], in1=gw[b][:], op=ALU.mult),
             reads=["wt", ("gw", b)], writes=["wt"])
        for s in range(128):
            vb = nv % NUB
            nv += 1
            S.op("gpsimd", lambda e, vb=vb, b=b, s=s: e.indirect_dma_start(
                out=Vt[vb][:], out_offset=None, in_=vd,
                in_offset=bass.IndirectOffsetOnAxis(ap=it[b][:, s:s + 1], axis=0)),
                reads=[("it", b)], writes=[("V", vb)], dma=True)
            if s == 0:
                S.op("vector", lambda e, vb=vb, s=s: e.tensor_scalar(
                    out=y[:], in0=Vt[vb][:], scalar1=wt[:, s:s + 1], scalar2=None, op0=ALU.mult),
                    reads=[("V", vb), "wt"], writes=["y"])
            else:
                S.op("vector", lambda e, vb=vb, s=s: e.scalar_tensor_tensor(
                    out=y[:], in0=Vt[vb][:], scalar=wt[:, s:s + 1], in1=y[:], op0=ALU.mult, op1=ALU.add),
                    reads=[("V", vb), "wt", "y"], writes=["y"])
        S.op("vector", lambda e: e.tensor_tensor(out=y[:], in0=y[:], in1=G2[:], op=ALU.mult),
             reads=["y", "G2"], writes=["y"])
        S.op("vector", lambda e, b=b: e.scalar_tensor_tensor(
            out=y[:], in0=x1[b][:], scalar=ALPHA, in1=y[:], op0=ALU.mult, op1=ALU.add),
            reads=[("x1", b), "y"], writes=["y"])
        emit_ln(S, y, "y", stats, mv, lng, lnb, x1[b], ("x1", b), 0)
        S.op("sync", lambda e, b=b, rs=rs: e.dma_start(out=out[rs, :], in_=x1[b][:]),
             reads=[("x1", b)], dma=True, is_out=True)
    return finish_prog(nc, cx, S)


P3B_CORES = 8


def run_p3b(idx_list, gw_list, x1, x1c, mod_l, u, v, ln_g, ln_b):
    per = NCORES // P3B_CORES
    nc = get_prog("p3b", build_p3b, NBLK * per, NEXP)
    m = split_mod(mod_l)
    in_maps = []
    for p in range(P3B_CORES):
        lcs = list(range(p * per, (p + 1) * per))
        b = lcs[0] // 4
        modv = np.stack([np.stack([m["sc2"][b], m["sh2"][b], m["g2"][b]]),
                         np.stack([m["sc2"][2], m["sh2"][2], m["g2"][2]])])
        in_maps.append({"x1": np.concatenate([core_rows(x1, x1c, r) for r in lcs], 0),
                        "idx": np.concatenate([idx_list[r] for r in lcs], 0),
                        "gw": np.concatenate([gw_list[r] for r in lcs], 0),
                        "u": u, "v": v, "modv": np.ascontiguousarray(modv),
                        "lng": np.ascontiguousarray(ln_g.reshape(1, D)),
                        "lnb": np.ascontiguousarray(ln_b.reshape(1, D))})
    res = run_bass_kernel_spmd(nc, in_maps, core_ids=list(range(P3B_CORES))).results
    outs = []
    for p in range(P3B_CORES):
        o = res[p]["out"]
        for j in range(per):
            outs.append(o[j * NROW:(j + 1) * NROW])
    return gather_rows(outs, D, np.float32)


def run_p23(att_lat, att_ctx, x, xc, mod_l, w_out, ln_g, ln_b, wq, subkeys):
    nc = get_prog("p23", build_p23)
    m = split_mod(mod_l)
    ident = np.eye(128, dtype=np.float32).astype(NPBF)
    skt = np.ascontiguousarray(subkeys.reshape(16, 128, 128).transpose(2, 0, 1).reshape(128, 16 * 128))
    iot = np.arange(16, dtype=np.float32).reshape(1, 16)
    in_maps = []
    for r in range(NCORES):
        b = r // 4
        rows = core_rows(att_lat, att_ctx, r)
        aT = np.ascontiguousarray(rows.reshape(NBLK, 128, 16, 128).transpose(0, 3, 2, 1))
        modv = np.stack([np.stack([m["sc2"][b], m["sh2"][b]]), np.stack([m["sc2"][2], m["sh2"][2]])])
        in_maps.append({"attT": aT, "x": core_rows(x, xc, r), "wout": w_out,
                        "g1": np.ascontiguousarray(np.stack([m["g1"][b], m["g1"][2]])),
                        "lng": np.ascontiguousarray(ln_g.reshape(1, D)),
                        "lnb": np.ascontiguousarray(ln_b.reshape(1, D)),
                        "modv": np.ascontiguousarray(modv), "pwq": wq, "skt": skt, "ident": ident, "iot": iot})
    res = run(nc, in_maps)
    x1, x1c = gather_rows([res[r]["out"] for r in range(NCORES)], D, np.float32)
    return x1, x1c, [res[r]["idx"] for r in range(NCORES)], [res[r]["gw"] for r in range(NCORES)]


def kernel(x, c, ctx, c_ctx, mod_w, mod_b, ln_g, ln_b, ab_w_in, ab_w_out, a_sink, b_rpb,
           c_w_in, c_w_out, c_q_gain, c_k_gain, peer_wq, peer_subkeys, peer_u, peer_v):
    f = lambda a: np.ascontiguousarray(np.asarray(a, dtype=np.float32))
    x, c, ctx, c_ctx = f(x), f(c), f(ctx), f(c_ctx)
    mod = run_mod(c, c_ctx, f(mod_w), f(mod_b))
    xc = ctx
    dummy_gain = np.zeros((2, 128), np.float32)
    for layer in range(4):
        i = layer // 2
        mod_l = mod[layer]
        if layer % 2 == 0:
            q_lat, q_ctx = run_p1("ab", x, xc, mod_l, f(ab_w_in[i]), dummy_gain)
            att_lat, att_ctx = run_p2a_ab(q_lat, q_ctx, f(a_sink[i]), f(b_rpb[i]))
            w_out = f(ab_w_out[i])
        else:
            gains = np.ascontiguousarray(np.stack([f(c_q_gain[i]), f(c_k_gain[i])]))
            q_lat, q_ctx = run_p1("c", x, xc, mod_l, f(c_w_in[i]), gains)
            att_lat, att_ctx = run_p2a_c(q_lat, q_ctx)
            w_out = f(c_w_out[i])
        x1, x1c, idx_l, gw_l = run_p23(att_lat, att_ctx, x, xc, mod_l, w_out, f(ln_g[layer, 0]), f(ln_b[layer, 0]),
                                       f(peer_wq[layer]), f(peer_subkeys[layer]))
        x, xc = run_p3b(idx_l, gw_l, x1, x1c, mod_l, f(peer_u[layer]), f(peer_v[layer]),
                        f(ln_g[layer, 1]), f(ln_b[layer, 1]))
    return x
```

```python
import numpy as np
import ml_dtypes
import concourse.bass as bass
import concourse.mybir as mybir
from concourse.bass_utils import run_bass_kernel_spmd

F32 = mybir.dt.float32
BF16 = mybir.dt.bfloat16
U32 = mybir.dt.uint32
I32 = mybir.dt.int32
ALU = mybir.AluOpType
AF = mybir.ActivationFunctionType
AX = mybir.AxisListType
NPBF = ml_dtypes.bfloat16

NCORES = 8
D = 2048
ENGS = ["tensor", "vector", "scalar", "gpsimd", "sync"]
NDMA = 8


class Sched:
    def __init__(self, nc, sems):
        self.nc = nc
        self.sems = sems
        self.q = {e: [] for e in ENGS}
        self.cnt = {}
        self.waited = {e: {} for e in ENGS}
        self.last_w = {}
        self.readers = {}
        self.rr = {e: 0 for e in ENGS}
        self.out_dma = []

    def op(self, eng, fn, reads=(), writes=(), dma=False, is_out=False):
        deps = []
        for b in reads:
            if b in self.last_w:
                deps.append(self.last_w[b])
        for b in writes:
            if b in self.last_w:
                deps.append(self.last_w[b])
            deps.extend(self.readers.get(b, ()))
        if dma:
            key = (eng, "d", self.rr[eng])
            self.rr[eng] = (self.rr[eng] + 1) % NDMA
            inc = 16
        else:
            key = (eng, "c")
            inc = 1
        prev = self.cnt.get(key, 0)
        val = prev + inc
        self.cnt[key] = val
        waits = []
        w = self.waited[eng]
        if dma and prev > 0 and w.get(key, 0) < prev:
            w[key] = prev
            waits.append((key, prev))
        for (k, v, de, ddma) in deps:
            if de == eng and eng == "tensor" and not ddma:
                continue
            if w.get(k, 0) >= v:
                continue
            w[k] = v
            waits.append((k, v))
        self.q[eng].append((waits, fn, key, inc))
        rec = (key, val, eng, dma)
        for b in writes:
            self.last_w[b] = rec
            self.readers[b] = []
        for b in reads:
            self.readers.setdefault(b, []).append(rec)
        if is_out:
            self.out_dma.append(rec)
        return rec

    def finish(self):
        waits = []
        for (k, v, de, ddma) in self.out_dma:
            waits.append((k, v))
        self.q["sync"].append((waits, None, None, 0))

    def emit(self, block):
        nc = self.nc
        sems = self.sems

        def body(engname):
            def f(engine):
                for (waits, fn, key, inc) in self.q[engname]:
                    for (k, v) in waits:
                        engine.wait_ge(sems[k], v)
                    if fn is not None:
                        fn(engine).then_inc(sems[key], inc)
            return f

        block.tensor(body("tensor"))
        block.vector(body("vector"))
        block.scalar(body("scalar"))
        block.gpsimd(body("gpsimd"))
        block.sync(body("sync"))


def sem_keys():
    keys = []
    for e in ENGS:
        keys.append((e, "c"))
        for i in range(NDMA):
            keys.append((e, "d", i))
    return keys


class Ctx:
    def __init__(self, nc):
        self.nc = nc
        self.stack = []
        self.prefix = ""

    def mark(self):
        return len(self.stack)

    def release(self, mark):
        while len(self.stack) > mark:
            self.stack.pop().__exit__(None, None, None)

    def enter(self, cm):
        v = cm.__enter__()
        self.stack.append(cm)
        return v

    def close(self):
        while self.stack:
            self.stack.pop().__exit__(None, None, None)

    def sb(self, name, shape, dt):
        return self.enter(self.nc.sbuf_tensor(self.prefix + name, list(shape), dt))

    def ps(self, name, shape, dt):
        return self.enter(self.nc.psum_tensor(self.prefix + name, list(shape), dt))


def new_prog():
    nc = bass.Bass("TRN2", target_bir_lowering=False)
    cx = Ctx(nc)
    sems = {}
    for k in sem_keys():
        sems[k] = cx.enter(nc.semaphore("s_" + "_".join(str(x) for x in k)))
    S = Sched(nc, sems)
    return nc, cx, S


LAST_S = [None]


def finish_prog(nc, cx, S):
    LAST_S[0] = S
    S.finish()
    block = cx.enter(nc.Block())
    S.emit(block)
    cx.close()
    return nc


def run(nc, in_maps):
    res = run_bass_kernel_spmd(nc, in_maps, core_ids=list(range(NCORES)))
    return res.results


MOD_CPC = 6 * D // NCORES


def build_mod():
    nc, cx, S = new_prog()
    cT = nc.dram_tensor("cT", [128, 16, 3], F32, kind="ExternalInput").ap()
    w = nc.dram_tensor("w", [4, D, MOD_CPC], F32, kind="ExternalInput").ap()
    b = nc.dram_tensor("b", [3, 4 * MOD_CPC], F32, kind="ExternalInput").ap()
    out = nc.dram_tensor("out", [3, 4 * MOD_CPC], F32, kind="ExternalOutput").ap()
    c_sb = cx.sb("c_sb", [128, 16, 3], F32)
    s_sb = cx.sb("s_sb", [128, 16, 3], F32)
    b_sb = cx.sb("b_sb", [3, 4 * MOD_CPC], F32)
    o_sb = cx.sb("o_sb", [3, 4 * MOD_CPC], F32)
    wt = [cx.sb("wt%d" % i, [128, MOD_CPC], F32) for i in range(4)]
    ps = [cx.ps("ps%d" % i, [128, 512], F32) for i in range(3)]

    S.op("sync", lambda e: e.dma_start(out=c_sb[:], in_=cT), writes=["c_sb"], dma=True)
    S.op("sync", lambda e: e.dma_start(out=b_sb[:], in_=b), writes=["b_sb"], dma=True)
    S.op("scalar", lambda e: e.activation(out=s_sb[:], in_=c_sb[:], func=AF.Silu),
         reads=["c_sb"], writes=["s_sb"])
    n = 0
    for l in range(4):
        for ch in range(16):
            slot = n % 4
            n += 1
            S.op("sync", lambda e, slot=slot, l=l, ch=ch: e.dma_start(
                out=wt[slot][:], in_=w[l, ch * 128:(ch + 1) * 128, :]),
                writes=[("wt", slot)], dma=True)
            for j in range(3):
                S.op("tensor", lambda e, slot=slot, j=j, ch=ch: e.matmul(
                    ps[j][0:3, :], lhsT=s_sb[:, ch, :], rhs=wt[slot][:, j * 512:(j + 1) * 512],
                    start=(ch == 0), stop=(ch == 15)),
                    reads=[("wt", slot), "s_sb"], writes=[("ps", j)])
        for j in range(3):
            c0 = l * MOD_CPC + j * 512
            S.op("vector", lambda e, j=j, c0=c0: e.tensor_tensor(
                out=o_sb[:, c0:c0 + 512], in0=ps[j][0:3, :], in1=b_sb[:, c0:c0 + 512], op=ALU.add),
                reads=[("ps", j), "b_sb"], writes=["o_sb"])
    S.op("sync", lambda e: e.dma_start(out=out, in_=o_sb[:]), reads=["o_sb"], dma=True, is_out=True)
    return finish_prog(nc, cx, S)


def run_mod(c, c_ctx, mod_w, mod_b):
    cv = np.concatenate([c, c_ctx[None]], 0)
    cT = np.ascontiguousarray(cv.T.reshape(16, 128, 3).transpose(1, 0, 2))
    nc = build_mod()
    in_maps = []
    for r in range(NCORES):
        cs = slice(r * MOD_CPC, (r + 1) * MOD_CPC)
        wr = np.ascontiguousarray(mod_w[:, :, cs])
        br = np.ascontiguousarray(
            np.broadcast_to(mod_b[:, cs].reshape(1, 4 * MOD_CPC), (3, 4 * MOD_CPC)))
        in_maps.append({"cT": cT, "w": wr, "b": br})
    res = run(nc, in_maps)
    mod = np.zeros((4, 3, 6 * D), np.float32)
    for r in range(NCORES):
        o = res[r]["out"].reshape(3, 4, MOD_CPC)
        mod[:, :, r * MOD_CPC:(r + 1) * MOD_CPC] = o.transpose(1, 0, 2)
    return mod


def barrier(S):
    allw = [(k, v) for k, v in S.cnt.items()]
    for e in ENGS:
        waits = []
        for (k, v) in allw:
            if S.waited[e].get(k, 0) < v:
                S.waited[e][k] = v
                waits.append((k, v))
        S.q[e].append((waits, None, None, 0))


NBLK = 17
NROW = NBLK * 128
GW = 768
RMS_EPS = 1e-6


def p1_groups(kind):
    if kind == "ab":
        spec = [(0, 8, None, True), (8, 10, None, True)]
        nh = 36
    else:
        spec = [(0, 16, 0, True), (16, 20, 1, True)]
        nh = 24
    groups = []
    for g in range(nh // 6):
        lo, hi = g * 6, g * 6 + 6
        items = []
        for (a, b, gi, rp) in spec:
            s, e = max(a, lo), min(b, hi)
            if s < e:
                items.append((s - lo, e - lo, gi, rp))
        groups.append(items)
    return groups


def build_p1(kind):
    ncols = 4608 if kind == "ab" else 3072
    ngrp = ncols // GW
    groups = p1_groups(kind)
    nc, cx, S = new_prog()
    x = nc.dram_tensor("x", [NROW, D], F32, kind="ExternalInput").ap()
    modv = nc.dram_tensor("modv", [2, 2, D], F32, kind="ExternalInput").ap()
    w = nc.dram_tensor("w", [D, ncols], F32, kind="ExternalInput").ap()
    cosd = nc.dram_tensor("cos", [NROW, 128], F32, kind="ExternalInput").ap()
    sind = nc.dram_tensor("sin", [NROW, 128], F32, kind="ExternalInput").ap()
    gain = nc.dram_tensor("gain", [2, 128], F32, kind="ExternalInput").ap()
    ident = nc.dram_tensor("ident", [128, 128], BF16, kind="ExternalInput").ap()
    out = nc.dram_tensor("out", [NROW, ncols], BF16, kind="ExternalOutput").ap()

    id_sb = cx.sb("id_sb", [128, 128], BF16)
    hT = cx.sb("hT", [128, NBLK, 16, 128], BF16)
    Mt = cx.sb("Mt", [128, D], F32)
    SHt = cx.sb("SHt", [128, D], F32)
    xt = [cx.sb("xt%d" % i, [128, D], F32) for i in range(2)]
    hb = [cx.sb("hb%d" % i, [128, D], BF16) for i in range(2)]
    pT = [cx.ps("pT%d" % i, [128, 1024], BF16) for i in range(2)]
    pq = [cx.ps("pq%d" % i, [128, 512], F32) for i in range(4)]
    wf = [cx.sb("wf%d" % i, [128, GW], F32) for i in range(3)]
    wb = [cx.sb("wb%d" % i, [128, 16, GW], BF16) for i in range(2)]
    cst = [cx.sb("cst%d" % i, [128, 128], F32) for i in range(2)]
    snt = [cx.sb("snt%d" % i, [128, 128], F32) for i in range(2)]
    gn = cx.sb("gn", [128, 2, 128], F32)
    o = [cx.sb("o%d" % i, [128, GW], F32) for i in range(2)]
    t1 = cx.sb("t1", [128, GW], F32)
    t2 = cx.sb("t2", [128, GW], F32)
    ss = cx.sb("ss", [128, 8], F32)
    ob = [cx.sb("ob%d" % i, [128, GW], BF16) for i in range(2)]

    S.op("sync", lambda e: e.dma_start(out=id_sb[:], in_=ident), writes=["id"], dma=True)
    S.op("sync", lambda e: e.dma_start(
        out=gn[:], in_=gain.unsqueeze(0).broadcast_to([128, 2, 128])), writes=["gn"], dma=True)

    for blk in range(NBLK):
        if blk == 0 or blk == NBLK - 1:
            mi = 0 if blk == 0 else 1
            S.op("sync", lambda e, mi=mi: e.dma_start(
                out=Mt[:], in_=modv[mi, 0:1, :].broadcast_to([128, D])), writes=["Mt"], dma=True)
            S.op("sync", lambda e, mi=mi: e.dma_start(
                out=SHt[:], in_=modv[mi, 1:2, :].broadcast_to([128, D])), writes=["SHt"], dma=True)
            S.op("gpsimd", lambda e: e.tensor_scalar(
                out=Mt[:], in0=Mt[:], scalar1=1.0, scalar2=None, op0=ALU.add),
                reads=["Mt"], writes=["Mt"])
        b = blk % 2
        S.op("sync", lambda e, b=b, blk=blk: e.dma_start(
            out=xt[b][:], in_=x[blk * 128:(blk + 1) * 128, :]), writes=[("xt", b)], dma=True)
        S.op("gpsimd", lambda e, b=b: e.tensor_tensor(
            out=xt[b][:], in0=xt[b][:], in1=Mt[:], op=ALU.mult),
            reads=[("xt", b), "Mt"], writes=[("xt", b)])
        S.op("vector", lambda e, b=b: e.tensor_tensor(
            out=hb[b][:], in0=xt[b][:], in1=SHt[:], op=ALU.add),
            reads=[("xt", b), "SHt"], writes=[("hb", b)])
        for half in range(2):
            for j in range(8):
                ch = half * 8 + j
                S.op("tensor", lambda e, b=b, half=half, j=j, ch=ch: e.transpose(
                    out=pT[half][:, j * 128:(j + 1) * 128], in_=hb[b][:, ch * 128:(ch + 1) * 128],
                    identity=id_sb[:]),
                    reads=[("hb", b), "id"], writes=[("pT", half)])
            S.op("scalar", lambda e, half=half, blk=blk: e.copy(
                out=hT[:, blk, half * 8:(half + 1) * 8, :],
                in_=pT[half][:].rearrange("p (c t) -> p c t", c=8)),
                reads=[("pT", half)], writes=[("hT", blk)])

    nit = 0
    for g in range(ngrp):
        wslot = g % 2
        for ch in range(16):
            fs = (g * 16 + ch) % 3
            S.op("sync", lambda e, fs=fs, ch=ch, g=g: e.dma_start(
                out=wf[fs][:], in_=w[ch * 128:(ch + 1) * 128, g * GW:(g + 1) * GW]),
                writes=[("wf", fs)], dma=True)
            S.op("gpsimd", lambda e, fs=fs, ch=ch, wslot=wslot: e.tensor_copy(
                out=wb[wslot][:, ch, :], in_=wf[fs][:]),
                reads=[("wf", fs)], writes=[("wb", wslot)])
        for blk in range(NBLK):
            it = nit % 2
            nit += 1
            S.op("sync", lambda e, it=it, blk=blk: e.dma_start(
                out=cst[it][:], in_=cosd[blk * 128:(blk + 1) * 128, :]), writes=[("cs", it)], dma=True)
            S.op("sync", lambda e, it=it, blk=blk: e.dma_start(
                out=snt[it][:], in_=sind[blk * 128:(blk + 1) * 128, :]), writes=[("sn", it)], dma=True)
            for half in range(2):
                pb = it * 2 + half
                for ch in range(16):
                    S.op("tensor", lambda e, pb=pb, ch=ch, blk=blk, half=half, wslot=wslot: e.matmul(
                        pq[pb][:, 0:384], lhsT=hT[:, blk, ch, :],
                        rhs=wb[wslot][:, ch, half * 384:(half + 1) * 384],
                        start=(ch == 0), stop=(ch == 15)),
                        reads=[("hT", blk), ("wb", wslot)], writes=[("pq", pb)])
                S.op("scalar", lambda e, pb=pb, it=it, half=half: e.copy(
                    out=o[it][:, half * 384:(half + 1) * 384], in_=pq[pb][:, 0:384]),
                    reads=[("pq", pb)], writes=[("o", it)])
            for (h0, h1, gi, rp) in groups[g]:
                nh = h1 - h0
                osl = o[it][:, h0 * 128:h1 * 128]
                o3 = osl.rearrange("p (h d) -> p h d", d=128)
                if gi is not None:
                    t13 = t1[:, h0 * 128:h1 * 128].rearrange("p (h d) -> p h d", d=128)
                    S.op("scalar", lambda e, osl=osl, h0=h0, h1=h1: e.activation(
                        out=t1[:, h0 * 128:h1 * 128], in_=osl, func=AF.Square),
                        reads=[("o", it)], writes=["t1"])
                    S.op("vector", lambda e, t13=t13, nh=nh: e.tensor_reduce(
                        out=ss[:, 0:nh], in_=t13, axis=AX.X, op=ALU.add),
                        reads=["t1"], writes=["ss"])
                    S.op("vector", lambda e, nh=nh: e.tensor_scalar(
                        out=ss[:, 0:nh], in0=ss[:, 0:nh], scalar1=1.0 / 128, scalar2=RMS_EPS,
                        op0=ALU.mult, op1=ALU.add), reads=["ss"], writes=["ss"])
                    S.op("scalar", lambda e, nh=nh: e.activation(
                        out=ss[:, 0:nh], in_=ss[:, 0:nh], func=AF.Sqrt), reads=["ss"], writes=["ss"])
                    S.op("vector", lambda e, nh=nh: e.reciprocal(
                        out=ss[:, 0:nh], in_=ss[:, 0:nh]), reads=["ss"], writes=["ss"])
                    S.op("vector", lambda e, o3=o3, nh=nh: e.tensor_tensor(
                        out=o3, in0=o3, in1=ss[:, 0:nh].unsqueeze(2).broadcast_to([128, nh, 128]),
                        op=ALU.mult), reads=[("o", it), "ss"], writes=[("o", it)])
                    S.op("vector", lambda e, o3=o3, nh=nh, gi=gi: e.tensor_tensor(
                        out=o3, in0=o3, in1=gn[:, gi:gi + 1, :].broadcast_to([128, nh, 128]),
                        op=ALU.mult), reads=[("o", it), "gn"], writes=[("o", it)])
                if rp:
                    t13 = t1[:, h0 * 128:h1 * 128].rearrange("p (h d) -> p h d", d=128)
                    S.op("vector", lambda e, o3=o3, t13=t13, nh=nh, it=it: e.tensor_tensor(
                        out=t13, in0=o3, in1=cst[it][:].unsqueeze(1).broadcast_to([128, nh, 128]),
                        op=ALU.mult), reads=[("o", it), ("cs", it)], writes=["t1"])
                    o5 = osl.rearrange("p (h a b c) -> p h a b c", a=2, b=2, c=32)
                    t25 = t2[:, h0 * 128:h1 * 128].rearrange("p (h a b c) -> p h a b c", a=2, b=2, c=32)
                    s4 = snt[it][:].rearrange("p (a b c) -> p a b c", a=2, b=2, c=32)
                    for wh in range(2):
                        S.op("gpsimd", lambda e, o5=o5, t25=t25, s4=s4, wh=wh, nh=nh: e.tensor_tensor(
                            out=t25[:, :, :, wh, :], in0=o5[:, :, :, 1 - wh, :],
                            in1=s4[:, :, wh, :].unsqueeze(1).broadcast_to([128, nh, 2, 32]),
                            op=ALU.mult), reads=[("o", it), ("sn", it)], writes=["t2"])
                    S.op("vector", lambda e, h0=h0, h1=h1, it=it: e.tensor_tensor(
                        out=ob[it][:, h0 * 128:h1 * 128], in0=t1[:, h0 * 128:h1 * 128],
                        in1=t2[:, h0 * 128:h1 * 128], op=ALU.add),
                        reads=["t1", "t2"], writes=[("ob", it)])
            covered = [False] * 6
            for (h0, h1, gi, rp) in groups[g]:
                if rp:
                    for hh in range(h0, h1):
                        covered[hh] = True
            hh = 0
            while hh < 6:
                if covered[hh]:
                    hh += 1
                    continue
                h2 = hh
                while h2 < 6 and not covered[h2]:
                    h2 += 1
                S.op("vector", lambda e, hh=hh, h2=h2, it=it: e.tensor_copy(
                    out=ob[it][:, hh * 128:h2 * 128], in_=o[it][:, hh * 128:h2 * 128]),
                    reads=[("o", it)], writes=[("ob", it)])
                hh = h2
            S.op("sync", lambda e, it=it, blk=blk, g=g: e.dma_start(
                out=out[blk * 128:(blk + 1) * 128, g * GW:(g + 1) * GW], in_=ob[it][:]),
                reads=[("ob", it)], dma=True, is_out=True)
    return finish_prog(nc, cx, S)


_PROG_CACHE = {}


def get_prog(name, builder, *args):
    key = (name,) + tuple(args)
    if key not in _PROG_CACHE:
        _PROG_CACHE[key] = builder(*args)
    return _PROG_CACHE[key]


GRID_W = 64
SEQ = 8192
CTX = 256


def rope_tables():
    t = np.arange(SEQ)
    row = (t // GRID_W).astype(np.float32)
    col = (t % GRID_W).astype(np.float32)
    n_freq = 32
    inv_freq = (10000.0 ** (-np.arange(n_freq, dtype=np.float32) / n_freq)).astype(np.float32)
    ang_r = row[:, None] * inv_freq[None, :]
    ang_c = col[:, None] * inv_freq[None, :]
    ang = np.concatenate([ang_r, ang_r, ang_c, ang_c], axis=-1)
    cos = np.cos(ang).astype(np.float32)
    sin = np.sin(ang).astype(np.float32)
    sgn = np.concatenate([-np.ones(32), np.ones(32), -np.ones(32), np.ones(32)]).astype(np.float32)
    return cos, sin * sgn[None, :]


_ROPE = None


def core_rows(x, xc, r):
    b, c4 = r // 4, r % 4
    return np.concatenate([x[b, c4 * 2048:(c4 + 1) * 2048], xc[b, (c4 % 2) * 128:(c4 % 2 + 1) * 128]], 0)


def split_mod(mod_l):
    names = ["sh1", "sc1", "g1", "sh2", "sc2", "g2"]
    return {n: mod_l[:, i * D:(i + 1) * D] for i, n in enumerate(names)}


def run_p1(kind, x, xc, mod_l, w_in, gains):
    global _ROPE
    if _ROPE is None:
        _ROPE = rope_tables()
    cos, sinS = _ROPE
    m = split_mod(mod_l)
    ncols = w_in.shape[1]
    nc = get_prog("p1", build_p1, kind)
    ident = np.eye(128, dtype=np.float32).astype(NPBF)
    in_maps = []
    for r in range(NCORES):
        b, c4 = r // 4, r % 4
        modv = np.stack([np.stack([m["sc1"][b], m["sh1"][b]]), np.stack([m["sc1"][2], m["sh1"][2]])])
        cs = np.concatenate([cos[c4 * 2048:(c4 + 1) * 2048], np.ones((128, 128), np.float32)], 0)
        sn = np.concatenate([sinS[c4 * 2048:(c4 + 1) * 2048], np.zeros((128, 128), np.float32)], 0)
        in_maps.append({"x": core_rows(x, xc, r), "modv": np.ascontiguousarray(modv), "w": w_in,
                        "cos": cs, "sin": sn, "gain": gains, "ident": ident})
    res = run(nc, in_maps)
    q_lat = np.zeros((2, SEQ, ncols), NPBF)
    q_ctx = np.zeros((2, CTX, ncols), NPBF)
    for r in range(NCORES):
        b, c4 = r // 4, r % 4
        o = res[r]["out"]
        q_lat[b, c4 * 2048:(c4 + 1) * 2048] = o[:2048]
        if c4 < 2:
            q_ctx[b, c4 * 128:(c4 + 1) * 128] = o[2048:]
    return q_lat, q_ctx


ATTN_SCALE = 128 ** -0.5
NKB_C = 66


def emit_attn_unit(S, qmov, kblocks, P, Sps, Ops, nsub, uid, post_exp=None):
    n = len(kblocks)
    W = nsub * 128

    def s_mm(i):
        kT, kTk, _, _ = kblocks[i]
        sb = i % 2
        S.op("tensor", lambda e, kT=kT, sb=sb: e.matmul(
            Sps[sb][:, 0:W], lhsT=kT, rhs=qmov[0], start=True, stop=True),
            reads=kTk + qmov[1], writes=[("Sps", sb)])

    s_mm(0)
    for i in range(n):
        if i + 1 < n:
            s_mm(i + 1)
        sb = i % 2
        pb = (uid * 131 + i) % len(P)
        S.op("scalar", lambda e, sb=sb, pb=pb: e.activation(
            out=P[pb][:, 0:W], in_=Sps[sb][:, 0:W], func=AF.Exp, scale=ATTN_SCALE),
            reads=[("Sps", sb)], writes=[("P", pb)])
        if post_exp is not None:
            post_exp(i, pb)
        _, _, v, vk = kblocks[i]
        for j in range(nsub):
            S.op("tensor", lambda e, pb=pb, j=j, v=v, i=i: e.matmul(
                Ops[j][:, 0:129], lhsT=P[pb][:, j * 128:(j + 1) * 128], rhs=v,
                start=(i == 0), stop=(i == n - 1)),
                reads=[("P", pb)] + vk, writes=[("Ops", j)])


def build_p2a_c():
    nc, cx, S = new_prog()
    QT = nc.dram_tensor("QT", [4, 128, 4, NROW], BF16, kind="ExternalInput").ap()
    KT = nc.dram_tensor("KT", [4, 128, NKB_C * 128], BF16, kind="ExternalInput").ap()
    V = nc.dram_tensor("V", [4, 128, NKB_C, 128], BF16, kind="ExternalInput").ap()
    att = nc.dram_tensor("att", [NROW, D], BF16, kind="ExternalOutput").ap()

    KTs = [cx.sb("KTs%d" % i, [128, NKB_C * 128], BF16) for i in range(2)]
    Vs = [cx.sb("Vs%d" % i, [128, NKB_C, 129], BF16) for i in range(2)]
    QTs = [cx.sb("QTs%d" % i, [128, 4, NROW], BF16) for i in range(2)]
    P = [cx.sb("P%d" % i, [128, 512], BF16) for i in range(3)]
    at = [cx.sb("at%d" % i, [128, 512], BF16) for i in range(2)]
    rec = cx.sb("rec", [128, 4], F32)
    Sps = [cx.ps("Sps%d" % i, [128, 512], F32) for i in range(2)]
    Ops = [cx.ps("Ops%d" % i, [128, 512], F32) for i in range(4)]

    for sl in range(2):
        S.op("gpsimd", lambda e, sl=sl: e.memset(Vs[sl][:, :, 128:129], 1.0), writes=[("Vone", sl)])
    uid = 0
    for g in range(4):
        sl = g % 2
        S.op("sync", lambda e, sl=sl, g=g: e.dma_start(out=KTs[sl][:], in_=KT[g]),
             writes=[("KT", sl)], dma=True)
        S.op("sync", lambda e, sl=sl, g=g: e.dma_start(out=Vs[sl][:, :, 0:128], in_=V[g]),
             writes=[("V", sl)], dma=True)
        S.op("sync", lambda e, sl=sl, g=g: e.dma_start(out=QTs[sl][:], in_=QT[g]),
             writes=[("QT", sl)], dma=True)
        for qb in range(NBLK):
            nkb = NKB_C if qb < 16 else 2
            qmov = (QTs[sl][:, :, qb * 128:(qb + 1) * 128], [("QT", sl)])
            kblocks = [(KTs[sl][:, kb * 128:(kb + 1) * 128], [("KT", sl)],
                        Vs[sl][:, kb, :], [("V", sl), ("Vone", sl)]) for kb in range(nkb)]
            emit_attn_unit(S, qmov, kblocks, P, Sps, Ops, 4, uid)
            uid += 1
            ab = uid % 2
            for j in range(4):
                S.op("vector", lambda e, j=j: e.reciprocal(out=rec[:, j:j + 1], in_=Ops[j][:, 128:129]),
                     reads=[("Ops", j)], writes=[("rec", j)])
                S.op("vector", lambda e, j=j, ab=ab: e.tensor_scalar(
                    out=at[ab][:, j * 128:(j + 1) * 128], in0=Ops[j][:, 0:128], scalar1=rec[:, j:j + 1],
                    scalar2=None, op0=ALU.mult),
                    reads=[("Ops", j), ("rec", j)], writes=[("at", ab)])
            S.op("sync", lambda e, ab=ab, qb=qb, g=g: e.dma_start(
                out=att[qb * 128:(qb + 1) * 128, g * 512:(g + 1) * 512], in_=at[ab][:]),
                reads=[("at", ab)], dma=True, is_out=True)
    return finish_prog(nc, cx, S)


def to_headT(q_lat, q_ctx, r, h0, nh):
    rows = core_rows(q_lat, q_ctx, r)
    sub = rows[:, h0 * 128:(h0 + nh) * 128].reshape(NROW, nh, 128)
    return np.ascontiguousarray(sub.transpose(1, 2, 0))


def run_p2a_c(q_lat, q_ctx):
    nc = get_prog("p2a_c", build_p2a_c)
    in_maps = []
    kv = {}
    for b in range(2):
        allr = np.concatenate([q_ctx[b], q_lat[b]], 0)
        k = allr[:, 2048:2560].reshape(NKB_C * 128, 4, 128)
        v = allr[:, 2560:3072].reshape(NKB_C, 128, 4, 128)
        KTh = np.ascontiguousarray(k.transpose(1, 2, 0))
        Vh = np.ascontiguousarray(v.transpose(2, 1, 0, 3))
        kv[b] = (KTh, Vh)
    for r in range(NCORES):
        b = r // 4
        qt = to_headT(q_lat, q_ctx, r, 0, 16).reshape(4, 4, 128, NROW)
        qt = np.ascontiguousarray(qt.transpose(0, 2, 1, 3))
        in_maps.append({"QT": qt, "KT": kv[b][0], "V": kv[b][1]})
    res = run(nc, in_maps)
    return gather_rows([res[r]["att"] for r in range(NCORES)], D, NPBF)


def gather_rows(outs, ncols, dt):
    lat = np.zeros((2, SEQ, ncols), dt)
    ctx = np.zeros((2, CTX, ncols), dt)
    for r in range(NCORES):
        b, c4 = r // 4, r % 4
        o = outs[r]
        lat[b, c4 * 2048:(c4 + 1) * 2048] = o[:2048]
        if c4 < 2:
            ctx[b, c4 * 128:(c4 + 1) * 128] = o[2048:]
    return lat, ctx


NPAT = 25
PAT_CLASS = {0: 5, 1: 10, 14: 15, 15: 20}


def pat_base(lb):
    return PAT_CLASS.get(lb, 0)


def build_p2a_ab():
    nc, cx, S = new_prog()
    QTa = nc.dram_tensor("QTa", [2, 128, 4, NROW], BF16, kind="ExternalInput").ap()
    KTa = nc.dram_tensor("KTa", [2, 128, 20 * 128], BF16, kind="ExternalInput").ap()
    Va = nc.dram_tensor("Va", [2, 128, 20, 128], BF16, kind="ExternalInput").ap()
    QTb = nc.dram_tensor("QTb", [8, 128, NROW], BF16, kind="ExternalInput").ap()
    KTb = nc.dram_tensor("KTb", [8, 128, 22 * 128], BF16, kind="ExternalInput").ap()
    Vb = nc.dram_tensor("Vb", [8, 128, 22, 128], BF16, kind="ExternalInput").ap()
    maskA = nc.dram_tensor("maskA", [128, 4, 128], BF16, kind="ExternalInput").ap()
    sink = nc.dram_tensor("sink", [1, 8], F32, kind="ExternalInput").ap()
    biasB = nc.dram_tensor("biasB", [8, 128, NPAT * 128], F32, kind="ExternalInput").ap()
    maskB = nc.dram_tensor("maskB", [128, NPAT * 128], BF16, kind="ExternalInput").ap()
    att = nc.dram_tensor("att", [NROW, D], BF16, kind="ExternalOutput").ap()

    QTas = [cx.sb("QTas%d" % i, [128, 4, NROW], BF16) for i in range(2)]
    KTas = [cx.sb("KTas%d" % i, [128, 20 * 128], BF16) for i in range(2)]
    Vas = [cx.sb("Vas%d" % i, [128, 20, 129], BF16) for i in range(2)]
    QTbs = [cx.sb("QTbs%d" % i, [128, NROW], BF16) for i in range(2)]
    KTbs = [cx.sb("KTbs%d" % i, [128, 22 * 128], BF16) for i in range(2)]
    Vbs = [cx.sb("Vbs%d" % i, [128, 22, 129], BF16) for i in range(2)]
    mA = cx.sb("mA", [128, 4, 128], BF16)
    mB = cx.sb("mB", [128, NPAT * 128], BF16)
    bB = cx.sb("bB", [128, NPAT * 128], F32)
    eB = cx.sb("eB", [128, NPAT * 128], F32)
    E = [cx.sb("E%d" % i, [128, NPAT * 128], BF16) for i in range(2)]
    snk = cx.sb("snk", [128, 8], F32)
    esnk = cx.sb("esnk", [128, 8], F32)
    P = [cx.sb("P%d" % i, [128, 512], BF16) for i in range(3)]
    at = [cx.sb("at%d" % i, [128, 512], BF16) for i in range(2)]
    rec = cx.sb("rec", [128, 4], F32)
    Sps = [cx.ps("Sps%d" % i, [128, 512], F32) for i in range(2)]
    Ops = [cx.ps("Ops%d" % i, [128, 512], F32) for i in range(4)]

    for sl in range(2):
        S.op("gpsimd", lambda e, sl=sl: e.memset(Vas[sl][:, :, 128:129], 1.0), writes=[("Vaone", sl)])
        S.op("gpsimd", lambda e, sl=sl: e.memset(Vbs[sl][:, :, 128:129], 1.0), writes=[("Vbone", sl)])
    S.op("sync", lambda e: e.dma_start(out=mA[:], in_=maskA), writes=["mA"], dma=True)
    S.op("sync", lambda e: e.dma_start(out=mB[:], in_=maskB), writes=["mB"], dma=True)
    S.op("sync", lambda e: e.dma_start(out=snk[:], in_=sink.broadcast_to([128, 8])), writes=["snk"], dma=True)
    S.op("scalar", lambda e: e.activation(out=esnk[:], in_=snk[:], func=AF.Exp), reads=["snk"], writes=["esnk"])

    uid = 0
    for g in range(2):
        sl = g % 2
        S.op("sync", lambda e, sl=sl, g=g: e.dma_start(out=KTas[sl][:], in_=KTa[g]), writes=[("KTa", sl)], dma=True)
        S.op("sync", lambda e, sl=sl, g=g: e.dma_start(out=Vas[sl][:, :, 0:128], in_=Va[g]), writes=[("Va", sl)], dma=True)
        S.op("sync", lambda e, sl=sl, g=g: e.dma_start(out=QTas[sl][:], in_=QTa[g]), writes=[("QTa", sl)], dma=True)
        for qb in range(NBLK):
            pos = [qb, qb + 1, qb + 2, 18, 19] if qb < 16 else [18, 19]
            qmov = (QTas[sl][:, :, qb * 128:(qb + 1) * 128], [("QTa", sl)])
            kblocks = [(KTas[sl][:, p * 128:(p + 1) * 128], [("KTa", sl)],
                        Vas[sl][:, p, :], [("Va", sl), ("Vaone", sl)]) for p in pos]

            def post_exp(i, pb, qb=qb):
                if qb >= 16 or i not in (0, 2):
                    return
                mi = (0 if qb == 0 else 1) if i == 0 else (3 if qb == 15 else 2)
                S.op("vector", lambda e, pb=pb, mi=mi: e.tensor_tensor(
                    out=P[pb][:].rearrange("p (g q) -> p g q", g=4),
                    in0=P[pb][:].rearrange("p (g q) -> p g q", g=4),
                    in1=mA[:, mi:mi + 1, :].broadcast_to([128, 4, 128]), op=ALU.mult),
                    reads=[("P", pb), "mA"], writes=[("P", pb)])

            emit_attn_unit(S, qmov, kblocks, P, Sps, Ops, 4, uid, post_exp)
            uid += 1
            ab = uid % 2
            for j in range(4):
                hd = g * 4 + j
                S.op("vector", lambda e, j=j, hd=hd: e.tensor_tensor(
                    out=rec[:, j:j + 1], in0=Ops[j][:, 128:129], in1=esnk[:, hd:hd + 1], op=ALU.add),
                    reads=[("Ops", j), "esnk"], writes=[("rec", j)])
                S.op("vector", lambda e, j=j: e.reciprocal(out=rec[:, j:j + 1], in_=rec[:, j:j + 1]),
                     reads=[("rec", j)], writes=[("rec", j)])
                S.op("vector", lambda e, j=j, ab=ab: e.tensor_scalar(
                    out=at[ab][:, j * 128:(j + 1) * 128], in0=Ops[j][:, 0:128], scalar1=rec[:, j:j + 1],
                    scalar2=None, op0=ALU.mult),
                    reads=[("Ops", j), ("rec", j)], writes=[("at", ab)])
            S.op("sync", lambda e, ab=ab, qb=qb, g=g: e.dma_start(
                out=att[qb * 128:(qb + 1) * 128, g * 512:(g + 1) * 512], in_=at[ab][:]),
                reads=[("at", ab)], dma=True, is_out=True)

    for h in range(8):
        sl = h % 2
        S.op("sync", lambda e, sl=sl, h=h: e.dma_start(out=KTbs[sl][:], in_=KTb[h]), writes=[("KTb", sl)], dma=True)
        S.op("sync", lambda e, sl=sl, h=h: e.dma_start(out=Vbs[sl][:, :, 0:128], in_=Vb[h]), writes=[("Vb", sl)], dma=True)
        S.op("sync", lambda e, sl=sl, h=h: e.dma_start(out=QTbs[sl][:], in_=QTb[h]), writes=[("QTb", sl)], dma=True)
        S.op("sync", lambda e, h=h: e.dma_start(out=bB[:], in_=biasB[h]), writes=["bB"], dma=True)
        S.op("scalar", lambda e: e.activation(out=eB[:], in_=bB[:], func=AF.Exp), reads=["bB"], writes=["eB"])
        S.op("gpsimd", lambda e, sl=sl: e.tensor_tensor(out=E[sl][:], in0=eB[:], in1=mB[:], op=ALU.mult),
             reads=["eB", "mB"], writes=[("E", sl)])
        for qb in range(NBLK):
            pos = [qb + s for s in range(5)] + [20, 21] if qb < 16 else [20, 21]
            qmov = (QTbs[sl][:, qb * 128:(qb + 1) * 128], [("QTb", sl)])
            kblocks = [(KTbs[sl][:, p * 128:(p + 1) * 128], [("KTb", sl)],
                        Vbs[sl][:, p, :], [("Vb", sl), ("Vbone", sl)]) for p in pos]

            def post_exp(i, pb, qb=qb, sl=sl):
                if qb >= 16 or i >= 5:
                    return
                pi = pat_base(qb) + i
                S.op("vector", lambda e, pb=pb, pi=pi, sl=sl: e.tensor_tensor(
                    out=P[pb][:, 0:128], in0=P[pb][:, 0:128], in1=E[sl][:, pi * 128:(pi + 1) * 128],
                    op=ALU.mult), reads=[("P", pb), ("E", sl)], writes=[("P", pb)])

            emit_attn_unit(S, qmov, kblocks, P, Sps, Ops, 1, uid, post_exp)
            uid += 1
            ab = uid % 2
            S.op("vector", lambda e: e.reciprocal(out=rec[:, 0:1], in_=Ops[0][:, 128:129]),
                 reads=[("Ops", 0)], writes=[("rec", 0)])
            S.op("vector", lambda e, ab=ab: e.tensor_scalar(
                out=at[ab][:, 0:128], in0=Ops[0][:, 0:128], scalar1=rec[:, 0:1],
                scalar2=None, op0=ALU.mult),
                reads=[("Ops", 0), ("rec", 0)], writes=[("at", ab)])
            S.op("sync", lambda e, ab=ab, qb=qb, h=h: e.dma_start(
                out=att[qb * 128:(qb + 1) * 128, 1024 + h * 128:1024 + (h + 1) * 128], in_=at[ab][:, 0:128]),
                reads=[("at", ab)], dma=True, is_out=True)
    return finish_prog(nc, cx, S)


def nbr_geometry(c4):
    halo = [16 * c4 - 2 + p for p in range(20)]
    if c4 == 0:
        halo[0], halo[1] = 3, None
    if c4 == 3:
        halo[18], halo[19] = 60, None
    mask = np.zeros((128, NPAT, 128), bool)
    drow = np.zeros((128, NPAT, 128), np.int64)
    dcol = np.zeros((128, NPAT, 128), np.int64)
    k = np.arange(128)
    q = np.arange(128)
    for lb in [2, 0, 1, 14, 15]:
        m = 16 * c4 + lb
        seen = []
        for s in range(5):
            gb = halo[lb + s]
            pi = pat_base(lb) + s
            if gb is None or gb in seen or gb < 0 or gb > 63:
                continue
            seen.append(gb)
            kr = (2 * gb + k // 64)[:, None]
            kc = (k % 64)[:, None]
            qr = (2 * m + q // 64)[None, :]
            qc = (q % 64)[None, :]
            rstart = np.clip(qr - 4, 0, 120)
            wstart = np.clip(qc - 8, 0, 48)
            ok = (kr >= rstart) & (kr < rstart + 8) & (kc >= wstart) & (kc < wstart + 16)
            mask[:, pi, :] = ok
            drow[:, pi, :] = np.where(ok, kr - qr + 7, 0)
            dcol[:, pi, :] = np.clip(kc - qc + 15, 0, 30) * ok
    return halo, mask, drow, dcol


def run_p2a_ab(q_lat, q_ctx, sink, rpb):
    nc = get_prog("p2a_ab", build_p2a_ab)
    tri_prev = (np.arange(128)[:, None] >= np.arange(128)[None, :])
    tri_next = (np.arange(128)[:, None] <= np.arange(128)[None, :])
    zeros = np.zeros((128, 128), bool)
    in_maps = []
    for r in range(NCORES):
        b, c4 = r // 4, r % 4
        qt = to_headT(q_lat, q_ctx, r, 0, 8).reshape(2, 4, 128, NROW)
        QTa = np.ascontiguousarray(qt.transpose(0, 2, 1, 3))
        QTb = to_headT(q_lat, q_ctx, r, 12, 8)
        lat = q_lat[b].reshape(64, 128, 4608)
        cxb = q_ctx[b].reshape(2, 128, 4608)
        zb = np.zeros((128, 4608), NPBF)
        blocksA = []
        for p in range(18):
            gb = 16 * c4 - 1 + p
            blocksA.append(lat[gb] if 0 <= gb < 64 else zb)
        blocksA += [cxb[0], cxb[1]]
        A = np.stack(blocksA)
        ka = A[:, :, 1024:1280].reshape(20, 128, 2, 128)
        va = A[:, :, 1280:1536].reshape(20, 128, 2, 128)
        KTa = np.ascontiguousarray(ka.transpose(2, 3, 0, 1).reshape(2, 128, 20 * 128))
        Va = np.ascontiguousarray(va.transpose(2, 1, 0, 3))
        halo, mask, drow, dcol = nbr_geometry(c4)
        blocksB = [lat[gb] if gb is not None and 0 <= gb < 64 else zb for gb in halo] + [cxb[0], cxb[1]]
        Bk = np.stack(blocksB)
        kb = Bk[:, :, 2560:3584].reshape(22, 128, 8, 128)
        vb = Bk[:, :, 3584:4608].reshape(22, 128, 8, 128)
        KTb = np.ascontiguousarray(kb.transpose(2, 3, 0, 1).reshape(8, 128, 22 * 128))
        Vb = np.ascontiguousarray(vb.transpose(2, 1, 0, 3))
        mA = np.stack([zeros if c4 == 0 else tri_prev, tri_prev, tri_next, zeros if c4 == 3 else tri_next], 1)
        biasB = rpb[:, drow, dcol].astype(np.float32)
        in_maps.append({
            "QTa": QTa, "KTa": KTa, "Va": Va, "QTb": QTb, "KTb": KTb, "Vb": Vb,
            "maskA": np.ascontiguousarray(mA).astype(np.float32).astype(NPBF),
            "sink": np.ascontiguousarray(sink.reshape(1, 8)),
            "biasB": np.ascontiguousarray(biasB.reshape(8, 128, NPAT * 128)),
            "maskB": mask.reshape(128, NPAT * 128).astype(np.float32).astype(NPBF)})
    res = run(nc, in_maps)
    return gather_rows([res[r]["att"] for r in range(NCORES)], D, NPBF)


ALPHA = float((2 * 4) ** 0.25)
LN_EPS = 1e-5


def emit_ln(S, zt, zk, stats, mv, lng, lnb, outt, outk, tag):
    for c in range(4):
        S.op("vector", lambda e, c=c: e.bn_stats(out=stats[:, c, :], in_=zt[:, c * 512:(c + 1) * 512]),
             reads=[zk], writes=[("stats", tag)])
    S.op("vector", lambda e: e.bn_aggr(out=mv[:, 0:2], in_=stats[:].rearrange("p c s -> p (c s)")),
         reads=[("stats", tag)], writes=[("mv", tag)])
    S.op("vector", lambda e: e.tensor_scalar(out=mv[:, 2:3], in0=mv[:, 1:2], scalar1=LN_EPS, scalar2=None,
                                             op0=ALU.add), reads=[("mv", tag)], writes=[("mv", tag)])
    S.op("scalar", lambda e: e.activation(out=mv[:, 2:3], in_=mv[:, 2:3], func=AF.Sqrt),
         reads=[("mv", tag)], writes=[("mv", tag)])
    S.op("vector", lambda e: e.reciprocal(out=mv[:, 2:3], in_=mv[:, 2:3]),
         reads=[("mv", tag)], writes=[("mv", tag)])
    S.op("vector", lambda e: e.tensor_scalar(out=zt[:], in0=zt[:], scalar1=mv[:, 0:1], scalar2=mv[:, 2:3],
                                             op0=ALU.subtract, op1=ALU.mult),
         reads=[zk, ("mv", tag)], writes=[zk])
    S.op("gpsimd", lambda e: e.tensor_tensor(out=zt[:], in0=zt[:], in1=lng[:], op=ALU.mult),
         reads=[zk, "lng"], writes=[zk])
    S.op("vector", lambda e: e.tensor_tensor(out=outt[:], in0=zt[:], in1=lnb[:], op=ALU.add),
         reads=[zk, "lnb"], writes=[outk])


def build_p2b():
    nc, cx, S = new_prog()
    emit_p2b(nc, cx, S)
    return finish_prog(nc, cx, S)


def emit_p2b(nc, cx, S):
    attT = nc.dram_tensor("attT", [NBLK, 128, 16, 128], BF16, kind="ExternalInput").ap()
    x = nc.dram_tensor("x", [NROW, D], F32, kind="ExternalInput").ap()
    w = nc.dram_tensor("wout", [D, D], F32, kind="ExternalInput").ap()
    g1 = nc.dram_tensor("g1", [2, D], F32, kind="ExternalInput").ap()
    lngd = nc.dram_tensor("lng", [1, D], F32, kind="ExternalInput").ap()
    lnbd = nc.dram_tensor("lnb", [1, D], F32, kind="ExternalInput").ap()
    out = nc.dram_tensor("out", [NROW, D], F32, kind="ExternalOutput").ap()

    wob = cx.sb("wob", [128, 16, D], BF16)
    wf = [cx.sb("wf%d" % i, [128, D], F32) for i in range(2)]
    aT = [cx.sb("aT%d" % i, [128, 16, 128], BF16) for i in range(2)]
    xt = [cx.sb("xt%d" % i, [128, D], F32) for i in range(2)]
    zt = [cx.sb("zt%d" % i, [128, D], F32) for i in range(2)]
    G1 = cx.sb("G1", [128, D], F32)
    lng = cx.sb("lng_sb", [128, D], F32)
    lnb = cx.sb("lnb_sb", [128, D], F32)
    stats = cx.sb("stats", [128, 4, 6], F32)
    mv = cx.sb("mv", [128, 4], F32)
    yps = [cx.ps("yps%d" % i, [128, D], F32) for i in range(2)]

    S.op("sync", lambda e: e.dma_start(out=lng[:], in_=lngd.broadcast_to([128, D])), writes=["lng"], dma=True)
    S.op("sync", lambda e: e.dma_start(out=lnb[:], in_=lnbd.broadcast_to([128, D])), writes=["lnb"], dma=True)
    for ch in range(16):
        fs = ch % 2
        S.op("sync", lambda e, fs=fs, ch=ch: e.dma_start(out=wf[fs][:], in_=w[ch * 128:(ch + 1) * 128, :]),
             writes=[("wf", fs)], dma=True)
        S.op("gpsimd", lambda e, fs=fs, ch=ch: e.tensor_copy(out=wob[:, ch, :], in_=wf[fs][:]),
             reads=[("wf", fs)], writes=["wob"])
    for blk in range(NBLK):
        b = blk % 2
        if blk == 0 or blk == NBLK - 1:
            mi = 0 if blk == 0 else 1
            S.op("sync", lambda e, mi=mi: e.dma_start(out=G1[:], in_=g1[mi:mi + 1, :].broadcast_to([128, D])),
                 writes=["G1"], dma=True)
        S.op("sync", lambda e, b=b, blk=blk: e.dma_start(out=aT[b][:], in_=attT[blk]), writes=[("aT", b)], dma=True)
        S.op("sync", lambda e, b=b, blk=blk: e.dma_start(out=xt[b][:], in_=x[blk * 128:(blk + 1) * 128, :]),
             writes=[("xt", b)], dma=True)
        for cb in range(4):
            for h in range(16):
                S.op("tensor", lambda e, b=b, cb=cb, h=h: e.matmul(
                    yps[b][:, cb * 512:(cb + 1) * 512], lhsT=aT[b][:, h, :],
                    rhs=wob[:, h, cb * 512:(cb + 1) * 512], start=(h == 0), stop=(h == 15)),
                    reads=[("aT", b), "wob"], writes=[("yps", b, cb)])
        for cb in range(4):
            S.op("vector", lambda e, b=b, cb=cb: e.tensor_tensor(
                out=zt[b][:, cb * 512:(cb + 1) * 512], in0=yps[b][:, cb * 512:(cb + 1) * 512],
                in1=G1[:, cb * 512:(cb + 1) * 512], op=ALU.mult),
                reads=[("yps", b, cb), "G1"], writes=[("zt", b)])
        S.op("vector", lambda e, b=b: e.scalar_tensor_tensor(
            out=zt[b][:], in0=xt[b][:], scalar=ALPHA, in1=zt[b][:], op0=ALU.mult, op1=ALU.add),
            reads=[("xt", b), ("zt", b)], writes=[("zt", b)])
        emit_ln(S, zt[b], ("zt", b), stats, mv, lng, lnb, xt[b], ("xt", b), 0)
        S.op("sync", lambda e, b=b, blk=blk: e.dma_start(out=out[blk * 128:(blk + 1) * 128, :], in_=xt[b][:]),
             reads=[("xt", b)], writes=[("x1dram", blk)], dma=True, is_out=True)
    return out


def run_p2b(att_lat, att_ctx, x, xc, mod_l, w_out, ln_g, ln_b):
    nc = get_prog("p2b", build_p2b)
    m = split_mod(mod_l)
    in_maps = []
    for r in range(NCORES):
        b = r // 4
        rows = core_rows(att_lat, att_ctx, r)
        aT = np.ascontiguousarray(rows.reshape(NBLK, 128, 16, 128).transpose(0, 3, 2, 1))
        in_maps.append({"attT": aT, "x": core_rows(x, xc, r), "wout": w_out,
                        "g1": np.ascontiguousarray(np.stack([m["g1"][b], m["g1"][2]])),
                        "lng": np.ascontiguousarray(ln_g.reshape(1, D)),
                        "lnb": np.ascontiguousarray(ln_b.reshape(1, D))})
    res = run(nc, in_maps)
    return gather_rows([res[r]["out"] for r in range(NCORES)], D, np.float32)


import os
P3A_STOP = int(os.environ.get('P3A_STOP', '0'))
P3A_VAR = int(os.environ.get('P3A_VAR', '0'))
P3A_NB = int(os.environ.get('P3A_NB', '17'))


def build_p3a():
    nc, cx, S = new_prog()
    emit_p3a(nc, cx, S, None, True)
    return finish_prog(nc, cx, S)


def build_p23():
    nc, cx, S = new_prog()
    mark = cx.mark()
    cx.prefix = "a_"
    out = emit_p2b(nc, cx, S)
    barrier(S)
    cx.release(mark)
    cx.prefix = "b_"
    emit_p3a(nc, cx, S, out, False)
    return finish_prog(nc, cx, S)


def emit_p3a(nc, cx, S, x, want_h2):
    if x is None:
        x = nc.dram_tensor("x", [NROW, D], F32, kind="ExternalInput").ap()
    modv = nc.dram_tensor("modv", [2, 2, D], F32, kind="ExternalInput").ap()
    w = nc.dram_tensor("pwq", [D, D], F32, kind="ExternalInput").ap()
    skt = nc.dram_tensor("skt", [128, 16 * 128], F32, kind="ExternalInput").ap()
    ident = nc.dram_tensor("ident", [128, 128], BF16, kind="ExternalInput").ap()
    iot = nc.dram_tensor("iot", [1, 16], F32, kind="ExternalInput").ap()
    h2o = nc.dram_tensor("h2", [NROW, D], F32, kind="ExternalOutput").ap() if want_h2 else None
    idxo = nc.dram_tensor("idx", [NROW, 128], I32, kind="ExternalOutput").ap()
    gwo = nc.dram_tensor("gw", [NROW, 128], F32, kind="ExternalOutput").ap()

    id_sb = cx.sb("id_sb", [128, 128], BF16)
    io16 = cx.sb("io16", [128, 16], F32)
    wqb = cx.sb("wqb", [128, 16, D], BF16)
    wf = cx.sb("wf", [128, D], F32)
    skb = cx.sb("skb", [128, 16, 128], BF16)
    Mt = cx.sb("Mt", [128, D], F32)
    SHt = cx.sb("SHt", [128, D], F32)
    xt = [cx.sb("xt%d" % i, [128, D], F32) for i in range(2)]
    hb = cx.sb("hb", [128, D], BF16)
    hT = cx.sb("hT", [128, 16, 128], BF16)
    qsb = cx.sb("qsb", [128, D], BF16)
    qT = cx.sb("qT", [128, 16, 128], BF16)
    s1 = cx.sb("s1", [128, D], F32)
    s2 = cx.sb("s2", [128, D], F32)
    tv = cx.sb("tv", [128, 16, 16], F32)
    ti = cx.sb("ti", [128, 16, 16], U32)
    tif = cx.sb("tif", [128, 16, 16], F32)
    cand = cx.sb("cand", [128, 8, 256], F32)
    cand2 = cx.sb("cand2", [128, 8, 256], F32)
    cv = cx.sb("cv", [128, 8, 16], F32)
    cpos = cx.sb("cpos", [128, 8, 16], U32)
    pu = cx.sb("pu", [128, 8, 16], U32)
    pa = cx.sb("pa", [128, 8, 16], F32)
    pbf = cx.sb("pbf", [128, 8, 16], F32)
    sel = cx.sb("sel", [128, 8, 16, 16], F32)
    I1 = cx.sb("I1", [128, 8, 16], F32)
    I2 = cx.sb("I2", [128, 8, 16], F32)
    ef = cx.sb("ef", [128, 8, 16], F32)
    ei = [cx.sb("ei%d" % i, [128, 128], I32) for i in range(2)]
    gs = cx.sb("gs", [128, 8], F32)
    gt = [cx.sb("gt%d" % i, [128, 8, 16], F32) for i in range(2)]
    pT = [cx.ps("pT%d" % i, [128, 1024], BF16) for i in range(2)]
    qps = cx.ps("qps", [128, D], F32)

    S.op("sync", lambda e: e.dma_start(out=id_sb[:], in_=ident), writes=["id"], dma=True)
    S.op("sync", lambda e: e.dma_start(out=io16[:], in_=iot.broadcast_to([128, 16])), writes=["io16"], dma=True)
    S.op("sync", lambda e: e.dma_start(out=wf[:], in_=skt), writes=["wf"], dma=True)
    S.op("gpsimd", lambda e: e.tensor_copy(out=skb[:].rearrange("p a b -> p (a b)"), in_=wf[:]),
         reads=["wf"], writes=["skb"])
    for ch in range(16):
        S.op("sync", lambda e, ch=ch: e.dma_start(out=wf[:], in_=w[ch * 128:(ch + 1) * 128, :]),
             writes=["wf"], dma=True)
        S.op("gpsimd", lambda e, ch=ch: e.tensor_copy(out=wqb[:, ch, :], in_=wf[:]),
             reads=["wf"], writes=["wqb"])

    for blk in range(min(NBLK, P3A_NB)):
        b = blk % 2
        if blk == 0 or blk == NBLK - 1:
            mi = 0 if blk == 0 else 1
            S.op("sync", lambda e, mi=mi: e.dma_start(
                out=Mt[:], in_=modv[mi, 0:1, :].broadcast_to([128, D])), writes=["Mt"], dma=True)
            S.op("sync", lambda e, mi=mi: e.dma_start(
                out=SHt[:], in_=modv[mi, 1:2, :].broadcast_to([128, D])), writes=["SHt"], dma=True)
            S.op("gpsimd", lambda e: e.tensor_scalar(
                out=Mt[:], in0=Mt[:], scalar1=1.0, scalar2=None, op0=ALU.add),
                reads=["Mt"], writes=["Mt"])
        S.op("sync", lambda e, b=b, blk=blk: e.dma_start(
            out=xt[b][:], in_=x[blk * 128:(blk + 1) * 128, :]), reads=[("x1dram", blk)], writes=[("xt", b)], dma=True)
        S.op("gpsimd", lambda e, b=b: e.tensor_tensor(out=xt[b][:], in0=xt[b][:], in1=Mt[:], op=ALU.mult),
             reads=[("xt", b), "Mt"], writes=[("xt", b)])
        if P3A_VAR == 2:
            S.op("vector", lambda e, b=b: e.tensor_tensor(out=hb[:], in0=xt[b][:], in1=SHt[:], op=ALU.add),
                 reads=[("xt", b), "SHt"], writes=["hb"])
        else:
            S.op("vector", lambda e, b=b: e.tensor_tensor(out=xt[b][:], in0=xt[b][:], in1=SHt[:], op=ALU.add),
                 reads=[("xt", b), "SHt"], writes=[("xt", b)])
        if want_h2 and P3A_VAR != 2:
            S.op("sync", lambda e, b=b, blk=blk: e.dma_start(out=h2o[blk * 128:(blk + 1) * 128, :], in_=xt[b][:]),
                 reads=[("xt", b)], dma=True, is_out=True)
        if P3A_STOP == 1:
            continue
        if P3A_VAR == 2:
            pass
        elif P3A_VAR == 1:
            S.op("vector", lambda e, b=b: e.tensor_copy(out=hb[:], in_=xt[b][:]), reads=[("xt", b)], writes=["hb"])
        else:
            S.op("scalar", lambda e, b=b: e.copy(out=hb[:], in_=xt[b][:]), reads=[("xt", b)], writes=["hb"])
        if P3A_STOP == 10:
            continue
        for half in range(2):
            for j in range(8):
                ch = half * 8 + j
                S.op("tensor", lambda e, half=half, j=j, ch=ch: e.transpose(
                    out=pT[half][:, j * 128:(j + 1) * 128], in_=hb[:, ch * 128:(ch + 1) * 128],
                    identity=id_sb[:]), reads=["hb", "id"], writes=[("pT", half)])
            S.op("scalar", lambda e, half=half: e.copy(
                out=hT[:, half * 8:(half + 1) * 8, :], in_=pT[half][:].rearrange("p (c t) -> p c t", c=8)),
                reads=[("pT", half)], writes=["hT"])
        if P3A_STOP == 2:
            continue
        for cb in range(4):
            for ch in range(16):
                S.op("tensor", lambda e, cb=cb, ch=ch: e.matmul(
                    qps[:, cb * 512:(cb + 1) * 512], lhsT=hT[:, ch, :],
                    rhs=wqb[:, ch, cb * 512:(cb + 1) * 512], start=(ch == 0), stop=(ch == 15)),
                    reads=["hT", "wqb"], writes=[("qps", cb)])
            S.op("scalar", lambda e, cb=cb: e.copy(out=qsb[:, cb * 512:(cb + 1) * 512],
                                                   in_=qps[:, cb * 512:(cb + 1) * 512]),
                 reads=[("qps", cb)], writes=["qsb"])
        if P3A_STOP == 3:
            continue
        for half in range(2):
            for j in range(8):
                hp = half * 8 + j
                S.op("tensor", lambda e, half=half, j=j, hp=hp: e.transpose(
                    out=pT[half][:, j * 128:(j + 1) * 128], in_=qsb[:, hp * 128:(hp + 1) * 128],
                    identity=id_sb[:]), reads=["qsb", "id"], writes=[("pT", half)])
            S.op("scalar", lambda e, half=half: e.copy(
                out=qT[:, half * 8:(half + 1) * 8, :], in_=pT[half][:].rearrange("p (c t) -> p c t", c=8)),
                reads=[("pT", half)], writes=["qT"])
        if P3A_STOP == 4:
            continue
        for hp in range(16):
            cb = hp // 4
            S.op("tensor", lambda e, hp=hp: e.matmul(
                qps[:, hp * 128:(hp + 1) * 128], lhsT=qT[:, hp, :], rhs=skb[:, hp, :], start=True, stop=True),
                reads=["qT", "skb"], writes=[("qps", cb)])
        for cb in range(4):
            S.op("scalar", lambda e, cb=cb: e.copy(out=s1[:, cb * 512:(cb + 1) * 512],
                                                   in_=qps[:, cb * 512:(cb + 1) * 512]),
                 reads=[("qps", cb)], writes=["s1"])
        if P3A_STOP == 5:
            continue
        for hp in range(16):
            sl = slice(hp * 128, (hp + 1) * 128)
            S.op("vector", lambda e, hp=hp, sl=sl: e.max(out=tv[:, hp, 0:8], in_=s1[:, sl]),
                 reads=["s1"], writes=["tv"])
            S.op("vector", lambda e, hp=hp, sl=sl: e.max_index(out=ti[:, hp, 0:8], in_max=tv[:, hp, 0:8],
                                                               in_values=s1[:, sl]),
                 reads=["s1", "tv"], writes=["ti"])
            S.op("vector", lambda e, hp=hp, sl=sl: e.match_replace(
                out=s2[:, sl], in_to_replace=tv[:, hp, 0:8], in_values=s1[:, sl], imm_value=-1e30),
                reads=["s1", "tv"], writes=["s2"])
            S.op("vector", lambda e, hp=hp, sl=sl: e.max(out=tv[:, hp, 8:16], in_=s2[:, sl]),
                 reads=["s2"], writes=["tv"])
            S.op("vector", lambda e, hp=hp, sl=sl: e.max_index(out=ti[:, hp, 8:16], in_max=tv[:, hp, 8:16],
                                                               in_values=s2[:, sl]),
                 reads=["s2", "tv"], writes=["ti"])
        if P3A_STOP == 6:
            continue
        tv4 = tv[:].rearrange("p (h two) k -> p h two k", two=2)
        c4v = cand[:].rearrange("p h (i j) -> p h i j", j=16)
        S.op("vector", lambda e: e.tensor_tensor(
            out=c4v, in0=tv4[:, :, 0, :].unsqueeze(3).broadcast_to([128, 8, 16, 16]),
            in1=tv4[:, :, 1, :].unsqueeze(2).broadcast_to([128, 8, 16, 16]), op=ALU.add),
            reads=["tv"], writes=["cand"])
        for h in range(8):
            S.op("vector", lambda e, h=h: e.max(out=cv[:, h, 0:8], in_=cand[:, h, :]),
                 reads=["cand"], writes=["cv"])
            S.op("vector", lambda e, h=h: e.max_index(out=cpos[:, h, 0:8], in_max=cv[:, h, 0:8],
                                                      in_values=cand[:, h, :]),
                 reads=["cand", "cv"], writes=["cpos"])
            S.op("vector", lambda e, h=h: e.match_replace(
                out=cand2[:, h, :], in_to_replace=cv[:, h, 0:8], in_values=cand[:, h, :], imm_value=-1e30),
                reads=["cand", "cv"], writes=["cand2"])
            S.op("vector", lambda e, h=h: e.max(out=cv[:, h, 8:16], in_=cand2[:, h, :]),
                 reads=["cand2"], writes=["cv"])
            S.op("vector", lambda e, h=h: e.max_index(out=cpos[:, h, 8:16], in_max=cv[:, h, 8:16],
                                                      in_values=cand2[:, h, :]),
                 reads=["cand2", "cv"], writes=["cpos"])
        if P3A_STOP == 7:
            continue
        gb = blk % 2
        S.op("vector", lambda e, gb=gb: e.tensor_tensor(
            out=gt[gb][:], in0=cv[:], in1=cv[:, :, 0:1].broadcast_to([128, 8, 16]), op=ALU.subtract),
            reads=["cv"], writes=[("gt", gb)])
        S.op("scalar", lambda e, gb=gb: e.activation(out=gt[gb][:], in_=gt[gb][:], func=AF.Exp),
             reads=[("gt", gb)], writes=[("gt", gb)])
        S.op("vector", lambda e, gb=gb: e.tensor_reduce(out=gs[:], in_=gt[gb][:], axis=AX.X, op=ALU.add),
             reads=[("gt", gb)], writes=["gs"])
        S.op("vector", lambda e: e.reciprocal(out=gs[:], in_=gs[:]), reads=["gs"], writes=["gs"])
        S.op("vector", lambda e, gb=gb: e.tensor_tensor(
            out=gt[gb][:], in0=gt[gb][:], in1=gs[:].unsqueeze(2).broadcast_to([128, 8, 16]), op=ALU.mult),
            reads=[("gt", gb), "gs"], writes=[("gt", gb)])
        S.op("sync", lambda e, gb=gb, blk=blk: e.dma_start(
            out=gwo[blk * 128:(blk + 1) * 128, :], in_=gt[gb][:].rearrange("p h k -> p (h k)")),
            reads=[("gt", gb)], dma=True, is_out=True)
        if P3A_STOP == 8:
            continue
        S.op("vector", lambda e: e.tensor_copy(out=tif[:], in_=ti[:]), reads=["ti"], writes=["tif"])
        tif4 = tif[:].rearrange("p (h two) k -> p h two k", two=2)
        for (which, dst) in [(0, I1), (1, I2)]:
            if which == 0:
                S.op("vector", lambda e: e.tensor_single_scalar(out=pu[:], in_=cpos[:], scalar=4,
                                                                op=ALU.logical_shift_right),
                     reads=["cpos"], writes=["pu"])
            else:
                S.op("vector", lambda e: e.tensor_single_scalar(out=pu[:], in_=cpos[:], scalar=15,
                                                                op=ALU.bitwise_and),
                     reads=["cpos"], writes=["pu"])
            S.op("vector", lambda e: e.tensor_copy(out=pa[:], in_=pu[:]), reads=["pu"], writes=["pa"])
            S.op("vector", lambda e: e.tensor_tensor(
                out=sel[:], in0=io16[:].unsqueeze(1).unsqueeze(1).broadcast_to([128, 8, 16, 16]),
                in1=pa[:].unsqueeze(3).broadcast_to([128, 8, 16, 16]), op=ALU.is_equal),
                reads=["io16", "pa"], writes=["sel"])
            S.op("vector", lambda e, which=which: e.tensor_tensor(
                out=sel[:], in0=sel[:], in1=tif4[:, :, which, :].unsqueeze(2).broadcast_to([128, 8, 16, 16]),
                op=ALU.mult), reads=["sel", "tif"], writes=["sel"])
            S.op("vector", lambda e, dst=dst: e.tensor_reduce(out=dst[:], in_=sel[:], axis=AX.X, op=ALU.add),
                 reads=["sel"], writes=["I%d" % which])
        S.op("vector", lambda e: e.scalar_tensor_tensor(
            out=ef[:].rearrange("p h k -> p (h k)"), in0=I1[:].rearrange("p h k -> p (h k)"), scalar=128.0,
            in1=I2[:].rearrange("p h k -> p (h k)"), op0=ALU.mult, op1=ALU.add),
            reads=["I0", "I1"], writes=["ef"])
        S.op("vector", lambda e, gb=gb: e.tensor_copy(out=ei[gb][:], in_=ef[:].rearrange("p h k -> p (h k)")),
             reads=["ef"], writes=[("ei", gb)])
        S.op("sync", lambda e, gb=gb, blk=blk: e.dma_start(out=idxo[blk * 128:(blk + 1) * 128, :], in_=ei[gb][:]),
             reads=[("ei", gb)], dma=True, is_out=True)


def run_p3a(x1, x1c, mod_l, wq, subkeys):
    nc = get_prog("p3a", build_p3a)
    m = split_mod(mod_l)
    ident = np.eye(128, dtype=np.float32).astype(NPBF)
    skt = np.ascontiguousarray(subkeys.reshape(16, 128, 128).transpose(2, 0, 1).reshape(128, 16 * 128))
    iot = np.arange(16, dtype=np.float32).reshape(1, 16)
    in_maps = []
    for r in range(NCORES):
        b = r // 4
        modv = np.stack([np.stack([m["sc2"][b], m["sh2"][b]]), np.stack([m["sc2"][2], m["sh2"][2]])])
        in_maps.append({"x": core_rows(x1, x1c, r), "modv": np.ascontiguousarray(modv), "pwq": wq,
                        "skt": skt, "ident": ident, "iot": iot})
    res = run(nc, in_maps)
    return res


NEXP = 16384
NUB = 4


P3B_PE_Y = True
CAST_ROWS = 512


def build_p3b(NBLK=NBLK, NEXP=NEXP):
    NROW = NBLK * 128
    nc, cx, S = new_prog()
    x1d = nc.dram_tensor("x1", [NROW, D], F32, kind="ExternalInput").ap()
    idxd = nc.dram_tensor("idx", [NROW, 128], I32, kind="ExternalInput").ap()
    gwd = nc.dram_tensor("gw", [NROW, 128], F32, kind="ExternalInput").ap()
    ud = nc.dram_tensor("u", [NEXP, D], F32, kind="ExternalInput").ap()
    vd = nc.dram_tensor("v", [NEXP, D], F32, kind="ExternalInput").ap()
    modv = nc.dram_tensor("modv", [2, 3, D], F32, kind="ExternalInput").ap()
    lngd = nc.dram_tensor("lng", [1, D], F32, kind="ExternalInput").ap()
    lnbd = nc.dram_tensor("lnb", [1, D], F32, kind="ExternalInput").ap()
    identd = nc.dram_tensor("identf", [128, 128], F32, kind="ExternalInput").ap()
    out = nc.dram_tensor("out", [NROW, D], F32, kind="ExternalOutput").ap()
    ubd = nc.dram_tensor("ub_scr", [NEXP, D], BF16).ap()
    vbd = nc.dram_tensor("vb_scr", [NEXP, D], BF16).ap()

    mark = cx.mark()
    cx.prefix = "c_"
    RP = CAST_ROWS // 128
    stg = [cx.sb("stg%d" % i, [128, RP * D], F32) for i in range(3)]
    stb = [cx.sb("stb%d" % i, [128, RP * D], BF16) for i in range(3)]
    engs = ["scalar", "vector", "gpsimd"]
    n = 0
    for (src, dst) in [(ud, ubd), (vd, vbd)]:
        for c in range(NEXP // CAST_ROWS):
            k = n % 3
            n += 1
            rs = slice(c * CAST_ROWS, (c + 1) * CAST_ROWS)
            S.op("sync", lambda e, k=k, src=src, rs=rs: e.dma_start(
                out=stg[k][:], in_=src[rs, :].rearrange("(p r) d -> p (r d)", p=128)),
                writes=[("stg", k)], dma=True)
            eng = engs[k]
            if eng == "scalar":
                S.op("scalar", lambda e, k=k: e.copy(out=stb[k][:], in_=stg[k][:]),
                     reads=[("stg", k)], writes=[("stb", k)])
            else:
                S.op(eng, lambda e, k=k: e.tensor_copy(out=stb[k][:], in_=stg[k][:]),
                     reads=[("stg", k)], writes=[("stb", k)])
            S.op("sync", lambda e, k=k, dst=dst, rs=rs: e.dma_start(
                out=dst[rs, :].rearrange("(p r) d -> p (r d)", p=128), in_=stb[k][:]),
                reads=[("stb", k)], dma=True)
    barrier(S)
    cx.release(mark)
    cx.prefix = ""

    x1 = [cx.sb("x1t%d" % i, [128, D], F32) for i in range(2)]
    h2 = [cx.sb("h2t%d" % i, [128, D], F32) for i in range(2)]
    it = [cx.sb("it%d" % i, [128, 128], I32) for i in range(2)]
    gw = [cx.sb("gw%d" % i, [128, 128], F32) for i in range(2)]
    NB = 8
    U = [cx.sb("U%d" % i, [128, D], BF16) for i in range(NB)]
    Vt = [cx.sb("V%d" % i, [128, D], BF16) for i in range(NB)]
    junk = cx.sb("junk", [128, D], BF16)
    act = cx.sb("act", [128, 128], F32)
    wt = cx.sb("wt", [128, 128], F32)
    y = cx.sb("y", [128, D], F32)
    Mt = cx.sb("Mt", [128, D], F32)
    SHt = cx.sb("SHt", [128, D], F32)
    G2 = cx.sb("G2", [128, D], F32)
    lng = cx.sb("lng_sb", [128, D], F32)
    lnb = cx.sb("lnb_sb", [128, D], F32)
    stats = cx.sb("stats", [128, 4, 6], F32)
    mv = cx.sb("mv", [128, 4], F32)
    idf = cx.sb("idf", [128, 128], F32)
    if P3B_PE_Y:
        dg = cx.sb("dg", [128, 128, 128], BF16)
        yps = cx.ps("yps", [128, D], F32)

    S.op("sync", lambda e: e.dma_start(out=lng[:], in_=lngd.broadcast_to([128, D])), writes=["lng"], dma=True)
    S.op("sync", lambda e: e.dma_start(out=lnb[:], in_=lnbd.broadcast_to([128, D])), writes=["lnb"], dma=True)
    S.op("sync", lambda e: e.dma_start(out=idf[:], in_=identd), writes=["idf"], dma=True)
    nu = 0
    nv = 0
    cur = None
    for blk in range(NBLK):
        b = blk % 2
        mi = 1 if blk % 17 == 16 else 0
        if mi != cur:
            cur = mi
            S.op("sync", lambda e, mi=mi: e.dma_start(out=Mt[:], in_=modv[mi, 0:1, :].broadcast_to([128, D])),
                 writes=["Mt"], dma=True)
            S.op("sync", lambda e, mi=mi: e.dma_start(out=SHt[:], in_=modv[mi, 1:2, :].broadcast_to([128, D])),
                 writes=["SHt"], dma=True)
            S.op("sync", lambda e, mi=mi: e.dma_start(out=G2[:], in_=modv[mi, 2:3, :].broadcast_to([128, D])),
                 writes=["G2"], dma=True)
            S.op("vector", lambda e: e.tensor_scalar(out=Mt[:], in0=Mt[:], scalar1=1.0, scalar2=None, op0=ALU.add),
                 reads=["Mt"], writes=["Mt"])
        rs = slice(blk * 128, (blk + 1) * 128)
        S.op("sync", lambda e, b=b, rs=rs: e.dma_start(out=x1[b][:], in_=x1d[rs, :]), writes=[("x1", b)], dma=True)
        S.op("sync", lambda e, b=b, rs=rs: e.dma_start(out=it[b][:], in_=idxd[rs, :]), writes=[("it", b)], dma=True)
        S.op("sync", lambda e, b=b, rs=rs: e.dma_start(out=gw[b][:], in_=gwd[rs, :]), writes=[("gw", b)], dma=True)
        S.op("vector", lambda e, b=b: e.tensor_tensor(out=h2[b][:], in0=x1[b][:], in1=Mt[:], op=ALU.mult),
             reads=[("x1", b), "Mt"], writes=[("h2", b)])
        S.op("vector", lambda e, b=b: e.tensor_tensor(out=h2[b][:], in0=h2[b][:], in1=SHt[:], op=ALU.add),
             reads=[("h2", b), "SHt"], writes=[("h2", b)])
        for s in range(128):
            ub = nu % NB
            nu += 1
            S.op("gpsimd", lambda e, ub=ub, b=b, s=s: e.indirect_dma_start(
                out=U[ub][:], out_offset=None, in_=ubd,
                in_offset=bass.IndirectOffsetOnAxis(ap=it[b][:, s:s + 1], axis=0)),
                reads=[("it", b)], writes=[("U", ub)], dma=True)
            S.op("vector", lambda e, ub=ub, b=b, s=s: e.scalar_tensor_tensor(
                out=junk[:], in0=U[ub][:], scalar=1.0, in1=h2[b][:], op0=ALU.mult, op1=ALU.mult,
                accum_out=act[:, s:s + 1]),
                reads=[("U", ub), ("h2", b)], writes=["junk", "act"])
        S.op("scalar", lambda e: e.activation(out=wt[:], in_=act[:], func=AF.Gelu), reads=["act"], writes=["wt"])
        S.op("vector", lambda e, b=b: e.tensor_tensor(out=wt[:], in0=wt[:], in1=gw[b][:], op=ALU.mult),
             reads=["wt", ("gw", b)], writes=["wt"])
        if P3B_PE_Y:
            S.op("vector", lambda e: e.tensor_tensor(
                out=dg[:], in0=idf[:].unsqueeze(1).broadcast_to([128, 128, 128]),
                in1=wt[:].unsqueeze(2).broadcast_to([128, 128, 128]), op=ALU.mult),
                reads=["idf", "wt"], writes=["dg"])
        for s in range(128):
            vb = nv % NB
            nv += 1
            S.op("gpsimd", lambda e, vb=vb, b=b, s=s: e.indirect_dma_start(
                out=Vt[vb][:], out_offset=None, in_=vbd,
                in_offset=bass.IndirectOffsetOnAxis(ap=it[b][:, s:s + 1], axis=0)),
                reads=[("it", b)], writes=[("V", vb)], dma=True)
            if P3B_PE_Y:
                for cb in range(4):
                    S.op("tensor", lambda e, vb=vb, s=s, cb=cb: e.matmul(
                        yps[:, cb * 512:(cb + 1) * 512], lhsT=dg[:, s, :], rhs=Vt[vb][:, cb * 512:(cb + 1) * 512],
                        start=(s == 0), stop=(s == 127)),
                        reads=[("V", vb), "dg"], writes=[("yps", cb)])
            elif s == 0:
                S.op("vector", lambda e, vb=vb, s=s: e.tensor_scalar(
                    out=y[:], in0=Vt[vb][:], scalar1=wt[:, s:s + 1], scalar2=None, op0=ALU.mult),
                    reads=[("V", vb), "wt"], writes=["y"])
            else:
                S.op("vector", lambda e, vb=vb, s=s: e.scalar_tensor_tensor(
                    out=y[:], in0=Vt[vb][:], scalar=wt[:, s:s + 1], in1=y[:], op0=ALU.mult, op1=ALU.add),
                    reads=[("V", vb), "wt", "y"], writes=["y"])
        if P3B_PE_Y:
            for cb in range(4):
                S.op("vector", lambda e, cb=cb: e.tensor_tensor(
                    out=y[:, cb * 512:(cb + 1) * 512], in0=yps[:, cb * 512:(cb + 1) * 512],
                    in1=G2[:, cb * 512:(cb + 1) * 512], op=ALU.mult),
                    reads=[("yps", cb), "G2"], writes=["y"])
        else:
            S.op("vector", lambda e: e.tensor_tensor(out=y[:], in0=y[:], in1=G2[:], op=ALU.mult),
                 reads=["y", "G2"], writes=["y"])
        S.op("vector", lambda e, b=b: e.scalar_tensor_tensor(
            out=y[:], in0=x1[b][:], scalar=ALPHA, in1=y[:], op0=ALU.mult, op1=ALU.add),
            reads=[("x1", b), "y"], writes=["y"])
        emit_ln(S, y, "y", stats, mv, lng, lnb, x1[b], ("x1", b), 0)
        S.op("sync", lambda e, b=b, rs=rs: e.dma_start(out=out[rs, :], in_=x1[b][:]),
             reads=[("x1", b)], dma=True, is_out=True)
    return finish_prog(nc, cx, S)


P3B_CORES = 8


def run_p3b(idx_list, gw_list, x1, x1c, mod_l, u, v, ln_g, ln_b):
    per = NCORES // P3B_CORES
    nc = get_prog("p3b", build_p3b, NBLK * per, NEXP)
    m = split_mod(mod_l)
    in_maps = []
    for p in range(P3B_CORES):
        lcs = list(range(p * per, (p + 1) * per))
        b = lcs[0] // 4
        modv = np.stack([np.stack([m["sc2"][b], m["sh2"][b], m["g2"][b]]),
                         np.stack([m["sc2"][2], m["sh2"][2], m["g2"][2]])])
        in_maps.append({"x1": np.concatenate([core_rows(x1, x1c, r) for r in lcs], 0),
                        "idx": np.concatenate([idx_list[r] for r in lcs], 0),
                        "gw": np.concatenate([gw_list[r] for r in lcs], 0),
                        "u": u, "v": v, "modv": np.ascontiguousarray(modv),
                        "identf": np.eye(128, dtype=np.float32),
                        "lng": np.ascontiguousarray(ln_g.reshape(1, D)),
                        "lnb": np.ascontiguousarray(ln_b.reshape(1, D))})
    res = run_bass_kernel_spmd(nc, in_maps, core_ids=list(range(P3B_CORES))).results
    outs = []
    for p in range(P3B_CORES):
        o = res[p]["out"]
        for j in range(per):
            outs.append(o[j * NROW:(j + 1) * NROW])
    return gather_rows(outs, D, np.float32)


def run_p23(att_lat, att_ctx, x, xc, mod_l, w_out, ln_g, ln_b, wq, subkeys):
    nc = get_prog("p23", build_p23)
    m = split_mod(mod_l)
    ident = np.eye(128, dtype=np.float32).astype(NPBF)
    skt = np.ascontiguousarray(subkeys.reshape(16, 128, 128).transpose(2, 0, 1).reshape(128, 16 * 128))
    iot = np.arange(16, dtype=np.float32).reshape(1, 16)
    in_maps = []
    for r in range(NCORES):
        b = r // 4
        rows = core_rows(att_lat, att_ctx, r)
        aT = np.ascontiguousarray(rows.reshape(NBLK, 128, 16, 128).transpose(0, 3, 2, 1))
        modv = np.stack([np.stack([m["sc2"][b], m["sh2"][b]]), np.stack([m["sc2"][2], m["sh2"][2]])])
        in_maps.append({"attT": aT, "x": core_rows(x, xc, r), "wout": w_out,
                        "g1": np.ascontiguousarray(np.stack([m["g1"][b], m["g1"][2]])),
                        "lng": np.ascontiguousarray(ln_g.reshape(1, D)),
                        "lnb": np.ascontiguousarray(ln_b.reshape(1, D)),
                        "modv": np.ascontiguousarray(modv), "pwq": wq, "skt": skt, "ident": ident, "iot": iot})
    res = run(nc, in_maps)
    x1, x1c = gather_rows([res[r]["out"] for r in range(NCORES)], D, np.float32)
    return x1, x1c, [res[r]["idx"] for r in range(NCORES)], [res[r]["gw"] for r in range(NCORES)]


def kernel(x, c, ctx, c_ctx, mod_w, mod_b, ln_g, ln_b, ab_w_in, ab_w_out, a_sink, b_rpb,
           c_w_in, c_w_out, c_q_gain, c_k_gain, peer_wq, peer_subkeys, peer_u, peer_v):
    f = lambda a: np.ascontiguousarray(np.asarray(a, dtype=np.float32))
    x, c, ctx, c_ctx = f(x), f(c), f(ctx), f(c_ctx)
    mod = run_mod(c, c_ctx, f(mod_w), f(mod_b))
    xc = ctx
    dummy_gain = np.zeros((2, 128), np.float32)
    for layer in range(4):
        i = layer // 2
        mod_l = mod[layer]
        if layer % 2 == 0:
            q_lat, q_ctx = run_p1("ab", x, xc, mod_l, f(ab_w_in[i]), dummy_gain)
            att_lat, att_ctx = run_p2a_ab(q_lat, q_ctx, f(a_sink[i]), f(b_rpb[i]))
            w_out = f(ab_w_out[i])
        else:
            gains = np.ascontiguousarray(np.stack([f(c_q_gain[i]), f(c_k_gain[i])]))
            q_lat, q_ctx = run_p1("c", x, xc, mod_l, f(c_w_in[i]), gains)
            att_lat, att_ctx = run_p2a_c(q_lat, q_ctx)
            w_out = f(c_w_out[i])
        x1, x1c, idx_l, gw_l = run_p23(att_lat, att_ctx, x, xc, mod_l, w_out, f(ln_g[layer, 0]), f(ln_b[layer, 0]),
                                       f(peer_wq[layer]), f(peer_subkeys[layer]))
        x, xc = run_p3b(idx_l, gw_l, x1, x1c, mod_l, f(peer_u[layer]), f(peer_v[layer]),
                        f(ln_g[layer, 1]), f(ln_b[layer, 1]))
    return x
```

```python
import numpy as np
import ml_dtypes
import concourse.bass as bass
import concourse.mybir as mybir
from concourse.bass_utils import run_bass_kernel_spmd

F32 = mybir.dt.float32
BF16 = mybir.dt.bfloat16
U32 = mybir.dt.uint32
I32 = mybir.dt.int32
ALU = mybir.AluOpType
AF = mybir.ActivationFunctionType
AX = mybir.AxisListType
NPBF = ml_dtypes.bfloat16

NCORES = 8
D = 2048
ENGS = ["tensor", "vector", "scalar", "gpsimd", "sync"]
NDMA = 8


class Sched:
    def __init__(self, nc, sems):
        self.nc = nc
        self.sems = sems
        self.q = {e: [] for e in ENGS}
        self.cnt = {}
        self.waited = {e: {} for e in ENGS}
        self.last_w = {}
        self.readers = {}
        self.rr = {e: 0 for e in ENGS}
        self.out_dma = []

    def op(self, eng, fn, reads=(), writes=(), dma=False, is_out=False):
        deps = []
        for b in reads:
            if b in self.last_w:
                deps.append(self.last_w[b])
        for b in writes:
            if b in self.last_w:
                deps.append(self.last_w[b])
            deps.extend(self.readers.get(b, ()))
        if dma:
            key = (eng, "d", self.rr[eng])
            self.rr[eng] = (self.rr[eng] + 1) % NDMA
            inc = 16
        else:
            key = (eng, "c")
            inc = 1
        prev = self.cnt.get(key, 0)
        val = prev + inc
        self.cnt[key] = val
        waits = []
        w = self.waited[eng]
        if dma and prev > 0 and w.get(key, 0) < prev:
            w[key] = prev
            waits.append((key, prev))
        for (k, v, de, ddma) in deps:
            if de == eng and eng == "tensor" and not ddma:
                continue
            if w.get(k, 0) >= v:
                continue
            w[k] = v
            waits.append((k, v))
        self.q[eng].append((waits, fn, key, inc))
        rec = (key, val, eng, dma)
        for b in writes:
            self.last_w[b] = rec
            self.readers[b] = []
        for b in reads:
            self.readers.setdefault(b, []).append(rec)
        if is_out:
            self.out_dma.append(rec)
        return rec

    def finish(self):
        waits = []
        for (k, v, de, ddma) in self.out_dma:
            waits.append((k, v))
        self.q["sync"].append((waits, None, None, 0))

    def emit(self, block):
        nc = self.nc
        sems = self.sems

        def body(engname):
            def f(engine):
                for (waits, fn, key, inc) in self.q[engname]:
                    for (k, v) in waits:
                        engine.wait_ge(sems[k], v)
                    if fn is not None:
                        fn(engine).then_inc(sems[key], inc)
            return f

        block.tensor(body("tensor"))
        block.vector(body("vector"))
        block.scalar(body("scalar"))
        block.gpsimd(body("gpsimd"))
        block.sync(body("sync"))


def sem_keys():
    keys = []
    for e in ENGS:
        keys.append((e, "c"))
        for i in range(NDMA):
            keys.append((e, "d", i))
    return keys


class Ctx:
    def __init__(self, nc):
        self.nc = nc
        self.stack = []
        self.prefix = ""

    def mark(self):
        return len(self.stack)

    def release(self, mark):
        while len(self.stack) > mark:
            self.stack.pop().__exit__(None, None, None)

    def enter(self, cm):
        v = cm.__enter__()
        self.stack.append(cm)
        return v

    def close(self):
        while self.stack:
            self.stack.pop().__exit__(None, None, None)

    def sb(self, name, shape, dt):
        return self.enter(self.nc.sbuf_tensor(self.prefix + name, list(shape), dt))

    def ps(self, name, shape, dt):
        return self.enter(self.nc.psum_tensor(self.prefix + name, list(shape), dt))


def new_prog():
    nc = bass.Bass("TRN2", target_bir_lowering=False)
    cx = Ctx(nc)
    sems = {}
    for k in sem_keys():
        sems[k] = cx.enter(nc.semaphore("s_" + "_".join(str(x) for x in k)))
    S = Sched(nc, sems)
    return nc, cx, S


LAST_S = [None]


def finish_prog(nc, cx, S):
    LAST_S[0] = S
    S.finish()
    block = cx.enter(nc.Block())
    S.emit(block)
    cx.close()
    return nc


def run(nc, in_maps):
    res = run_bass_kernel_spmd(nc, in_maps, core_ids=list(range(NCORES)))
    return res.results


MOD_CPC = 6 * D // NCORES


def build_mod():
    nc, cx, S = new_prog()
    cT = nc.dram_tensor("cT", [128, 16, 3], F32, kind="ExternalInput").ap()
    w = nc.dram_tensor("w", [4, D, MOD_CPC], F32, kind="ExternalInput").ap()
    b = nc.dram_tensor("b", [3, 4 * MOD_CPC], F32, kind="ExternalInput").ap()
    out = nc.dram_tensor("out", [3, 4 * MOD_CPC], F32, kind="ExternalOutput").ap()
    c_sb = cx.sb("c_sb", [128, 16, 3], F32)
    s_sb = cx.sb("s_sb", [128, 16, 3], F32)
    b_sb = cx.sb("b_sb", [3, 4 * MOD_CPC], F32)
    o_sb = cx.sb("o_sb", [3, 4 * MOD_CPC], F32)
    wt = [cx.sb("wt%d" % i, [128, MOD_CPC], F32) for i in range(4)]
    ps = [cx.ps("ps%d" % i, [128, 512], F32) for i in range(3)]

    S.op("sync", lambda e: e.dma_start(out=c_sb[:], in_=cT), writes=["c_sb"], dma=True)
    S.op("sync", lambda e: e.dma_start(out=b_sb[:], in_=b), writes=["b_sb"], dma=True)
    S.op("scalar", lambda e: e.activation(out=s_sb[:], in_=c_sb[:], func=AF.Silu),
         reads=["c_sb"], writes=["s_sb"])
    n = 0
    for l in range(4):
        for ch in range(16):
            slot = n % 4
            n += 1
            S.op("sync", lambda e, slot=slot, l=l, ch=ch: e.dma_start(
                out=wt[slot][:], in_=w[l, ch * 128:(ch + 1) * 128, :]),
                writes=[("wt", slot)], dma=True)
            for j in range(3):
                S.op("tensor", lambda e, slot=slot, j=j, ch=ch: e.matmul(
                    ps[j][0:3, :], lhsT=s_sb[:, ch, :], rhs=wt[slot][:, j * 512:(j + 1) * 512],
                    start=(ch == 0), stop=(ch == 15)),
                    reads=[("wt", slot), "s_sb"], writes=[("ps", j)])
        for j in range(3):
            c0 = l * MOD_CPC + j * 512
            S.op("vector", lambda e, j=j, c0=c0: e.tensor_tensor(
                out=o_sb[:, c0:c0 + 512], in0=ps[j][0:3, :], in1=b_sb[:, c0:c0 + 512], op=ALU.add),
                reads=[("ps", j), "b_sb"], writes=["o_sb"])
    S.op("sync", lambda e: e.dma_start(out=out, in_=o_sb[:]), reads=["o_sb"], dma=True, is_out=True)
    return finish_prog(nc, cx, S)


def run_mod(c, c_ctx, mod_w, mod_b):
    cv = np.concatenate([c, c_ctx[None]], 0)
    cT = np.ascontiguousarray(cv.T.reshape(16, 128, 3).transpose(1, 0, 2))
    nc = build_mod()
    in_maps = []
    for r in range(NCORES):
        cs = slice(r * MOD_CPC, (r + 1) * MOD_CPC)
        wr = np.ascontiguousarray(mod_w[:, :, cs])
        br = np.ascontiguousarray(
            np.broadcast_to(mod_b[:, cs].reshape(1, 4 * MOD_CPC), (3, 4 * MOD_CPC)))
        in_maps.append({"cT": cT, "w": wr, "b": br})
    res = run(nc, in_maps)
    mod = np.zeros((4, 3, 6 * D), np.float32)
    for r in range(NCORES):
        o = res[r]["out"].reshape(3, 4, MOD_CPC)
        mod[:, :, r * MOD_CPC:(r + 1) * MOD_CPC] = o.transpose(1, 0, 2)
    return mod


def barrier(S):
    allw = [(k, v) for k, v in S.cnt.items()]
    for e in ENGS:
        waits = []
        for (k, v) in allw:
            if S.waited[e].get(k, 0) < v:
                S.waited[e][k] = v
                waits.append((k, v))
        S.q[e].append((waits, None, None, 0))


NBLK = 17
NROW = NBLK * 128
GW = 768
RMS_EPS = 1e-6


def p1_groups(kind):
    if kind == "ab":
        spec = [(0, 8, None, True), (8, 10, None, True)]
        nh = 36
    else:
        spec = [(0, 16, 0, True), (16, 20, 1, True)]
        nh = 24
    groups = []
    for g in range(nh // 6):
        lo, hi = g * 6, g * 6 + 6
        items = []
        for (a, b, gi, rp) in spec:
            s, e = max(a, lo), min(b, hi)
            if s < e:
                items.append((s - lo, e - lo, gi, rp))
        groups.append(items)
    return groups


def build_p1(kind):
    ncols = 4608 if kind == "ab" else 3072
    ngrp = ncols // GW
    groups = p1_groups(kind)
    nc, cx, S = new_prog()
    x = nc.dram_tensor("x", [NROW, D], F32, kind="ExternalInput").ap()
    modv = nc.dram_tensor("modv", [2, 2, D], F32, kind="ExternalInput").ap()
    w = nc.dram_tensor("w", [D, ncols], F32, kind="ExternalInput").ap()
    cosd = nc.dram_tensor("cos", [NROW, 128], F32, kind="ExternalInput").ap()
    sind = nc.dram_tensor("sin", [NROW, 128], F32, kind="ExternalInput").ap()
    gain = nc.dram_tensor("gain", [2, 128], F32, kind="ExternalInput").ap()
    ident = nc.dram_tensor("ident", [128, 128], BF16, kind="ExternalInput").ap()
    out = nc.dram_tensor("out", [NROW, ncols], BF16, kind="ExternalOutput").ap()

    id_sb = cx.sb("id_sb", [128, 128], BF16)
    hT = cx.sb("hT", [128, NBLK, 16, 128], BF16)
    Mt = cx.sb("Mt", [128, D], F32)
    SHt = cx.sb("SHt", [128, D], F32)
    xt = [cx.sb("xt%d" % i, [128, D], F32) for i in range(2)]
    hb = [cx.sb("hb%d" % i, [128, D], BF16) for i in range(2)]
    pT = [cx.ps("pT%d" % i, [128, 1024], BF16) for i in range(2)]
    pq = [cx.ps("pq%d" % i, [128, 512], F32) for i in range(4)]
    wf = [cx.sb("wf%d" % i, [128, GW], F32) for i in range(3)]
    wb = [cx.sb("wb%d" % i, [128, 16, GW], BF16) for i in range(2)]
    cst = [cx.sb("cst%d" % i, [128, 128], F32) for i in range(2)]
    snt = [cx.sb("snt%d" % i, [128, 128], F32) for i in range(2)]
    gn = cx.sb("gn", [128, 2, 128], F32)
    o = [cx.sb("o%d" % i, [128, GW], F32) for i in range(2)]
    t1 = cx.sb("t1", [128, GW], F32)
    t2 = cx.sb("t2", [128, GW], F32)
    ss = cx.sb("ss", [128, 8], F32)
    ob = [cx.sb("ob%d" % i, [128, GW], BF16) for i in range(2)]

    S.op("sync", lambda e: e.dma_start(out=id_sb[:], in_=ident), writes=["id"], dma=True)
    S.op("sync", lambda e: e.dma_start(
        out=gn[:], in_=gain.unsqueeze(0).broadcast_to([128, 2, 128])), writes=["gn"], dma=True)

    for blk in range(NBLK):
        if blk == 0 or blk == NBLK - 1:
            mi = 0 if blk == 0 else 1
            S.op("sync", lambda e, mi=mi: e.dma_start(
                out=Mt[:], in_=modv[mi, 0:1, :].broadcast_to([128, D])), writes=["Mt"], dma=True)
            S.op("sync", lambda e, mi=mi: e.dma_start(
                out=SHt[:], in_=modv[mi, 1:2, :].broadcast_to([128, D])), writes=["SHt"], dma=True)
            S.op("gpsimd", lambda e: e.tensor_scalar(
                out=Mt[:], in0=Mt[:], scalar1=1.0, scalar2=None, op0=ALU.add),
                reads=["Mt"], writes=["Mt"])
        b = blk % 2
        S.op("sync", lambda e, b=b, blk=blk: e.dma_start(
            out=xt[b][:], in_=x[blk * 128:(blk + 1) * 128, :]), writes=[("xt", b)], dma=True)
        S.op("gpsimd", lambda e, b=b: e.tensor_tensor(
            out=xt[b][:], in0=xt[b][:], in1=Mt[:], op=ALU.mult),
            reads=[("xt", b), "Mt"], writes=[("xt", b)])
        S.op("vector", lambda e, b=b: e.tensor_tensor(
            out=hb[b][:], in0=xt[b][:], in1=SHt[:], op=ALU.add),
            reads=[("xt", b), "SHt"], writes=[("hb", b)])
        for half in range(2):
            for j in range(8):
                ch = half * 8 + j
                S.op("tensor", lambda e, b=b, half=half, j=j, ch=ch: e.transpose(
                    out=pT[half][:, j * 128:(j + 1) * 128], in_=hb[b][:, ch * 128:(ch + 1) * 128],
                    identity=id_sb[:]),
                    reads=[("hb", b), "id"], writes=[("pT", half)])
            S.op("scalar", lambda e, half=half, blk=blk: e.copy(
                out=hT[:, blk, half * 8:(half + 1) * 8, :],
                in_=pT[half][:].rearrange("p (c t) -> p c t", c=8)),
                reads=[("pT", half)], writes=[("hT", blk)])

    nit = 0
    for g in range(ngrp):
        wslot = g % 2
        for ch in range(16):
            fs = (g * 16 + ch) % 3
            S.op("sync", lambda e, fs=fs, ch=ch, g=g: e.dma_start(
                out=wf[fs][:], in_=w[ch * 128:(ch + 1) * 128, g * GW:(g + 1) * GW]),
                writes=[("wf", fs)], dma=True)
            S.op("gpsimd", lambda e, fs=fs, ch=ch, wslot=wslot: e.tensor_copy(
                out=wb[wslot][:, ch, :], in_=wf[fs][:]),
                reads=[("wf", fs)], writes=[("wb", wslot)])
        for blk in range(NBLK):
            it = nit % 2
            nit += 1
            S.op("sync", lambda e, it=it, blk=blk: e.dma_start(
                out=cst[it][:], in_=cosd[blk * 128:(blk + 1) * 128, :]), writes=[("cs", it)], dma=True)
            S.op("sync", lambda e, it=it, blk=blk: e.dma_start(
                out=snt[it][:], in_=sind[blk * 128:(blk + 1) * 128, :]), writes=[("sn", it)], dma=True)
            for half in range(2):
                pb = it * 2 + half
                for ch in range(16):
                    S.op("tensor", lambda e, pb=pb, ch=ch, blk=blk, half=half, wslot=wslot: e.matmul(
                        pq[pb][:, 0:384], lhsT=hT[:, blk, ch, :],
                        rhs=wb[wslot][:, ch, half * 384:(half + 1) * 384],
                        start=(ch == 0), stop=(ch == 15)),
                        reads=[("hT", blk), ("wb", wslot)], writes=[("pq", pb)])
                S.op("scalar", lambda e, pb=pb, it=it, half=half: e.copy(
                    out=o[it][:, half * 384:(half + 1) * 384], in_=pq[pb][:, 0:384]),
                    reads=[("pq", pb)], writes=[("o", it)])
            for (h0, h1, gi, rp) in groups[g]:
                nh = h1 - h0
                osl = o[it][:, h0 * 128:h1 * 128]
                o3 = osl.rearrange("p (h d) -> p h d", d=128)
                if gi is not None:
                    t13 = t1[:, h0 * 128:h1 * 128].rearrange("p (h d) -> p h d", d=128)
                    S.op("scalar", lambda e, osl=osl, h0=h0, h1=h1: e.activation(
                        out=t1[:, h0 * 128:h1 * 128], in_=osl, func=AF.Square),
                        reads=[("o", it)], writes=["t1"])
                    S.op("vector", lambda e, t13=t13, nh=nh: e.tensor_reduce(
                        out=ss[:, 0:nh], in_=t13, axis=AX.X, op=ALU.add),
                        reads=["t1"], writes=["ss"])
                    S.op("vector", lambda e, nh=nh: e.tensor_scalar(
                        out=ss[:, 0:nh], in0=ss[:, 0:nh], scalar1=1.0 / 128, scalar2=RMS_EPS,
                        op0=ALU.mult, op1=ALU.add), reads=["ss"], writes=["ss"])
                    S.op("scalar", lambda e, nh=nh: e.activation(
                        out=ss[:, 0:nh], in_=ss[:, 0:nh], func=AF.Sqrt), reads=["ss"], writes=["ss"])
                    S.op("vector", lambda e, nh=nh: e.reciprocal(
                        out=ss[:, 0:nh], in_=ss[:, 0:nh]), reads=["ss"], writes=["ss"])
                    S.op("vector", lambda e, o3=o3, nh=nh: e.tensor_tensor(
                        out=o3, in0=o3, in1=ss[:, 0:nh].unsqueeze(2).broadcast_to([128, nh, 128]),
                        op=ALU.mult), reads=[("o", it), "ss"], writes=[("o", it)])
                    S.op("vector", lambda e, o3=o3, nh=nh, gi=gi: e.tensor_tensor(
                        out=o3, in0=o3, in1=gn[:, gi:gi + 1, :].broadcast_to([128, nh, 128]),
                        op=ALU.mult), reads=[("o", it), "gn"], writes=[("o", it)])
                if rp:
                    t13 = t1[:, h0 * 128:h1 * 128].rearrange("p (h d) -> p h d", d=128)
                    S.op("vector", lambda e, o3=o3, t13=t13, nh=nh, it=it: e.tensor_tensor(
                        out=t13, in0=o3, in1=cst[it][:].unsqueeze(1).broadcast_to([128, nh, 128]),
                        op=ALU.mult), reads=[("o", it), ("cs", it)], writes=["t1"])
                    o5 = osl.rearrange("p (h a b c) -> p h a b c", a=2, b=2, c=32)
                    t25 = t2[:, h0 * 128:h1 * 128].rearrange("p (h a b c) -> p h a b c", a=2, b=2, c=32)
                    s4 = snt[it][:].rearrange("p (a b c) -> p a b c", a=2, b=2, c=32)
                    for wh in range(2):
                        S.op("gpsimd", lambda e, o5=o5, t25=t25, s4=s4, wh=wh, nh=nh: e.tensor_tensor(
                            out=t25[:, :, :, wh, :], in0=o5[:, :, :, 1 - wh, :],
                            in1=s4[:, :, wh, :].unsqueeze(1).broadcast_to([128, nh, 2, 32]),
                            op=ALU.mult), reads=[("o", it), ("sn", it)], writes=["t2"])
                    S.op("vector", lambda e, h0=h0, h1=h1, it=it: e.tensor_tensor(
                        out=ob[it][:, h0 * 128:h1 * 128], in0=t1[:, h0 * 128:h1 * 128],
                        in1=t2[:, h0 * 128:h1 * 128], op=ALU.add),
                        reads=["t1", "t2"], writes=[("ob", it)])
            covered = [False] * 6
            for (h0, h1, gi, rp) in groups[g]:
                if rp:
                    for hh in range(h0, h1):
                        covered[hh] = True
            hh = 0
            while hh < 6:
                if covered[hh]:
                    hh += 1
                    continue
                h2 = hh
                while h2 < 6 and not covered[h2]:
                    h2 += 1
                S.op("vector", lambda e, hh=hh, h2=h2, it=it: e.tensor_copy(
                    out=ob[it][:, hh * 128:h2 * 128], in_=o[it][:, hh * 128:h2 * 128]),
                    reads=[("o", it)], writes=[("ob", it)])
                hh = h2
            S.op("sync", lambda e, it=it, blk=blk, g=g: e.dma_start(
                out=out[blk * 128:(blk + 1) * 128, g * GW:(g + 1) * GW], in_=ob[it][:]),
                reads=[("ob", it)], dma=True, is_out=True)
    return finish_prog(nc, cx, S)


_PROG_CACHE = {}


def get_prog(name, builder, *args):
    key = (name,) + tuple(args)
    if key not in _PROG_CACHE:
        _PROG_CACHE[key] = builder(*args)
    return _PROG_CACHE[key]


GRID_W = 64
SEQ = 8192
CTX = 256


def rope_tables():
    t = np.arange(SEQ)
    row = (t // GRID_W).astype(np.float32)
    col = (t % GRID_W).astype(np.float32)
    n_freq = 32
    inv_freq = (10000.0 ** (-np.arange(n_freq, dtype=np.float32) / n_freq)).astype(np.float32)
    ang_r = row[:, None] * inv_freq[None, :]
    ang_c = col[:, None] * inv_freq[None, :]
    ang = np.concatenate([ang_r, ang_r, ang_c, ang_c], axis=-1)
    cos = np.cos(ang).astype(np.float32)
    sin = np.sin(ang).astype(np.float32)
    sgn = np.concatenate([-np.ones(32), np.ones(32), -np.ones(32), np.ones(32)]).astype(np.float32)
    return cos, sin * sgn[None, :]


_ROPE = None


def core_rows(x, xc, r):
    b, c4 = r // 4, r % 4
    return np.concatenate([x[b, c4 * 2048:(c4 + 1) * 2048], xc[b, (c4 % 2) * 128:(c4 % 2 + 1) * 128]], 0)


def split_mod(mod_l):
    names = ["sh1", "sc1", "g1", "sh2", "sc2", "g2"]
    return {n: mod_l[:, i * D:(i + 1) * D] for i, n in enumerate(names)}


def run_p1(kind, x, xc, mod_l, w_in, gains):
    global _ROPE
    if _ROPE is None:
        _ROPE = rope_tables()
    cos, sinS = _ROPE
    m = split_mod(mod_l)
    ncols = w_in.shape[1]
    nc = get_prog("p1", build_p1, kind)
    ident = np.eye(128, dtype=np.float32).astype(NPBF)
    in_maps = []
    for r in range(NCORES):
        b, c4 = r // 4, r % 4
        modv = np.stack([np.stack([m["sc1"][b], m["sh1"][b]]), np.stack([m["sc1"][2], m["sh1"][2]])])
        cs = np.concatenate([cos[c4 * 2048:(c4 + 1) * 2048], np.ones((128, 128), np.float32)], 0)
        sn = np.concatenate([sinS[c4 * 2048:(c4 + 1) * 2048], np.zeros((128, 128), np.float32)], 0)
        in_maps.append({"x": core_rows(x, xc, r), "modv": np.ascontiguousarray(modv), "w": w_in,
                        "cos": cs, "sin": sn, "gain": gains, "ident": ident})
    res = run(nc, in_maps)
    q_lat = np.zeros((2, SEQ, ncols), NPBF)
    q_ctx = np.zeros((2, CTX, ncols), NPBF)
    for r in range(NCORES):
        b, c4 = r // 4, r % 4
        o = res[r]["out"]
        q_lat[b, c4 * 2048:(c4 + 1) * 2048] = o[:2048]
        if c4 < 2:
            q_ctx[b, c4 * 128:(c4 + 1) * 128] = o[2048:]
    return q_lat, q_ctx


ATTN_SCALE = 128 ** -0.5
NKB_C = 66


def emit_attn_unit(S, qmov, kblocks, P, Sps, Ops, nsub, uid, post_exp=None):
    n = len(kblocks)
    W = nsub * 128

    def s_mm(i):
        kT, kTk, _, _ = kblocks[i]
        sb = i % 2
        S.op("tensor", lambda e, kT=kT, sb=sb: e.matmul(
            Sps[sb][:, 0:W], lhsT=kT, rhs=qmov[0], start=True, stop=True),
            reads=kTk + qmov[1], writes=[("Sps", sb)])

    s_mm(0)
    for i in range(n):
        if i + 1 < n:
            s_mm(i + 1)
        sb = i % 2
        pb = (uid * 131 + i) % len(P)
        S.op("scalar", lambda e, sb=sb, pb=pb: e.activation(
            out=P[pb][:, 0:W], in_=Sps[sb][:, 0:W], func=AF.Exp, scale=ATTN_SCALE),
            reads=[("Sps", sb)], writes=[("P", pb)])
        if post_exp is not None:
            post_exp(i, pb)
        _, _, v, vk = kblocks[i]
        for j in range(nsub):
            S.op("tensor", lambda e, pb=pb, j=j, v=v, i=i: e.matmul(
                Ops[j][:, 0:129], lhsT=P[pb][:, j * 128:(j + 1) * 128], rhs=v,
                start=(i == 0), stop=(i == n - 1)),
                reads=[("P", pb)] + vk, writes=[("Ops", j)])


def build_p2a_c():
    nc, cx, S = new_prog()
    QT = nc.dram_tensor("QT", [4, 128, 4, NROW], BF16, kind="ExternalInput").ap()
    KT = nc.dram_tensor("KT", [4, 128, NKB_C * 128], BF16, kind="ExternalInput").ap()
    V = nc.dram_tensor("V", [4, 128, NKB_C, 128], BF16, kind="ExternalInput").ap()
    att = nc.dram_tensor("att", [NROW, D], BF16, kind="ExternalOutput").ap()

    KTs = [cx.sb("KTs%d" % i, [128, NKB_C * 128], BF16) for i in range(2)]
    Vs = [cx.sb("Vs%d" % i, [128, NKB_C, 129], BF16) for i in range(2)]
    QTs = [cx.sb("QTs%d" % i, [128, 4, NROW], BF16) for i in range(2)]
    P = [cx.sb("P%d" % i, [128, 512], BF16) for i in range(3)]
    at = [cx.sb("at%d" % i, [128, 512], BF16) for i in range(2)]
    rec = cx.sb("rec", [128, 4], F32)
    Sps = [cx.ps("Sps%d" % i, [128, 512], F32) for i in range(2)]
    Ops = [cx.ps("Ops%d" % i, [128, 512], F32) for i in range(4)]

    for sl in range(2):
        S.op("gpsimd", lambda e, sl=sl: e.memset(Vs[sl][:, :, 128:129], 1.0), writes=[("Vone", sl)])
    uid = 0
    for g in range(4):
        sl = g % 2
        S.op("sync", lambda e, sl=sl, g=g: e.dma_start(out=KTs[sl][:], in_=KT[g]),
             writes=[("KT", sl)], dma=True)
        S.op("sync", lambda e, sl=sl, g=g: e.dma_start(out=Vs[sl][:, :, 0:128], in_=V[g]),
             writes=[("V", sl)], dma=True)
        S.op("sync", lambda e, sl=sl, g=g: e.dma_start(out=QTs[sl][:], in_=QT[g]),
             writes=[("QT", sl)], dma=True)
        for qb in range(NBLK):
            nkb = NKB_C if qb < 16 else 2
            qmov = (QTs[sl][:, :, qb * 128:(qb + 1) * 128], [("QT", sl)])
            kblocks = [(KTs[sl][:, kb * 128:(kb + 1) * 128], [("KT", sl)],
                        Vs[sl][:, kb, :], [("V", sl), ("Vone", sl)]) for kb in range(nkb)]
            emit_attn_unit(S, qmov, kblocks, P, Sps, Ops, 4, uid)
            uid += 1
            ab = uid % 2
            for j in range(4):
                S.op("vector", lambda e, j=j: e.reciprocal(out=rec[:, j:j + 1], in_=Ops[j][:, 128:129]),
                     reads=[("Ops", j)], writes=[("rec", j)])
                S.op("vector", lambda e, j=j, ab=ab: e.tensor_scalar(
                    out=at[ab][:, j * 128:(j + 1) * 128], in0=Ops[j][:, 0:128], scalar1=rec[:, j:j + 1],
                    scalar2=None, op0=ALU.mult),
                    reads=[("Ops", j), ("rec", j)], writes=[("at", ab)])
            S.op("sync", lambda e, ab=ab, qb=qb, g=g: e.dma_start(
                out=att[qb * 128:(qb + 1) * 128, g * 512:(g + 1) * 512], in_=at[ab][:]),
                reads=[("at", ab)], dma=True, is_out=True)
    return finish_prog(nc, cx, S)


def to_headT(q_lat, q_ctx, r, h0, nh):
    rows = core_rows(q_lat, q_ctx, r)
    sub = rows[:, h0 * 128:(h0 + nh) * 128].reshape(NROW, nh, 128)
    return np.ascontiguousarray(sub.transpose(1, 2, 0))


def run_p2a_c(q_lat, q_ctx):
    nc = get_prog("p2a_c", build_p2a_c)
    in_maps = []
    kv = {}
    for b in range(2):
        allr = np.concatenate([q_ctx[b], q_lat[b]], 0)
        k = allr[:, 2048:2560].reshape(NKB_C * 128, 4, 128)
        v = allr[:, 2560:3072].reshape(NKB_C, 128, 4, 128)
        KTh = np.ascontiguousarray(k.transpose(1, 2, 0))
        Vh = np.ascontiguousarray(v.transpose(2, 1, 0, 3))
        kv[b] = (KTh, Vh)
    for r in range(NCORES):
        b = r // 4
        qt = to_headT(q_lat, q_ctx, r, 0, 16).reshape(4, 4, 128, NROW)
        qt = np.ascontiguousarray(qt.transpose(0, 2, 1, 3))
        in_maps.append({"QT": qt, "KT": kv[b][0], "V": kv[b][1]})
    res = run(nc, in_maps)
    return gather_rows([res[r]["att"] for r in range(NCORES)], D, NPBF)


def gather_rows(outs, ncols, dt):
    lat = np.zeros((2, SEQ, ncols), dt)
    ctx = np.zeros((2, CTX, ncols), dt)
    for r in range(NCORES):
        b, c4 = r // 4, r % 4
        o = outs[r]
        lat[b, c4 * 2048:(c4 + 1) * 2048] = o[:2048]
        if c4 < 2:
            ctx[b, c4 * 128:(c4 + 1) * 128] = o[2048:]
    return lat, ctx


NPAT = 25
PAT_CLASS = {0: 5, 1: 10, 14: 15, 15: 20}


def pat_base(lb):
    return PAT_CLASS.get(lb, 0)


def build_p2a_ab():
    nc, cx, S = new_prog()
    QTa = nc.dram_tensor("QTa", [2, 128, 4, NROW], BF16, kind="ExternalInput").ap()
    KTa = nc.dram_tensor("KTa", [2, 128, 20 * 128], BF16, kind="ExternalInput").ap()
    Va = nc.dram_tensor("Va", [2, 128, 20, 128], BF16, kind="ExternalInput").ap()
    QTb = nc.dram_tensor("QTb", [8, 128, NROW], BF16, kind="ExternalInput").ap()
    KTb = nc.dram_tensor("KTb", [8, 128, 22 * 128], BF16, kind="ExternalInput").ap()
    Vb = nc.dram_tensor("Vb", [8, 128, 22, 128], BF16, kind="ExternalInput").ap()
    maskA = nc.dram_tensor("maskA", [128, 4, 128], BF16, kind="ExternalInput").ap()
    sink = nc.dram_tensor("sink", [1, 8], F32, kind="ExternalInput").ap()
    biasB = nc.dram_tensor("biasB", [8, 128, NPAT * 128], F32, kind="ExternalInput").ap()
    maskB = nc.dram_tensor("maskB", [128, NPAT * 128], BF16, kind="ExternalInput").ap()
    att = nc.dram_tensor("att", [NROW, D], BF16, kind="ExternalOutput").ap()

    QTas = [cx.sb("QTas%d" % i, [128, 4, NROW], BF16) for i in range(2)]
    KTas = [cx.sb("KTas%d" % i, [128, 20 * 128], BF16) for i in range(2)]
    Vas = [cx.sb("Vas%d" % i, [128, 20, 129], BF16) for i in range(2)]
    QTbs = [cx.sb("QTbs%d" % i, [128, NROW], BF16) for i in range(2)]
    KTbs = [cx.sb("KTbs%d" % i, [128, 22 * 128], BF16) for i in range(2)]
    Vbs = [cx.sb("Vbs%d" % i, [128, 22, 129], BF16) for i in range(2)]
    mA = cx.sb("mA", [128, 4, 128], BF16)
    mB = cx.sb("mB", [128, NPAT * 128], BF16)
    bB = cx.sb("bB", [128, NPAT * 128], F32)
    eB = cx.sb("eB", [128, NPAT * 128], F32)
    E = [cx.sb("E%d" % i, [128, NPAT * 128], BF16) for i in range(2)]
    snk = cx.sb("snk", [128, 8], F32)
    esnk = cx.sb("esnk", [128, 8], F32)
    P = [cx.sb("P%d" % i, [128, 512], BF16) for i in range(3)]
    at = [cx.sb("at%d" % i, [128, 512], BF16) for i in range(2)]
    rec = cx.sb("rec", [128, 4], F32)
    Sps = [cx.ps("Sps%d" % i, [128, 512], F32) for i in range(2)]
    Ops = [cx.ps("Ops%d" % i, [128, 512], F32) for i in range(4)]

    for sl in range(2):
        S.op("gpsimd", lambda e, sl=sl: e.memset(Vas[sl][:, :, 128:129], 1.0), writes=[("Vaone", sl)])
        S.op("gpsimd", lambda e, sl=sl: e.memset(Vbs[sl][:, :, 128:129], 1.0), writes=[("Vbone", sl)])
    S.op("sync", lambda e: e.dma_start(out=mA[:], in_=maskA), writes=["mA"], dma=True)
    S.op("sync", lambda e: e.dma_start(out=mB[:], in_=maskB), writes=["mB"], dma=True)
    S.op("sync", lambda e: e.dma_start(out=snk[:], in_=sink.broadcast_to([128, 8])), writes=["snk"], dma=True)
    S.op("scalar", lambda e: e.activation(out=esnk[:], in_=snk[:], func=AF.Exp), reads=["snk"], writes=["esnk"])

    uid = 0
    for g in range(2):
        sl = g % 2
        S.op("sync", lambda e, sl=sl, g=g: e.dma_start(out=KTas[sl][:], in_=KTa[g]), writes=[("KTa", sl)], dma=True)
        S.op("sync", lambda e, sl=sl, g=g: e.dma_start(out=Vas[sl][:, :, 0:128], in_=Va[g]), writes=[("Va", sl)], dma=True)
        S.op("sync", lambda e, sl=sl, g=g: e.dma_start(out=QTas[sl][:], in_=QTa[g]), writes=[("QTa", sl)], dma=True)
        for qb in range(NBLK):
            pos = [qb, qb + 1, qb + 2, 18, 19] if qb < 16 else [18, 19]
            qmov = (QTas[sl][:, :, qb * 128:(qb + 1) * 128], [("QTa", sl)])
            kblocks = [(KTas[sl][:, p * 128:(p + 1) * 128], [("KTa", sl)],
                        Vas[sl][:, p, :], [("Va", sl), ("Vaone", sl)]) for p in pos]

            def post_exp(i, pb, qb=qb):
                if qb >= 16 or i not in (0, 2):
                    return
                mi = (0 if qb == 0 else 1) if i == 0 else (3 if qb == 15 else 2)
                S.op("vector", lambda e, pb=pb, mi=mi: e.tensor_tensor(
                    out=P[pb][:].rearrange("p (g q) -> p g q", g=4),
                    in0=P[pb][:].rearrange("p (g q) -> p g q", g=4),
                    in1=mA[:, mi:mi + 1, :].broadcast_to([128, 4, 128]), op=ALU.mult),
                    reads=[("P", pb), "mA"], writes=[("P", pb)])

            emit_attn_unit(S, qmov, kblocks, P, Sps, Ops, 4, uid, post_exp)
            uid += 1
            ab = uid % 2
            for j in range(4):
                hd = g * 4 + j
                S.op("vector", lambda e, j=j, hd=hd: e.tensor_tensor(
                    out=rec[:, j:j + 1], in0=Ops[j][:, 128:129], in1=esnk[:, hd:hd + 1], op=ALU.add),
                    reads=[("Ops", j), "esnk"], writes=[("rec", j)])
                S.op("vector", lambda e, j=j: e.reciprocal(out=rec[:, j:j + 1], in_=rec[:, j:j + 1]),
                     reads=[("rec", j)], writes=[("rec", j)])
                S.op("vector", lambda e, j=j, ab=ab: e.tensor_scalar(
                    out=at[ab][:, j * 128:(j + 1) * 128], in0=Ops[j][:, 0:128], scalar1=rec[:, j:j + 1],
                    scalar2=None, op0=ALU.mult),
                    reads=[("Ops", j), ("rec", j)], writes=[("at", ab)])
            S.op("sync", lambda e, ab=ab, qb=qb, g=g: e.dma_start(
                out=att[qb * 128:(qb + 1) * 128, g * 512:(g + 1) * 512], in_=at[ab][:]),
                reads=[("at", ab)], dma=True, is_out=True)

    for h in range(8):
        sl = h % 2
        S.op("sync", lambda e, sl=sl, h=h: e.dma_start(out=KTbs[sl][:], in_=KTb[h]), writes=[("KTb", sl)], dma=True)
        S.op("sync", lambda e, sl=sl, h=h: e.dma_start(out=Vbs[sl][:, :, 0:128], in_=Vb[h]), writes=[("Vb", sl)], dma=True)
        S.op("sync", lambda e, sl=sl, h=h: e.dma_start(out=QTbs[sl][:], in_=QTb[h]), writes=[("QTb", sl)], dma=True)
        S.op("sync", lambda e, h=h: e.dma_start(out=bB[:], in_=biasB[h]), writes=["bB"], dma=True)
        S.op("scalar", lambda e: e.activation(out=eB[:], in_=bB[:], func=AF.Exp), reads=["bB"], writes=["eB"])
        S.op("gpsimd", lambda e, sl=sl: e.tensor_tensor(out=E[sl][:], in0=eB[:], in1=mB[:], op=ALU.mult),
             reads=["eB", "mB"], writes=[("E", sl)])
        for qb in range(NBLK):
            pos = [qb + s for s in range(5)] + [20, 21] if qb < 16 else [20, 21]
            qmov = (QTbs[sl][:, qb * 128:(qb + 1) * 128], [("QTb", sl)])
            kblocks = [(KTbs[sl][:, p * 128:(p + 1) * 128], [("KTb", sl)],
                        Vbs[sl][:, p, :], [("Vb", sl), ("Vbone", sl)]) for p in pos]

            def post_exp(i, pb, qb=qb, sl=sl):
                if qb >= 16 or i >= 5:
                    return
                pi = pat_base(qb) + i
                S.op("vector", lambda e, pb=pb, pi=pi, sl=sl: e.tensor_tensor(
                    out=P[pb][:, 0:128], in0=P[pb][:, 0:128], in1=E[sl][:, pi * 128:(pi + 1) * 128],
                    op=ALU.mult), reads=[("P", pb), ("E", sl)], writes=[("P", pb)])

            emit_attn_unit(S, qmov, kblocks, P, Sps, Ops, 1, uid, post_exp)
            uid += 1
            ab = uid % 2
            S.op("vector", lambda e: e.reciprocal(out=rec[:, 0:1], in_=Ops[0][:, 128:129]),
                 reads=[("Ops", 0)], writes=[("rec", 0)])
            S.op("vector", lambda e, ab=ab: e.tensor_scalar(
                out=at[ab][:, 0:128], in0=Ops[0][:, 0:128], scalar1=rec[:, 0:1],
                scalar2=None, op0=ALU.mult),
                reads=[("Ops", 0), ("rec", 0)], writes=[("at", ab)])
            S.op("sync", lambda e, ab=ab, qb=qb, h=h: e.dma_start(
                out=att[qb * 128:(qb + 1) * 128, 1024 + h * 128:1024 + (h + 1) * 128], in_=at[ab][:, 0:128]),
                reads=[("at", ab)], dma=True, is_out=True)
    return finish_prog(nc, cx, S)


def nbr_geometry(c4):
    halo = [16 * c4 - 2 + p for p in range(20)]
    if c4 == 0:
        halo[0], halo[1] = 3, None
    if c4 == 3:
        halo[18], halo[19] = 60, None
    mask = np.zeros((128, NPAT, 128), bool)
    drow = np.zeros((128, NPAT, 128), np.int64)
    dcol = np.zeros((128, NPAT, 128), np.int64)
    k = np.arange(128)
    q = np.arange(128)
    for lb in [2, 0, 1, 14, 15]:
        m = 16 * c4 + lb
        seen = []
        for s in range(5):
            gb = halo[lb + s]
            pi = pat_base(lb) + s
            if gb is None or gb in seen or gb < 0 or gb > 63:
                continue
            seen.append(gb)
            kr = (2 * gb + k // 64)[:, None]
            kc = (k % 64)[:, None]
            qr = (2 * m + q // 64)[None, :]
            qc = (q % 64)[None, :]
            rstart = np.clip(qr - 4, 0, 120)
            wstart = np.clip(qc - 8, 0, 48)
            ok = (kr >= rstart) & (kr < rstart + 8) & (kc >= wstart) & (kc < wstart + 16)
            mask[:, pi, :] = ok
            drow[:, pi, :] = np.where(ok, kr - qr + 7, 0)
            dcol[:, pi, :] = np.clip(kc - qc + 15, 0, 30) * ok
    return halo, mask, drow, dcol


def run_p2a_ab(q_lat, q_ctx, sink, rpb):
    nc = get_prog("p2a_ab", build_p2a_ab)
    tri_prev = (np.arange(128)[:, None] >= np.arange(128)[None, :])
    tri_next = (np.arange(128)[:, None] <= np.arange(128)[None, :])
    zeros = np.zeros((128, 128), bool)
    in_maps = []
    for r in range(NCORES):
        b, c4 = r // 4, r % 4
        qt = to_headT(q_lat, q_ctx, r, 0, 8).reshape(2, 4, 128, NROW)
        QTa = np.ascontiguousarray(qt.transpose(0, 2, 1, 3))
        QTb = to_headT(q_lat, q_ctx, r, 12, 8)
        lat = q_lat[b].reshape(64, 128, 4608)
        cxb = q_ctx[b].reshape(2, 128, 4608)
        zb = np.zeros((128, 4608), NPBF)
        blocksA = []
        for p in range(18):
            gb = 16 * c4 - 1 + p
            blocksA.append(lat[gb] if 0 <= gb < 64 else zb)
        blocksA += [cxb[0], cxb[1]]
        A = np.stack(blocksA)
        ka = A[:, :, 1024:1280].reshape(20, 128, 2, 128)
        va = A[:, :, 1280:1536].reshape(20, 128, 2, 128)
        KTa = np.ascontiguousarray(ka.transpose(2, 3, 0, 1).reshape(2, 128, 20 * 128))
        Va = np.ascontiguousarray(va.transpose(2, 1, 0, 3))
        halo, mask, drow, dcol = nbr_geometry(c4)
        blocksB = [lat[gb] if gb is not None and 0 <= gb < 64 else zb for gb in halo] + [cxb[0], cxb[1]]
        Bk = np.stack(blocksB)
        kb = Bk[:, :, 2560:3584].reshape(22, 128, 8, 128)
        vb = Bk[:, :, 3584:4608].reshape(22, 128, 8, 128)
        KTb = np.ascontiguousarray(kb.transpose(2, 3, 0, 1).reshape(8, 128, 22 * 128))
        Vb = np.ascontiguousarray(vb.transpose(2, 1, 0, 3))
        mA = np.stack([zeros if c4 == 0 else tri_prev, tri_prev, tri_next, zeros if c4 == 3 else tri_next], 1)
        biasB = rpb[:, drow, dcol].astype(np.float32)
        in_maps.append({
            "QTa": QTa, "KTa": KTa, "Va": Va, "QTb": QTb, "KTb": KTb, "Vb": Vb,
            "maskA": np.ascontiguousarray(mA).astype(np.float32).astype(NPBF),
            "sink": np.ascontiguousarray(sink.reshape(1, 8)),
            "biasB": np.ascontiguousarray(biasB.reshape(8, 128, NPAT * 128)),
            "maskB": mask.reshape(128, NPAT * 128).astype(np.float32).astype(NPBF)})
    res = run(nc, in_maps)
    return gather_rows([res[r]["att"] for r in range(NCORES)], D, NPBF)


ALPHA = float((2 * 4) ** 0.25)
LN_EPS = 1e-5


def emit_ln(S, zt, zk, stats, mv, lng, lnb, outt, outk, tag):
    for c in range(4):
        S.op("vector", lambda e, c=c: e.bn_stats(out=stats[:, c, :], in_=zt[:, c * 512:(c + 1) * 512]),
             reads=[zk], writes=[("stats", tag)])
    S.op("vector", lambda e: e.bn_aggr(out=mv[:, 0:2], in_=stats[:].rearrange("p c s -> p (c s)")),
         reads=[("stats", tag)], writes=[("mv", tag)])
    S.op("vector", lambda e: e.tensor_scalar(out=mv[:, 2:3], in0=mv[:, 1:2], scalar1=LN_EPS, scalar2=None,
                                             op0=ALU.add), reads=[("mv", tag)], writes=[("mv", tag)])
    S.op("scalar", lambda e: e.activation(out=mv[:, 2:3], in_=mv[:, 2:3], func=AF.Sqrt),
         reads=[("mv", tag)], writes=[("mv", tag)])
    S.op("vector", lambda e: e.reciprocal(out=mv[:, 2:3], in_=mv[:, 2:3]),
         reads=[("mv", tag)], writes=[("mv", tag)])
    S.op("vector", lambda e: e.tensor_scalar(out=zt[:], in0=zt[:], scalar1=mv[:, 0:1], scalar2=mv[:, 2:3],
                                             op0=ALU.subtract, op1=ALU.mult),
         reads=[zk, ("mv", tag)], writes=[zk])
    S.op("gpsimd", lambda e: e.tensor_tensor(out=zt[:], in0=zt[:], in1=lng[:], op=ALU.mult),
         reads=[zk, "lng"], writes=[zk])
    S.op("vector", lambda e: e.tensor_tensor(out=outt[:], in0=zt[:], in1=lnb[:], op=ALU.add),
         reads=[zk, "lnb"], writes=[outk])


def build_p2b():
    nc, cx, S = new_prog()
    emit_p2b(nc, cx, S)
    return finish_prog(nc, cx, S)


def emit_p2b(nc, cx, S):
    attT = nc.dram_tensor("attT", [NBLK, 128, 16, 128], BF16, kind="ExternalInput").ap()
    x = nc.dram_tensor("x", [NROW, D], F32, kind="ExternalInput").ap()
    w = nc.dram_tensor("wout", [D, D], F32, kind="ExternalInput").ap()
    g1 = nc.dram_tensor("g1", [2, D], F32, kind="ExternalInput").ap()
    lngd = nc.dram_tensor("lng", [1, D], F32, kind="ExternalInput").ap()
    lnbd = nc.dram_tensor("lnb", [1, D], F32, kind="ExternalInput").ap()
    out = nc.dram_tensor("out", [NROW, D], F32, kind="ExternalOutput").ap()

    wob = cx.sb("wob", [128, 16, D], BF16)
    wf = [cx.sb("wf%d" % i, [128, D], F32) for i in range(2)]
    aT = [cx.sb("aT%d" % i, [128, 16, 128], BF16) for i in range(2)]
    xt = [cx.sb("xt%d" % i, [128, D], F32) for i in range(2)]
    zt = [cx.sb("zt%d" % i, [128, D], F32) for i in range(2)]
    G1 = cx.sb("G1", [128, D], F32)
    lng = cx.sb("lng_sb", [128, D], F32)
    lnb = cx.sb("lnb_sb", [128, D], F32)
    stats = cx.sb("stats", [128, 4, 6], F32)
    mv = cx.sb("mv", [128, 4], F32)
    yps = [cx.ps("yps%d" % i, [128, D], F32) for i in range(2)]

    S.op("sync", lambda e: e.dma_start(out=lng[:], in_=lngd.broadcast_to([128, D])), writes=["lng"], dma=True)
    S.op("sync", lambda e: e.dma_start(out=lnb[:], in_=lnbd.broadcast_to([128, D])), writes=["lnb"], dma=True)
    for ch in range(16):
        fs = ch % 2
        S.op("sync", lambda e, fs=fs, ch=ch: e.dma_start(out=wf[fs][:], in_=w[ch * 128:(ch + 1) * 128, :]),
             writes=[("wf", fs)], dma=True)
        S.op("gpsimd", lambda e, fs=fs, ch=ch: e.tensor_copy(out=wob[:, ch, :], in_=wf[fs][:]),
             reads=[("wf", fs)], writes=["wob"])
    for blk in range(NBLK):
        b = blk % 2
        if blk == 0 or blk == NBLK - 1:
            mi = 0 if blk == 0 else 1
            S.op("sync", lambda e, mi=mi: e.dma_start(out=G1[:], in_=g1[mi:mi + 1, :].broadcast_to([128, D])),
                 writes=["G1"], dma=True)
        S.op("sync", lambda e, b=b, blk=blk: e.dma_start(out=aT[b][:], in_=attT[blk]), writes=[("aT", b)], dma=True)
        S.op("sync", lambda e, b=b, blk=blk: e.dma_start(out=xt[b][:], in_=x[blk * 128:(blk + 1) * 128, :]),
             writes=[("xt", b)], dma=True)
        for cb in range(4):
            for h in range(16):
                S.op("tensor", lambda e, b=b, cb=cb, h=h: e.matmul(
                    yps[b][:, cb * 512:(cb + 1) * 512], lhsT=aT[b][:, h, :],
                    rhs=wob[:, h, cb * 512:(cb + 1) * 512], start=(h == 0), stop=(h == 15)),
                    reads=[("aT", b), "wob"], writes=[("yps", b, cb)])
        for cb in range(4):
            S.op("vector", lambda e, b=b, cb=cb: e.tensor_tensor(
                out=zt[b][:, cb * 512:(cb + 1) * 512], in0=yps[b][:, cb * 512:(cb + 1) * 512],
                in1=G1[:, cb * 512:(cb + 1) * 512], op=ALU.mult),
                reads=[("yps", b, cb), "G1"], writes=[("zt", b)])
        S.op("vector", lambda e, b=b: e.scalar_tensor_tensor(
            out=zt[b][:], in0=xt[b][:], scalar=ALPHA, in1=zt[b][:], op0=ALU.mult, op1=ALU.add),
            reads=[("xt", b), ("zt", b)], writes=[("zt", b)])
        emit_ln(S, zt[b], ("zt", b), stats, mv, lng, lnb, xt[b], ("xt", b), 0)
        S.op("sync", lambda e, b=b, blk=blk: e.dma_start(out=out[blk * 128:(blk + 1) * 128, :], in_=xt[b][:]),
             reads=[("xt", b)], writes=[("x1dram", blk)], dma=True, is_out=True)
    return out


def run_p2b(att_lat, att_ctx, x, xc, mod_l, w_out, ln_g, ln_b):
    nc = get_prog("p2b", build_p2b)
    m = split_mod(mod_l)
    in_maps = []
    for r in range(NCORES):
        b = r // 4
        rows = core_rows(att_lat, att_ctx, r)
        aT = np.ascontiguousarray(rows.reshape(NBLK, 128, 16, 128).transpose(0, 3, 2, 1))
        in_maps.append({"attT": aT, "x": core_rows(x, xc, r), "wout": w_out,
                        "g1": np.ascontiguousarray(np.stack([m["g1"][b], m["g1"][2]])),
                        "lng": np.ascontiguousarray(ln_g.reshape(1, D)),
                        "lnb": np.ascontiguousarray(ln_b.reshape(1, D))})
    res = run(nc, in_maps)
    return gather_rows([res[r]["out"] for r in range(NCORES)], D, np.float32)


import os
P3A_STOP = int(os.environ.get('P3A_STOP', '0'))
P3A_VAR = int(os.environ.get('P3A_VAR', '0'))
P3A_NB = int(os.environ.get('P3A_NB', '17'))


def build_p3a():
    nc, cx, S = new_prog()
    emit_p3a(nc, cx, S, None, True)
    return finish_prog(nc, cx, S)


def build_p23():
    nc, cx, S = new_prog()
    mark = cx.mark()
    cx.prefix = "a_"
    out = emit_p2b(nc, cx, S)
    barrier(S)
    cx.release(mark)
    cx.prefix = "b_"
    emit_p3a(nc, cx, S, out, False)
    return finish_prog(nc, cx, S)


def emit_p3a(nc, cx, S, x, want_h2):
    if x is None:
        x = nc.dram_tensor("x", [NROW, D], F32, kind="ExternalInput").ap()
    modv = nc.dram_tensor("modv", [2, 2, D], F32, kind="ExternalInput").ap()
    w = nc.dram_tensor("pwq", [D, D], F32, kind="ExternalInput").ap()
    skt = nc.dram_tensor("skt", [128, 16 * 128], F32, kind="ExternalInput").ap()
    ident = nc.dram_tensor("ident", [128, 128], BF16, kind="ExternalInput").ap()
    iot = nc.dram_tensor("iot", [1, 16], F32, kind="ExternalInput").ap()
    h2o = nc.dram_tensor("h2", [NROW, D], F32, kind="ExternalOutput").ap() if want_h2 else None
    idxo = nc.dram_tensor("idx", [NROW, 128], I32, kind="ExternalOutput").ap()
    gwo = nc.dram_tensor("gw", [NROW, 128], F32, kind="ExternalOutput").ap()

    id_sb = cx.sb("id_sb", [128, 128], BF16)
    io16 = cx.sb("io16", [128, 16], F32)
    wqb = cx.sb("wqb", [128, 16, D], BF16)
    wf = cx.sb("wf", [128, D], F32)
    skb = cx.sb("skb", [128, 16, 128], BF16)
    Mt = cx.sb("Mt", [128, D], F32)
    SHt = cx.sb("SHt", [128, D], F32)
    xt = [cx.sb("xt%d" % i, [128, D], F32) for i in range(2)]
    hb = cx.sb("hb", [128, D], BF16)
    hT = cx.sb("hT", [128, 16, 128], BF16)
    qsb = cx.sb("qsb", [128, D], BF16)
    qT = cx.sb("qT", [128, 16, 128], BF16)
    s1 = cx.sb("s1", [128, D], F32)
    s2 = cx.sb("s2", [128, D], F32)
    tv = cx.sb("tv", [128, 16, 16], F32)
    ti = cx.sb("ti", [128, 16, 16], U32)
    tif = cx.sb("tif", [128, 16, 16], F32)
    cand = cx.sb("cand", [128, 8, 256], F32)
    cand2 = cx.sb("cand2", [128, 8, 256], F32)
    cv = cx.sb("cv", [128, 8, 16], F32)
    cpos = cx.sb("cpos", [128, 8, 16], U32)
    pu = cx.sb("pu", [128, 8, 16], U32)
    pa = cx.sb("pa", [128, 8, 16], F32)
    pbf = cx.sb("pbf", [128, 8, 16], F32)
    sel = cx.sb("sel", [128, 8, 16, 16], F32)
    I1 = cx.sb("I1", [128, 8, 16], F32)
    I2 = cx.sb("I2", [128, 8, 16], F32)
    ef = cx.sb("ef", [128, 8, 16], F32)
    ei = [cx.sb("ei%d" % i, [128, 128], I32) for i in range(2)]
    gs = cx.sb("gs", [128, 8], F32)
    gt = [cx.sb("gt%d" % i, [128, 8, 16], F32) for i in range(2)]
    pT = [cx.ps("pT%d" % i, [128, 1024], BF16) for i in range(2)]
    qps = cx.ps("qps", [128, D], F32)

    S.op("sync", lambda e: e.dma_start(out=id_sb[:], in_=ident), writes=["id"], dma=True)
    S.op("sync", lambda e: e.dma_start(out=io16[:], in_=iot.broadcast_to([128, 16])), writes=["io16"], dma=True)
    S.op("sync", lambda e: e.dma_start(out=wf[:], in_=skt), writes=["wf"], dma=True)
    S.op("gpsimd", lambda e: e.tensor_copy(out=skb[:].rearrange("p a b -> p (a b)"), in_=wf[:]),
         reads=["wf"], writes=["skb"])
    for ch in range(16):
        S.op("sync", lambda e, ch=ch: e.dma_start(out=wf[:], in_=w[ch * 128:(ch + 1) * 128, :]),
             writes=["wf"], dma=True)
        S.op("gpsimd", lambda e, ch=ch: e.tensor_copy(out=wqb[:, ch, :], in_=wf[:]),
             reads=["wf"], writes=["wqb"])

    for blk in range(min(NBLK, P3A_NB)):
        b = blk % 2
        if blk == 0 or blk == NBLK - 1:
            mi = 0 if blk == 0 else 1
            S.op("sync", lambda e, mi=mi: e.dma_start(
                out=Mt[:], in_=modv[mi, 0:1, :].broadcast_to([128, D])), writes=["Mt"], dma=True)
            S.op("sync", lambda e, mi=mi: e.dma_start(
                out=SHt[:], in_=modv[mi, 1:2, :].broadcast_to([128, D])), writes=["SHt"], dma=True)
            S.op("gpsimd", lambda e: e.tensor_scalar(
                out=Mt[:], in0=Mt[:], scalar1=1.0, scalar2=None, op0=ALU.add),
                reads=["Mt"], writes=["Mt"])
        S.op("sync", lambda e, b=b, blk=blk: e.dma_start(
            out=xt[b][:], in_=x[blk * 128:(blk + 1) * 128, :]), reads=[("x1dram", blk)], writes=[("xt", b)], dma=True)
        S.op("gpsimd", lambda e, b=b: e.tensor_tensor(out=xt[b][:], in0=xt[b][:], in1=Mt[:], op=ALU.mult),
             reads=[("xt", b), "Mt"], writes=[("xt", b)])
        if P3A_VAR == 2:
            S.op("vector", lambda e, b=b: e.tensor_tensor(out=hb[:], in0=xt[b][:], in1=SHt[:], op=ALU.add),
                 reads=[("xt", b), "SHt"], writes=["hb"])
        else:
            S.op("vector", lambda e, b=b: e.tensor_tensor(out=xt[b][:], in0=xt[b][:], in1=SHt[:], op=ALU.add),
                 reads=[("xt", b), "SHt"], writes=[("xt", b)])
        if want_h2 and P3A_VAR != 2:
            S.op("sync", lambda e, b=b, blk=blk: e.dma_start(out=h2o[blk * 128:(blk + 1) * 128, :], in_=xt[b][:]),
                 reads=[("xt", b)], dma=True, is_out=True)
        if P3A_STOP == 1:
            continue
        if P3A_VAR == 2:
            pass
        elif P3A_VAR == 1:
            S.op("vector", lambda e, b=b: e.tensor_copy(out=hb[:], in_=xt[b][:]), reads=[("xt", b)], writes=["hb"])
        else:
            S.op("scalar", lambda e, b=b: e.copy(out=hb[:], in_=xt[b][:]), reads=[("xt", b)], writes=["hb"])
        if P3A_STOP == 10:
            continue
        for half in range(2):
            for j in range(8):
                ch = half * 8 + j
                S.op("tensor", lambda e, half=half, j=j, ch=ch: e.transpose(
                    out=pT[half][:, j * 128:(j + 1) * 128], in_=hb[:, ch * 128:(ch + 1) * 128],
                    identity=id_sb[:]), reads=["hb", "id"], writes=[("pT", half)])
            S.op("scalar", lambda e, half=half: e.copy(
                out=hT[:, half * 8:(half + 1) * 8, :], in_=pT[half][:].rearrange("p (c t) -> p c t", c=8)),
                reads=[("pT", half)], writes=["hT"])
        if P3A_STOP == 2:
            continue
        for cb in range(4):
            for ch in range(16):
                S.op("tensor", lambda e, cb=cb, ch=ch: e.matmul(
                    qps[:, cb * 512:(cb + 1) * 512], lhsT=hT[:, ch, :],
                    rhs=wqb[:, ch, cb * 512:(cb + 1) * 512], start=(ch == 0), stop=(ch == 15)),
                    reads=["hT", "wqb"], writes=[("qps", cb)])
            S.op("scalar", lambda e, cb=cb: e.copy(out=qsb[:, cb * 512:(cb + 1) * 512],
                                                   in_=qps[:, cb * 512:(cb + 1) * 512]),
                 reads=[("qps", cb)], writes=["qsb"])
        if P3A_STOP == 3:
            continue
        for half in range(2):
            for j in range(8):
                hp = half * 8 + j
                S.op("tensor", lambda e, half=half, j=j, hp=hp: e.transpose(
                    out=pT[half][:, j * 128:(j + 1) * 128], in_=qsb[:, hp * 128:(hp + 1) * 128],
                    identity=id_sb[:]), reads=["qsb", "id"], writes=[("pT", half)])
            S.op("scalar", lambda e, half=half: e.copy(
                out=qT[:, half * 8:(half + 1) * 8, :], in_=pT[half][:].rearrange("p (c t) -> p c t", c=8)),
                reads=[("pT", half)], writes=["qT"])
        if P3A_STOP == 4:
            continue
        for hp in range(16):
            cb = hp // 4
            S.op("tensor", lambda e, hp=hp: e.matmul(
                qps[:, hp * 128:(hp + 1) * 128], lhsT=qT[:, hp, :], rhs=skb[:, hp, :], start=True, stop=True),
                reads=["qT", "skb"], writes=[("qps", cb)])
        for cb in range(4):
            S.op("scalar", lambda e, cb=cb: e.copy(out=s1[:, cb * 512:(cb + 1) * 512],
                                                   in_=qps[:, cb * 512:(cb + 1) * 512]),
                 reads=[("qps", cb)], writes=["s1"])
        if P3A_STOP == 5:
            continue
        for hp in range(16):
            sl = slice(hp * 128, (hp + 1) * 128)
            S.op("vector", lambda e, hp=hp, sl=sl: e.max(out=tv[:, hp, 0:8], in_=s1[:, sl]),
                 reads=["s1"], writes=["tv"])
            S.op("vector", lambda e, hp=hp, sl=sl: e.max_index(out=ti[:, hp, 0:8], in_max=tv[:, hp, 0:8],
                                                               in_values=s1[:, sl]),
                 reads=["s1", "tv"], writes=["ti"])
            S.op("vector", lambda e, hp=hp, sl=sl: e.match_replace(
                out=s2[:, sl], in_to_replace=tv[:, hp, 0:8], in_values=s1[:, sl], imm_value=-1e30),
                reads=["s1", "tv"], writes=["s2"])
            S.op("vector", lambda e, hp=hp, sl=sl: e.max(out=tv[:, hp, 8:16], in_=s2[:, sl]),
                 reads=["s2"], writes=["tv"])
            S.op("vector", lambda e, hp=hp, sl=sl: e.max_index(out=ti[:, hp, 8:16], in_max=tv[:, hp, 8:16],
                                                               in_values=s2[:, sl]),
                 reads=["s2", "tv"], writes=["ti"])
        if P3A_STOP == 6:
            continue
        tv4 = tv[:].rearrange("p (h two) k -> p h two k", two=2)
        c4v = cand[:].rearrange("p h (i j) -> p h i j", j=16)
        S.op("vector", lambda e: e.tensor_tensor(
            out=c4v, in0=tv4[:, :, 0, :].unsqueeze(3).broadcast_to([128, 8, 16, 16]),
            in1=tv4[:, :, 1, :].unsqueeze(2).broadcast_to([128, 8, 16, 16]), op=ALU.add),
            reads=["tv"], writes=["cand"])
        for h in range(8):
            S.op("vector", lambda e, h=h: e.max(out=cv[:, h, 0:8], in_=cand[:, h, :]),
                 reads=["cand"], writes=["cv"])
            S.op("vector", lambda e, h=h: e.max_index(out=cpos[:, h, 0:8], in_max=cv[:, h, 0:8],
                                                      in_values=cand[:, h, :]),
                 reads=["cand", "cv"], writes=["cpos"])
            S.op("vector", lambda e, h=h: e.match_replace(
                out=cand2[:, h, :], in_to_replace=cv[:, h, 0:8], in_values=cand[:, h, :], imm_value=-1e30),
                reads=["cand", "cv"], writes=["cand2"])
            S.op("vector", lambda e, h=h: e.max(out=cv[:, h, 8:16], in_=cand2[:, h, :]),
                 reads=["cand2"], writes=["cv"])
            S.op("vector", lambda e, h=h: e.max_index(out=cpos[:, h, 8:16], in_max=cv[:, h, 8:16],
                                                      in_values=cand2[:, h, :]),
                 reads=["cand2", "cv"], writes=["cpos"])
        if P3A_STOP == 7:
            continue
        gb = blk % 2
        S.op("vector", lambda e, gb=gb: e.tensor_tensor(
            out=gt[gb][:], in0=cv[:], in1=cv[:, :, 0:1].broadcast_to([128, 8, 16]), op=ALU.subtract),
            reads=["cv"], writes=[("gt", gb)])
        S.op("scalar", lambda e, gb=gb: e.activation(out=gt[gb][:], in_=gt[gb][:], func=AF.Exp),
             reads=[("gt", gb)], writes=[("gt", gb)])
        S.op("vector", lambda e, gb=gb: e.tensor_reduce(out=gs[:], in_=gt[gb][:], axis=AX.X, op=ALU.add),
             reads=[("gt", gb)], writes=["gs"])
        S.op("vector", lambda e: e.reciprocal(out=gs[:], in_=gs[:]), reads=["gs"], writes=["gs"])
        S.op("vector", lambda e, gb=gb: e.tensor_tensor(
            out=gt[gb][:], in0=gt[gb][:], in1=gs[:].unsqueeze(2).broadcast_to([128, 8, 16]), op=ALU.mult),
            reads=[("gt", gb), "gs"], writes=[("gt", gb)])
        S.op("sync", lambda e, gb=gb, blk=blk: e.dma_start(
            out=gwo[blk * 128:(blk + 1) * 128, :], in_=gt[gb][:].rearrange("p h k -> p (h k)")),
            reads=[("gt", gb)], dma=True, is_out=True)
        if P3A_STOP == 8:
            continue
        S.op("vector", lambda e: e.tensor_copy(out=tif[:], in_=ti[:]), reads=["ti"], writes=["tif"])
        tif4 = tif[:].rearrange("p (h two) k -> p h two k", two=2)
        for (which, dst) in [(0, I1), (1, I2)]:
            if which == 0:
                S.op("vector", lambda e: e.tensor_single_scalar(out=pu[:], in_=cpos[:], scalar=4,
                                                                op=ALU.logical_shift_right),
                     reads=["cpos"], writes=["pu"])
            else:
                S.op("vector", lambda e: e.tensor_single_scalar(out=pu[:], in_=cpos[:], scalar=15,
                                                                op=ALU.bitwise_and),
                     reads=["cpos"], writes=["pu"])
            S.op("vector", lambda e: e.tensor_copy(out=pa[:], in_=pu[:]), reads=["pu"], writes=["pa"])
            S.op("vector", lambda e: e.tensor_tensor(
                out=sel[:], in0=io16[:].unsqueeze(1).unsqueeze(1).broadcast_to([128, 8, 16, 16]),
                in1=pa[:].unsqueeze(3).broadcast_to([128, 8, 16, 16]), op=ALU.is_equal),
                reads=["io16", "pa"], writes=["sel"])
            S.op("vector", lambda e, which=which: e.tensor_tensor(
                out=sel[:], in0=sel[:], in1=tif4[:, :, which, :].unsqueeze(2).broadcast_to([128, 8, 16, 16]),
                op=ALU.mult), reads=["sel", "tif"], writes=["sel"])
            S.op("vector", lambda e, dst=dst: e.tensor_reduce(out=dst[:], in_=sel[:], axis=AX.X, op=ALU.add),
                 reads=["sel"], writes=["I%d" % which])
        S.op("vector", lambda e: e.scalar_tensor_tensor(
            out=ef[:].rearrange("p h k -> p (h k)"), in0=I1[:].rearrange("p h k -> p (h k)"), scalar=128.0,
            in1=I2[:].rearrange("p h k -> p (h k)"), op0=ALU.mult, op1=ALU.add),
            reads=["I0", "I1"], writes=["ef"])
        S.op("vector", lambda e, gb=gb: e.tensor_copy(out=ei[gb][:], in_=ef[:].rearrange("p h k -> p (h k)")),
             reads=["ef"], writes=[("ei", gb)])
        S.op("sync", lambda e, gb=gb, blk=blk: e.dma_start(out=idxo[blk * 128:(blk + 1) * 128, :], in_=ei[gb][:]),
             reads=[("ei", gb)], dma=True, is_out=True)


def run_p3a(x1, x1c, mod_l, wq, subkeys):
    nc = get_prog("p3a", build_p3a)
    m = split_mod(mod_l)
    ident = np.eye(128, dtype=np.float32).astype(NPBF)
    skt = np.ascontiguousarray(subkeys.reshape(16, 128, 128).transpose(2, 0, 1).reshape(128, 16 * 128))
    iot = np.arange(16, dtype=np.float32).reshape(1, 16)
    in_maps = []
    for r in range(NCORES):
        b = r // 4
        modv = np.stack([np.stack([m["sc2"][b], m["sh2"][b]]), np.stack([m["sc2"][2], m["sh2"][2]])])
        in_maps.append({"x": core_rows(x1, x1c, r), "modv": np.ascontiguousarray(modv), "pwq": wq,
                        "skt": skt, "ident": ident, "iot": iot})
    res = run(nc, in_maps)
    return res


NEXP = 16384
NUB = 4


P3B_PE_Y = True
CAST_ROWS = 512


def build_p3b(NBLK=NBLK, NEXP=NEXP):
    NROW = NBLK * 128
    nc, cx, S = new_prog()
    x1d = nc.dram_tensor("x1", [NROW, D], F32, kind="ExternalInput").ap()
    idxd = nc.dram_tensor("idx", [NROW, 128], I32, kind="ExternalInput").ap()
    gwd = nc.dram_tensor("gw", [NROW, 128], F32, kind="ExternalInput").ap()
    ud = nc.dram_tensor("u", [NEXP, D], F32, kind="ExternalInput").ap()
    vd = nc.dram_tensor("v", [NEXP, D], F32, kind="ExternalInput").ap()
    modv = nc.dram_tensor("modv", [2, 3, D], F32, kind="ExternalInput").ap()
    lngd = nc.dram_tensor("lng", [1, D], F32, kind="ExternalInput").ap()
    lnbd = nc.dram_tensor("lnb", [1, D], F32, kind="ExternalInput").ap()
    identd = nc.dram_tensor("identf", [128, 128], F32, kind="ExternalInput").ap()
    out = nc.dram_tensor("out", [NROW, D], F32, kind="ExternalOutput").ap()
    uvd = nc.dram_tensor("uv_scr", [NEXP, 2 * D], BF16).ap()
    ubd = uvd[:, 0:D]
    vbd = uvd[:, D:2 * D]

    mark = cx.mark()
    cx.prefix = "c_"
    RP = CAST_ROWS // 128
    stg = [cx.sb("stg%d" % i, [128, RP * D], F32) for i in range(3)]
    stb = [cx.sb("stb%d" % i, [128, RP * D], BF16) for i in range(3)]
    engs = ["scalar", "vector", "gpsimd"]
    n = 0
    for (src, dst) in [(ud, ubd), (vd, vbd)]:
        for c in range(NEXP // CAST_ROWS):
            k = n % 3
            n += 1
            rs = slice(c * CAST_ROWS, (c + 1) * CAST_ROWS)
            S.op("sync", lambda e, k=k, src=src, rs=rs: e.dma_start(
                out=stg[k][:], in_=src[rs, :].rearrange("(p r) d -> p (r d)", p=128)),
                writes=[("stg", k)], dma=True)
            eng = engs[k]
            if eng == "scalar":
                S.op("scalar", lambda e, k=k: e.copy(out=stb[k][:], in_=stg[k][:]),
                     reads=[("stg", k)], writes=[("stb", k)])
            else:
                S.op(eng, lambda e, k=k: e.tensor_copy(out=stb[k][:], in_=stg[k][:]),
                     reads=[("stg", k)], writes=[("stb", k)])
            S.op("sync", lambda e, k=k, dst=dst, rs=rs: e.dma_start(
                out=dst[rs, :].rearrange("(p r) d -> p r d", p=128),
                in_=stb[k][:].rearrange("p (r d) -> p r d", d=D)),
                reads=[("stb", k)], dma=True)
    barrier(S)
    cx.release(mark)
    cx.prefix = ""

    x1 = [cx.sb("x1t%d" % i, [128, D], F32) for i in range(2)]
    h2ps = cx.ps("h2ps", [128, D], F32)
    h2 = [h2ps, h2ps]
    it = [cx.sb("it%d" % i, [128, 128], I32) for i in range(2)]
    gw = [cx.sb("gw%d" % i, [128, 128], F32) for i in range(2)]
    NB = 8
    UV = [cx.sb("UV%d" % i, [128, 2 * D], BF16) for i in range(NB)]
    junk = cx.sb("junk", [128, D], BF16)
    act = cx.sb("act", [128, 128], F32)
    wg = cx.sb("wg", [128, 128], F32)
    y = cx.sb("y", [128, D], F32)
    Mt = cx.sb("Mt", [128, D], F32)
    SHt = cx.sb("SHt", [128, D], F32)
    G2 = cx.sb("G2", [128, D], F32)
    lng = cx.sb("lng_sb", [128, D], F32)
    lnb = cx.sb("lnb_sb", [128, D], F32)
    stats = cx.sb("stats", [128, 4, 6], F32)
    mv = cx.sb("mv", [128, 4], F32)
    idf = cx.sb("idf", [128, 128], F32)
    NDG = 4
    dgs = [cx.sb("dg%d" % i, [128, 128], BF16) for i in range(NDG)]
    yps = cx.ps("yps", [128, D], F32)

    S.op("sync", lambda e: e.dma_start(out=lng[:], in_=lngd.broadcast_to([128, D])), writes=["lng"], dma=True)
    S.op("sync", lambda e: e.dma_start(out=lnb[:], in_=lnbd.broadcast_to([128, D])), writes=["lnb"], dma=True)
    S.op("sync", lambda e: e.dma_start(out=idf[:], in_=identd), writes=["idf"], dma=True)
    nu = 0
    cur = None
    for blk in range(NBLK):
        b = blk % 2
        mi = 1 if blk % 17 == 16 else 0
        if mi != cur:
            cur = mi
            S.op("sync", lambda e, mi=mi: e.dma_start(out=Mt[:], in_=modv[mi, 0:1, :].broadcast_to([128, D])),
                 writes=["Mt"], dma=True)
            S.op("sync", lambda e, mi=mi: e.dma_start(out=SHt[:], in_=modv[mi, 1:2, :].broadcast_to([128, D])),
                 writes=["SHt"], dma=True)
            S.op("sync", lambda e, mi=mi: e.dma_start(out=G2[:], in_=modv[mi, 2:3, :].broadcast_to([128, D])),
                 writes=["G2"], dma=True)
            S.op("vector", lambda e: e.tensor_scalar(out=Mt[:], in0=Mt[:], scalar1=1.0, scalar2=None, op0=ALU.add),
                 reads=["Mt"], writes=["Mt"])
        rs = slice(blk * 128, (blk + 1) * 128)
        S.op("sync", lambda e, b=b, rs=rs: e.dma_start(out=x1[b][:], in_=x1d[rs, :]), writes=[("x1", b)], dma=True)
        S.op("sync", lambda e, b=b, rs=rs: e.dma_start(out=it[b][:], in_=idxd[rs, :]), writes=[("it", b)], dma=True)
        S.op("sync", lambda e, b=b, rs=rs: e.dma_start(out=gw[b][:], in_=gwd[rs, :]), writes=[("gw", b)], dma=True)
        S.op("vector", lambda e, b=b: e.tensor_tensor(out=h2[b][:], in0=x1[b][:], in1=Mt[:], op=ALU.mult),
             reads=[("x1", b), "Mt"], writes=["h2"])
        S.op("vector", lambda e, b=b: e.tensor_tensor(out=h2[b][:], in0=h2[b][:], in1=SHt[:], op=ALU.add),
             reads=["h2", "SHt"], writes=["h2"])
        for s in range(128):
            ub = nu % NB
            dk = nu % NDG
            nu += 1
            S.op("gpsimd", lambda e, ub=ub, b=b, s=s: e.indirect_dma_start(
                out=UV[ub][:], out_offset=None, in_=uvd,
                in_offset=bass.IndirectOffsetOnAxis(ap=it[b][:, s:s + 1], axis=0)),
                reads=[("it", b)], writes=[("UV", ub)], dma=True)
            S.op("vector", lambda e, ub=ub, b=b, s=s: e.scalar_tensor_tensor(
                out=junk[:], in0=UV[ub][:, 0:D], scalar=1.0, in1=h2[b][:], op0=ALU.mult, op1=ALU.mult,
                accum_out=act[:, s:s + 1]),
                reads=[("UV", ub), "h2"], writes=["junk", ("act", s)])
            S.op("scalar", lambda e, s=s: e.activation(out=wg[:, s:s + 1], in_=act[:, s:s + 1], func=AF.Gelu),
                 reads=[("act", s)], writes=[("wg", s)])
            S.op("vector", lambda e, s=s, dk=dk, b=b: e.tensor_scalar(
                out=dgs[dk][:], in0=idf[:], scalar1=wg[:, s:s + 1], scalar2=gw[b][:, s:s + 1],
                op0=ALU.mult, op1=ALU.mult),
                reads=["idf", ("wg", s), ("gw", b)], writes=[("dg", dk)])
            for cb in range(4):
                S.op("tensor", lambda e, ub=ub, s=s, cb=cb, dk=dk: e.matmul(
                    yps[:, cb * 512:(cb + 1) * 512], lhsT=dgs[dk][:],
                    rhs=UV[ub][:, D + cb * 512:D + (cb + 1) * 512],
                    start=(s == 0), stop=(s == 127)),
                    reads=[("UV", ub), ("dg", dk)], writes=[("yps", cb)])
        if P3B_PE_Y:
            for cb in range(4):
                S.op("vector", lambda e, cb=cb: e.tensor_tensor(
                    out=y[:, cb * 512:(cb + 1) * 512], in0=yps[:, cb * 512:(cb + 1) * 512],
                    in1=G2[:, cb * 512:(cb + 1) * 512], op=ALU.mult),
                    reads=[("yps", cb), "G2"], writes=["y"])
        else:
            S.op("vector", lambda e: e.tensor_tensor(out=y[:], in0=y[:], in1=G2[:], op=ALU.mult),
                 reads=["y", "G2"], writes=["y"])
        S.op("vector", lambda e, b=b: e.scalar_tensor_tensor(
            out=y[:], in0=x1[b][:], scalar=ALPHA, in1=y[:], op0=ALU.mult, op1=ALU.add),
            reads=[("x1", b), "y"], writes=["y"])
        emit_ln(S, y, "y", stats, mv, lng, lnb, x1[b], ("x1", b), 0)
        S.op("sync", lambda e, b=b, rs=rs: e.dma_start(out=out[rs, :], in_=x1[b][:]),
             reads=[("x1", b)], dma=True, is_out=True)
    return finish_prog(nc, cx, S)


P3B_CORES = 8


def run_p3b(idx_list, gw_list, x1, x1c, mod_l, u, v, ln_g, ln_b):
    per = NCORES // P3B_CORES
    nc = get_prog("p3b", build_p3b, NBLK * per, NEXP)
    m = split_mod(mod_l)
    in_maps = []
    for p in range(P3B_CORES):
        lcs = list(range(p * per, (p + 1) * per))
        b = lcs[0] // 4
        modv = np.stack([np.stack([m["sc2"][b], m["sh2"][b], m["g2"][b]]),
                         np.stack([m["sc2"][2], m["sh2"][2], m["g2"][2]])])
        in_maps.append({"x1": np.concatenate([core_rows(x1, x1c, r) for r in lcs], 0),
                        "idx": np.concatenate([idx_list[r] for r in lcs], 0),
                        "gw": np.concatenate([gw_list[r] for r in lcs], 0),
                        "u": u, "v": v, "modv": np.ascontiguousarray(modv),
                        "identf": np.eye(128, dtype=np.float32),
                        "lng": np.ascontiguousarray(ln_g.reshape(1, D)),
                        "lnb": np.ascontiguousarray(ln_b.reshape(1, D))})
    res = run_bass_kernel_spmd(nc, in_maps, core_ids=list(range(P3B_CORES))).results
    outs = []
    for p in range(P3B_CORES):
        o = res[p]["out"]
        for j in range(per):
            outs.append(o[j * NROW:(j + 1) * NROW])
    return gather_rows(outs, D, np.float32)


def run_p23(att_lat, att_ctx, x, xc, mod_l, w_out, ln_g, ln_b, wq, subkeys):
    nc = get_prog("p23", build_p23)
    m = split_mod(mod_l)
    ident = np.eye(128, dtype=np.float32).astype(NPBF)
    skt = np.ascontiguousarray(subkeys.reshape(16, 128, 128).transpose(2, 0, 1).reshape(128, 16 * 128))
    iot = np.arange(16, dtype=np.float32).reshape(1, 16)
    in_maps = []
    for r in range(NCORES):
        b = r // 4
        rows = core_rows(att_lat, att_ctx, r)
        aT = np.ascontiguousarray(rows.reshape(NBLK, 128, 16, 128).transpose(0, 3, 2, 1))
        modv = np.stack([np.stack([m["sc2"][b], m["sh2"][b]]), np.stack([m["sc2"][2], m["sh2"][2]])])
        in_maps.append({"attT": aT, "x": core_rows(x, xc, r), "wout": w_out,
                        "g1": np.ascontiguousarray(np.stack([m["g1"][b], m["g1"][2]])),
                        "lng": np.ascontiguousarray(ln_g.reshape(1, D)),
                        "lnb": np.ascontiguousarray(ln_b.reshape(1, D)),
                        "modv": np.ascontiguousarray(modv), "pwq": wq, "skt": skt, "ident": ident, "iot": iot})
    res = run(nc, in_maps)
    x1, x1c = gather_rows([res[r]["out"] for r in range(NCORES)], D, np.float32)
    return x1, x1c, [res[r]["idx"] for r in range(NCORES)], [res[r]["gw"] for r in range(NCORES)]


def kernel(x, c, ctx, c_ctx, mod_w, mod_b, ln_g, ln_b, ab_w_in, ab_w_out, a_sink, b_rpb,
           c_w_in, c_w_out, c_q_gain, c_k_gain, peer_wq, peer_subkeys, peer_u, peer_v):
    f = lambda a: np.ascontiguousarray(np.asarray(a, dtype=np.float32))
    x, c, ctx, c_ctx = f(x), f(c), f(ctx), f(c_ctx)
    mod = run_mod(c, c_ctx, f(mod_w), f(mod_b))
    xc = ctx
    dummy_gain = np.zeros((2, 128), np.float32)
    for layer in range(4):
        i = layer // 2
        mod_l = mod[layer]
        if layer % 2 == 0:
            q_lat, q_ctx = run_p1("ab", x, xc, mod_l, f(ab_w_in[i]), dummy_gain)
            att_lat, att_ctx = run_p2a_ab(q_lat, q_ctx, f(a_sink[i]), f(b_rpb[i]))
            w_out = f(ab_w_out[i])
        else:
            gains = np.ascontiguousarray(np.stack([f(c_q_gain[i]), f(c_k_gain[i])]))
            q_lat, q_ctx = run_p1("c", x, xc, mod_l, f(c_w_in[i]), gains)
            att_lat, att_ctx = run_p2a_c(q_lat, q_ctx)
            w_out = f(c_w_out[i])
        x1, x1c, idx_l, gw_l = run_p23(att_lat, att_ctx, x, xc, mod_l, w_out, f(ln_g[layer, 0]), f(ln_b[layer, 0]),
                                       f(peer_wq[layer]), f(peer_subkeys[layer]))
        x, xc = run_p3b(idx_l, gw_l, x1, x1c, mod_l, f(peer_u[layer]), f(peer_v[layer]),
                        f(ln_g[layer, 1]), f(ln_b[layer, 1]))
    return x
```
